# Optimizing a Trainium2 kernel written in Bass

```python
import jax, jax.numpy as jnp
from jax import lax
import numpy as np

D_MODEL = 1024
BATCH = 1
SEQ = 16384
DEPTH = 1
DEC_BATCH = 32
DEC_SEQ = 32
PAST_LEN = 1024

CHUNK = 64
WINDOW = 128
WINDOW_CHUNKS = WINDOW // CHUNK
N_HEADS = 8
N_KV_HEADS = 2
HEAD_DIM = 64
GROUP = N_HEADS // N_KV_HEADS
ATTN_DIM = N_HEADS * HEAD_DIM
KV_DIM = N_KV_HEADS * HEAD_DIM
CONV_DIM = 512
CONV_WIDTH = 3
PLE_DIM = 256
ROPE_THETA = 10000.0
EPS = 1e-6
NEG = -1e30
SPLIT_SIZES = (ATTN_DIM, KV_DIM, KV_DIM, ATTN_DIM, CONV_DIM, CONV_DIM, CONV_DIM, CONV_DIM, D_MODEL, D_MODEL)
IN_DIM = 2 * ATTN_DIM + 2 * KV_DIM + 4 * CONV_DIM + 2 * D_MODEL

kernel_name = "hybrid_swa_sink_shortconv_stream_step"


def rmsnorm(x, g):
    xf = x.astype(jnp.float32)
    y = xf * lax.rsqrt(jnp.mean(xf * xf, axis=-1, keepdims=True) + EPS)
    return (y * g.astype(jnp.float32)).astype(x.dtype)


def rope(x, pos):
    half = HEAD_DIM // 2
    inv_freq = ROPE_THETA ** (-jnp.arange(0, half, dtype=jnp.float32) * 2.0 / HEAD_DIM)
    ang = pos[:, None] * inv_freq[None, :]
    cos = jnp.cos(ang)[None, :, None, :]
    sin = jnp.sin(ang)[None, :, None, :]
    xf = x.astype(jnp.float32)
    x1, x2 = xf[..., :half], xf[..., half:]
    return jnp.concatenate([x1 * cos - x2 * sin, x2 * cos + x1 * sin], axis=-1).astype(x.dtype)


def sink_softmax(s, mask, sink):
    s = jnp.where(mask, s, NEG)
    m = jnp.maximum(jnp.max(s, axis=-1, keepdims=True), sink)
    e = jnp.exp(s - m)
    return e / (jnp.sum(e, axis=-1, keepdims=True) + jnp.exp(sink - m))


def attend_prompt(q, k, v, sink):
    b, t = q.shape[0], q.shape[1]
    nc = t // CHUNK
    nb = WINDOW_CHUNKS + 1
    scale = HEAD_DIM ** -0.5
    qc = q.reshape(b, nc, CHUNK, N_KV_HEADS, GROUP, HEAD_DIM) * scale
    pad = ((0, 0), (WINDOW_CHUNKS, 0), (0, 0), (0, 0), (0, 0))
    kp = jnp.pad(k.reshape(b, nc, CHUNK, N_KV_HEADS, HEAD_DIM), pad)
    vp = jnp.pad(v.reshape(b, nc, CHUNK, N_KV_HEADS, HEAD_DIM), pad)
    kb = jnp.concatenate([kp[:, j:j + nc] for j in range(nb)], axis=2)
    vb = jnp.concatenate([vp[:, j:j + nc] for j in range(nb)], axis=2)
    key_chunk = jnp.arange(nc)[:, None] - WINDOW_CHUNKS + (jnp.arange(nb * CHUNK) // CHUNK)[None, :]
    mask = (key_chunk >= 0)[None, :, None, None, None, :]
    s = jnp.einsum('bncvgd,bnjvd->bnvgcj', qc, kb, preferred_element_type=jnp.float32)
    pr = sink_softmax(s, mask, sink.astype(jnp.float32).reshape(N_KV_HEADS, GROUP, 1, 1))
    o = jnp.einsum('bnvgcj,bnjvd->bncvgd', pr.astype(v.dtype), vb)
    return o.reshape(b, t, ATTN_DIM), k[:, -WINDOW:], v[:, -WINDOW:]


def attend_sample(q, k, v, sink, cache_k, cache_v):
    b, t = q.shape[0], q.shape[1]
    L = cache_k.shape[1]
    scale = HEAD_DIM ** -0.5
    kk = jnp.concatenate([cache_k, k], axis=1)
    vv = jnp.concatenate([cache_v, v], axis=1)
    q_pos = PAST_LEN + jnp.arange(t)
    k_pos = jnp.concatenate([PAST_LEN - L + jnp.arange(L), q_pos])
    qch = (q_pos // CHUNK)[:, None]
    kch = (k_pos // CHUNK)[None, :]
    mask = ((kch <= qch) & (kch >= qch - WINDOW_CHUNKS))[None, None, None]
    qg = q.reshape(b, t, N_KV_HEADS, GROUP, HEAD_DIM) * scale
    s = jnp.einsum('btvgd,bjvd->bvgtj', qg, kk, preferred_element_type=jnp.float32)
    pr = sink_softmax(s, mask, sink.astype(jnp.float32).reshape(N_KV_HEADS, GROUP, 1, 1))
    o = jnp.einsum('bvgtj,bjvd->btvgd', pr.astype(vv.dtype), vv)
    return o.reshape(b, t, ATTN_DIM), kk[:, -L:], vv[:, -L:]


def hybrid_layer(x, p, pos, conv_past, attend, ln_g, w_in, q_norm_g, k_norm_g, sink, conv_w,
                 w_attn_out, w_conv_out, w_o, w_ple_gate, w_ple_proj):
    b, t, _ = x.shape
    h = rmsnorm(x, ln_g)
    split_idx = np.cumsum(SPLIT_SIZES)[:-1].tolist()
    (q, k, v, gate_a, b_gate, c_gate, u_in, gate_c, merge_a, merge_c) = jnp.split(h @ w_in, split_idx, axis=-1)
    q = rope(rmsnorm(q.reshape(b, t, N_HEADS, HEAD_DIM), q_norm_g), pos)
    k = rope(rmsnorm(k.reshape(b, t, N_KV_HEADS, HEAD_DIM), k_norm_g), pos)
    v = v.reshape(b, t, N_KV_HEADS, HEAD_DIM)
    attn, new_k, new_v = attend(q, k, v, sink)
    u = c_gate * u_in
    up = jnp.concatenate([conv_past.astype(u.dtype), u], axis=1)
    conv = up[:, 0:t] * conv_w[0]
    for j in range(1, CONV_WIDTH):
        conv = conv + up[:, j:j + t] * conv_w[j]
    new_conv = up[:, -(CONV_WIDTH - 1):]
    y_a = (attn * jax.nn.silu(gate_a)) @ w_attn_out
    y_c = (b_gate * conv * jax.nn.silu(gate_c)) @ w_conv_out
    r = x + (jax.nn.sigmoid(merge_a) * y_a + jax.nn.sigmoid(merge_c) * y_c) @ w_o
    r = r + jax.nn.sigmoid(r @ w_ple_gate) * (p @ w_ple_proj)
    return r, new_k, new_v, new_conv


def setup_inputs(seed: int = 0) -> dict:
    key = jax.random.key(seed)
    ks = jax.random.split(key, 20)
    f32 = jnp.float32
    L = min(WINDOW, PAST_LEN)
    nrm = lambda k_, shape, s: jax.random.normal(k_, shape, f32) * s
    return {
        "x_prompt": nrm(ks[0], (BATCH, SEQ, D_MODEL), 1.0),
        "x_sample": nrm(ks[1], (DEC_BATCH, DEC_SEQ, D_MODEL), 1.0),
        "p_prompt": nrm(ks[2], (DEPTH, BATCH, SEQ, PLE_DIM), 1.0),
        "p_sample": nrm(ks[3], (DEPTH, DEC_BATCH, DEC_SEQ, PLE_DIM), 1.0),
        "cache_k": nrm(ks[4], (DEPTH, DEC_BATCH, L, N_KV_HEADS, HEAD_DIM), 1.0),
        "cache_v": nrm(ks[5], (DEPTH, DEC_BATCH, L, N_KV_HEADS, HEAD_DIM), 1.0),
        "state_conv": nrm(ks[6], (DEPTH, DEC_BATCH, CONV_WIDTH - 1, CONV_DIM), 1.0),
        "ln_g": 1.0 + nrm(ks[7], (DEPTH, D_MODEL), 0.05),
        "w_in": nrm(ks[8], (DEPTH, D_MODEL, IN_DIM), D_MODEL ** -0.5),
        "q_norm_g": 1.0 + nrm(ks[9], (DEPTH, HEAD_DIM), 0.05),
        "k_norm_g": 1.0 + nrm(ks[10], (DEPTH, HEAD_DIM), 0.05),
        "sink": nrm(ks[11], (DEPTH, N_HEADS), 0.5),
        "conv_w": nrm(ks[12], (DEPTH, CONV_WIDTH, CONV_DIM), CONV_WIDTH ** -0.5),
        "w_attn_out": nrm(ks[13], (DEPTH, ATTN_DIM, D_MODEL), ATTN_DIM ** -0.5),
        "w_conv_out": nrm(ks[14], (DEPTH, CONV_DIM, D_MODEL), CONV_DIM ** -0.5),
        "w_o": nrm(ks[15], (DEPTH, D_MODEL, D_MODEL), D_MODEL ** -0.5),
        "w_ple_gate": nrm(ks[16], (DEPTH, D_MODEL, D_MODEL), D_MODEL ** -0.5),
        "w_ple_proj": nrm(ks[17], (DEPTH, PLE_DIM, D_MODEL), PLE_DIM ** -0.5),
    }


def reference(x_prompt, x_sample, p_prompt, p_sample, cache_k, cache_v, state_conv,
              ln_g, w_in, q_norm_g, k_norm_g, sink, conv_w, w_attn_out, w_conv_out, w_o,
              w_ple_gate, w_ple_proj):
    t_p = x_prompt.shape[1]
    t_s = x_sample.shape[1]
    pos_p = jnp.arange(t_p, dtype=jnp.float32)
    pos_s = PAST_LEN + jnp.arange(t_s, dtype=jnp.float32)
    hp, hs = x_prompt, x_sample
    kp_l, vp_l, cp_l, ks_l, vs_l, cs_l = [], [], [], [], [], []
    for i in range(DEPTH):
        weights = (ln_g[i], w_in[i], q_norm_g[i], k_norm_g[i], sink[i], conv_w[i],
                   w_attn_out[i], w_conv_out[i], w_o[i], w_ple_gate[i], w_ple_proj[i])
        conv0 = jnp.zeros((hp.shape[0], CONV_WIDTH - 1, CONV_DIM), hp.dtype)
        hp, kp, vp, cp = hybrid_layer(hp, p_prompt[i], pos_p, conv0, attend_prompt, *weights)
        ck, cv = cache_k[i], cache_v[i]
        att_s = lambda q, k, v, s, ck=ck, cv=cv: attend_sample(q, k, v, s, ck, cv)
        hs, k_s, v_s, c_s = hybrid_layer(hs, p_sample[i], pos_s, state_conv[i], att_s, *weights)
        kp_l.append(kp); vp_l.append(vp); cp_l.append(cp)
        ks_l.append(k_s); vs_l.append(v_s); cs_l.append(c_s)
    return (hp, hs, jnp.stack(kp_l), jnp.stack(vp_l), jnp.stack(cp_l),
            jnp.stack(ks_l), jnp.stack(vs_l), jnp.stack(cs_l))
```

```python
import contextlib
import numpy as np
import concourse.bass as bass
import concourse.mybir as mybir
from concourse.bass_utils import run_bass_kernel_spmd

F32 = mybir.dt.float32
BF16 = mybir.dt.bfloat16
AF = mybir.ActivationFunctionType
ALU = mybir.AluOpType
AX = mybir.AxisListType

NCORES = 8
D = 1024
SEQ = 16384
TOK_PC = SEQ // NCORES
NPT = TOK_PC // 128
NROWS = 128 + TOK_PC + 128
ST = 256
IN_DIM = 5376
EPS = 1e-6
C_Q, C_K, C_V, C_GA, C_B, C_C, C_U, C_GC, C_MA, C_MB = 0, 512, 640, 768, 1280, 1792, 2304, 2816, 3328, 4352

ENGS = ("sp", "act", "dve", "pool", "pe")
STRICT_SAME_ENGINE = True
STRICT_ENGINES = ("pool", "dve")


class Res:
    def __init__(self, name):
        self.name = name
        self.w = None
        self.readers = {}
        self.rd_dma = None
        self.dsem = None
        self.dcount = 0
        self.rsem = None
        self.rcount = 0


class Op:
    __slots__ = ("kind", "eng", "idx", "fn", "waits", "sig", "inc", "sem", "val", "know")

    def __init__(self, kind, eng, idx, fn):
        self.kind = kind
        self.eng = eng
        self.idx = idx
        self.fn = fn
        self.waits = []
        self.sig = False
        self.inc = None
        self.sem = None
        self.val = 0
        self.know = {}


class Mark:
    kind = "dma"

    def __init__(self, sem, val, know=None):
        self.sem = sem
        self.val = val
        self.know = know or {}


class Planner:
    def __init__(self):
        self.ops = {e: [] for e in ENGS}
        self.seen = {e: {} for e in ENGS}
        self.seen_sem = {e: {} for e in ENGS}
        self.sem_names = []
        self.final_sem = self.new_sem("final")
        self.final_count = 0
        self.store_res = []

    def new_sem(self, name):
        n = "s%d_%s" % (len(self.sem_names), name)
        self.sem_names.append(n)
        return n

    def _dep(self, eng, waits, X, raw):
        if X is None:
            return
        if X.kind == "dma":
            if self.seen_sem[eng].get(X.sem, 0) >= X.val:
                return
            self.seen_sem[eng][X.sem] = X.val
            waits.append(("sem", X.sem, X.val))
            self._learn(eng, X.know)
            return
        if X.eng == eng and (eng == "pe" or not (raw or (STRICT_SAME_ENGINE and eng in STRICT_ENGINES))):
            return
        if self.seen[eng].get(X.eng, -1) >= X.idx:
            return
        self.seen[eng][X.eng] = X.idx
        X.sig = True
        waits.append(("op", X))
        self._learn(eng, X.know)

    def _learn(self, eng, know):
        se = self.seen[eng]
        for g, i in know.items():
            if se.get(g, -1) < i:
                se[g] = i

    def _deps(self, eng, reads, writes):
        waits = []
        for r in reads:
            self._dep(eng, waits, r.w, True)
        for w in writes:
            self._dep(eng, waits, w.w, False)
            for o in w.readers.values():
                self._dep(eng, waits, o, False)
            if w.rd_dma is not None:
                self._dep(eng, waits, w.rd_dma, False)
                w.rd_dma = None
        return waits

    def add(self, eng, fn, reads=(), writes=()):
        op = Op("op", eng, len(self.ops[eng]), fn)
        op.waits = self._deps(eng, reads, writes)
        self.ops[eng].append(op)
        op.know = dict(self.seen[eng])
        op.know[eng] = op.idx
        for r in reads:
            r.readers[eng] = op
        for w in writes:
            w.w = op
            w.readers = {}
        return op

    def dma(self, eng, fn, reads=(), writes=()):
        op = Op("dmaop", eng, len(self.ops[eng]), fn)
        op.waits = self._deps(eng, reads, writes)
        self.ops[eng].append(op)
        if writes:
            W = writes[0]
            if W.dsem is None:
                W.dsem = self.new_sem("d_" + W.name)
            W.dcount += 16
            op.inc = W.dsem
            W.w = Mark(W.dsem, W.dcount, dict(self.seen[eng]))
            W.readers = {}
        elif reads:
            R = reads[0]
            if R.rsem is None:
                R.rsem = self.new_sem("r_" + R.name)
                self.store_res.append(R)
            R.rcount += 16
            op.inc = R.rsem
            R.rd_dma = Mark(R.rsem, R.rcount, dict(self.seen[eng]))
        else:
            self.final_count += 16
            op.inc = self.final_sem
        return op


def build_program():
    nc = bass.Bass("TRN2", target_bir_lowering=False)
    P = Planner()

    def din(name, shape):
        return nc.dram_tensor(name, list(shape), F32, kind="ExternalInput").ap()

    def dout(name, shape):
        return nc.dram_tensor(name, list(shape), F32, kind="ExternalOutput").ap()

    xh = din("xh", [NROWS, D])
    ph = din("ph", [NROWS, 256])
    rope = din("rope", [NROWS, 128])
    ck_d = din("ck", [4, 128, 128])
    cv_d = din("cv", [4, 128, 128])
    sconv_d = din("sconv", [4, 2, 512])
    hbias_d = din("hbias", [128, 1])
    lng_d = din("ln_g", [D])
    win_d = din("w_in", [D, IN_DIM])
    qg_d = din("qg", [64])
    kg_d = din("kg", [64])
    sink_d = din("sink", [8])
    convw_d = din("conv_w", [3, 512])
    wa_d = din("w_a", [512, D])
    wb_d = din("w_b", [512, D])
    wo_d = din("w_o", [D, D])
    wpg_d = din("w_pg", [D, D])
    wpp_d = din("w_pp", [256, D])

    y_d = dout("y", [TOK_PC + 128, D])
    klast_d = dout("k_last", [128, 128])
    vlast_d = dout("v_last", [128, 128])
    clast_d = dout("c_last", [2, 512])
    ks_d = dout("ks", [4, 128, 128])
    vs_d = dout("vs", [4, 128, 128])
    cs_d = dout("cs", [4, 2, 512])

    es = contextlib.ExitStack()
    with es:
        res = {}

        def sb(name, shape, dt):
            t = es.enter_context(nc.sbuf_tensor("t_" + name, list(shape), dt))
            res["t_" + name] = Res(name)
            return t

        w_in = sb("w_in_sb", [128, 8, IN_DIM], BF16)
        w_a = sb("w_a_sb", [128, 4, D], BF16)
        w_b = sb("w_b_sb", [128, 4, D], BF16)
        w_o = sb("w_o_sb", [128, 8, D], BF16)
        w_pg = sb("w_pg_sb", [128, 8, D], BF16)
        w_pp = sb("w_pp_sb", [128, 2, D], BF16)
        WG = {}
        for nm in ("qkv", "gA", "b", "c", "u", "gC", "mA", "mB"):
            WG[nm] = Res("win_" + nm)
        R_wa, R_wb, R_wo, R_wpg, R_wpp = res["t_w_a_sb"], res["t_w_b_sb"], res["t_w_o_sb"], res["t_w_pg_sb"], res["t_w_pp_sb"]

        ident = sb("ident", [128, 128], BF16)
        identf = sb("identf", [128, 128], F32)
        ones = sb("ones", [128, 128], BF16)
        g_ln = sb("g_ln", [128, 8], F32)
        gq = sb("gq", [128, 64], F32)
        rgq = sb("rgq", [128, 64], F32)
        gk = sb("gk", [128, 64], F32)
        rgk = sb("rgk", [128, 64], F32)
        esink = sb("esink", [128, 8], F32)
        convw = sb("convw", [128, 4, 3], F32)
        hbias = sb("hbias", [128, 1], F32)
        stail = sb("stail", [128, 4, 4, 2], F32)

        NXS = 4
        xs = [sb("xs%d" % i, [128, D], F32) for i in range(NXS)]
        xn = sb("xn", [128, D], BF16)
        sx = [sb("sx%d" % i, [128, 4], F32) for i in range(2)]
        hT = sb("hT", [128, 8, ST], BF16)
        csl = [sb("cs%d" % i, [128, 128], F32) for i in range(2)]
        TCq = sb("TCq", [128, 64], F32)
        TSq = sb("TSq", [128, 64], F32)
        TCk = sb("TCk", [128, 64], F32)
        TSk = sb("TSk", [128, 64], F32)
        sq = sb("sq", [128, 10, 64], F32)
        sqflat = sq[:].rearrange("p h d -> p (h d)")
        qr = sb("qr", [128, 10, 64], F32)
        ssq = sb("ssq", [128, 10], F32)
        nwv = sb("nwv", [128, 10], F32)
        nwt = sb("nwt", [128, 10], F32)
        nwy = sb("nwy", [128, 10], F32)
        qro2 = [sb("qro%d" % i, [128, 8, 64], BF16) for i in range(2)]
        kf = [sb("kf%d" % i, [128, 2, 64], F32) for i in range(2)]
        kb2 = [sb("kb%d" % i, [128, 2, 64], BF16) for i in range(2)]
        vf = [sb("vf0", [128, 128], F32)] * 2
        vd = [sb("vd%d" % i, [128, 2, 128], BF16) for i in range(3)]
        QT = sb("QT", [64, 8, 128], BF16)
        KT = [sb("KT%d" % i, [64, 2, 128], BF16) for i in range(3)]
        ckf = sb("ckf", [128, 128], F32)
        ckb = sb("ckb", [128, 128], BF16)
        cvf = sb("cvf", [128, 128], F32)
        KTc = sb("KTc", [64, 2, 128], BF16)
        vdc = sb("vdc", [128, 2, 128], BF16)
        NPTB = 4
        PT = [sb("PT%d" % i, [128, 512], BF16) for i in range(NPTB)]
        nrm = sb("nrm", [128, 512], F32)
        AT = sb("AT", [128, 4, ST], BF16)
        tgate = [sb("tgate%d" % i, [128, ST], F32) for i in range(2)]
        csb = sb("csb", [128, ST], F32)
        up = sb("up", [128, ST + 8], F32)
        acc = sb("acc", [128, ST], F32)
        PC = sb("PC", [128, 4, ST], BF16)
        tail = sb("tail", [128, 4, 2], F32)
        couts = sb("couts", [128, 4, 4, 2], F32)
        coutp = sb("coutp", [128, 4, 2], F32)
        tA = sb("tA", [128, ST], F32)
        tB = sb("tB", [128, ST], F32)
        mT = sb("mT", [128, 8, ST], BF16)
        rb = sb("rb", [128, D], BF16)
        rT = sb("rT", [128, 8, 128], BF16)
        tg = sb("tg", [128, 512], F32)
        pb2 = [sb("pb%d" % i, [128, 256], BF16) for i in range(2)]
        pT = sb("pT", [128, 2, 128], BF16)

        def R(t):
            return res[t.name] if hasattr(t, "name") else t

        RhA, RhB = Res("hT_a"), Res("hT_b")
        RmT = [Res("mT%d" % k) for k in range(8)]

        def Rh(c0, n):
            out = []
            if c0 < 128:
                out.append(RhA)
            if c0 + n > 128:
                out.append(RhB)
            return out

        banks = []
        for i in range(8):
            t = es.enter_context(nc.psum_tensor("ps%d" % i, [128, 512], F32))
            banks.append((Res("ps%d" % i), t))
        bank_ctr = [0]

        held = set()

        def bank(hold=False):
            while True:
                i = bank_ctr[0] % 8
                bank_ctr[0] += 1
                if i not in held:
                    break
            if hold:
                held.add(i)
            r, t = banks[i]
            return r, t

        def unhold(r):
            for i, (rr, _) in enumerate(banks):
                if rr is r:
                    held.discard(i)

        MULTI = set()
        def mm(out, lhsT, rhs, start, stop, reads, writes):
            P.add("pe", lambda e: e.matmul(out, lhsT=lhsT, rhs=rhs, start=start, stop=stop), reads, writes)

        def tr(out, in_, idn, reads, writes):
            P.add("pe", lambda e: e.transpose(out=out, in_=in_, identity=idn), reads, writes)

        def act(out, in_, func, reads, writes, bias=None, scale=None, accum_out=None):
            kw = {}
            if bias is not None:
                kw["bias"] = bias
            if scale is not None:
                kw["scale"] = scale
            if accum_out is not None:
                kw["accum_out"] = accum_out
            op_ = P.add("act", lambda e: e.activation(out=out, in_=in_, func=func, **kw), reads, writes)
            if accum_out is not None:
                MULTI.add(id(op_))

        def tt(eng, out, in0, in1, op, reads, writes):
            P.add(eng, lambda e: e.tensor_tensor(out=out, in0=in0, in1=in1, op=op), reads, writes)

        def tsc(eng, out, in0, s1, s2, op0, op1, reads, writes):
            if op1 is None:
                P.add(eng, lambda e: e.tensor_scalar(out=out, in0=in0, scalar1=s1, scalar2=None, op0=op0), reads, writes)
            else:
                P.add(eng, lambda e: e.tensor_scalar(out=out, in0=in0, scalar1=s1, scalar2=s2, op0=op0, op1=op1), reads, writes)

        def stt(eng, out, in0, scalar, in1, op0, op1, reads, writes):
            P.add(eng, lambda e: e.scalar_tensor_tensor(out=out, in0=in0, scalar=scalar, in1=in1, op0=op0, op1=op1), reads, writes)

        def cp(eng, out, in_, reads, writes):
            if eng == "act":
                act(out, in_, AF.Copy, reads, writes)
            else:
                P.add(eng, lambda e: e.tensor_copy(out=out, in_=in_), reads, writes)

        def dma(eng, out, in_, reads, writes, slow=False):
            if slow:
                P.dma(eng, lambda e: e.dma_start(out=out, in_=in_, allow_slow_non_contiguous=True), reads, writes)
            else:
                P.dma(eng, lambda e: e.dma_start(out=out, in_=in_), reads, writes)

        Rid, Ridf, Rones = R(ident), R(identf), R(ones)
        P.add("pool", lambda e: e.memset(identf[:], 1.0), [], [Ridf])
        P.add("pool", lambda e: e.affine_select(out=identf[:], in_=identf[:], pattern=[[-1, 128]],
                                                compare_op=ALU.is_equal, fill=0.0, base=0, channel_multiplier=1),
              [Ridf], [Ridf])
        cp("pool", ident[:], identf[:], [Ridf], [Rid])
        P.add("pool", lambda e: e.memset(ones[:], 1.0), [], [Rones])
        P.add("pool", lambda e: e.memset(tail[:], 0.0), [], [R(tail)])

        dma("sp", xs[0][:], xh[0:128, :], [], [R(xs[0])])
        dma("sp", g_ln[:], lng_d.rearrange("(c p) -> p c", p=128), [], [R(g_ln)], slow=True)
        dma("sp", gq[:], qg_d.partition_broadcast(128), [], [R(gq)])
        dma("sp", gk[:], kg_d.partition_broadcast(128), [], [R(gk)])
        dma("sp", rgq[:, 0:32], qg_d[32:64].partition_broadcast(128), [], [R(rgq)])
        dma("sp", rgq[:, 32:64], qg_d[0:32].partition_broadcast(128), [], [R(rgq)])
        dma("sp", rgk[:, 0:32], kg_d[32:64].partition_broadcast(128), [], [R(rgk)])
        dma("sp", rgk[:, 32:64], kg_d[0:32].partition_broadcast(128), [], [R(rgk)])
        dma("sp", esink[:], sink_d.partition_broadcast(128), [], [R(esink)])
        dma("sp", hbias[:], hbias_d, [], [R(hbias)])
        for j in range(4):
            dma("act", convw[:, j, :], convw_d[:, j * 128:(j + 1) * 128].rearrange("t p -> p t"), [], [R(convw)], slow=True)
        act(esink[:], esink[:], AF.Exp, [R(esink)], [R(esink)])

        xs_ctr = [0]
        x_slot_of = {}

        def load_x(g):
            i = xs_ctr[0] % NXS
            xs_ctr[0] += 1
            x_slot_of[g] = i
            if g == 0:
                return
            dma("sp", xs[i][:], xh[g * 128:(g + 1) * 128, :], [], [R(xs[i])])

        def load_win(nm, c0, c1):
            step = 512
            for a in range(c0, c1, step):
                b = min(c1, a + step)
                dma("pool", w_in[:, :, a:b], win_d[:, a:b].rearrange("(c p) n -> p c n", p=128), [], [WG[nm]])

        load_x(0)
        load_x(1)
        wq = []
        wq.append(lambda: load_win("qkv", 0, 768))
        wq.append(lambda: load_win("c", C_C, C_C + 512))
        wq.append(lambda: load_win("u", C_U, C_U + 512))
        wq.append(lambda: load_win("b", C_B, C_B + 512))
        wq.append(lambda: load_win("gC", C_GC, C_GC + 512))
        wq.append(lambda: load_win("gA", C_GA, C_GA + 512))
        wq.append(lambda: dma("pool", w_b[:], wb_d.rearrange("(c p) n -> p c n", p=128), [], [R_wb]))
        wq.append(lambda: dma("pool", w_a[:], wa_d.rearrange("(c p) n -> p c n", p=128), [], [R_wa]))
        wq.append(lambda: load_win("mA", C_MA, C_MA + 1024))
        wq.append(lambda: load_win("mB", C_MB, C_MB + 1024))
        for hh in range(2):
            wq.append(lambda hh=hh: dma("pool", w_o[:, :, hh * 512:(hh + 1) * 512], wo_d[:, hh * 512:(hh + 1) * 512].rearrange("(c p) n -> p c n", p=128), [], [R_wo]))
        for hh in range(2):
            wq.append(lambda hh=hh: dma("pool", w_pg[:, :, hh * 512:(hh + 1) * 512], wpg_d[:, hh * 512:(hh + 1) * 512].rearrange("(c p) n -> p c n", p=128), [], [R_wpg]))
        wq.append(lambda: dma("pool", w_pp[:], wpp_d.rearrange("(c p) n -> p c n", p=128), [], [R_wpp]))

        def issue_w(n):
            for _ in range(n):
                if wq:
                    wq.pop(0)()

        issue_w(3)
        stail_jobs = [(b, j) for b in range(4) for j in range(4)]

        def load_stail(n):
            for _ in range(n):
                if stail_jobs:
                    b, j = stail_jobs.pop(0)
                    dma("sp", stail[:, j, b, :], sconv_d[b, :, j * 128:(j + 1) * 128].rearrange("t p -> p t"), [], [R(stail)], slow=True)

        out_y_res = [None]
        src_res = [None]

        norm_ctr = [0]

        def norm_tile(g, col0):
            i = x_slot_of[g]
            xt = xs[i]
            s = sx[norm_ctr[0] % 2]
            norm_ctr[0] += 1
            P.add("pool", lambda e: e.memset(s[:], 0.0), [], [R(s)])
            act(xn[:], xt[:], AF.Square, [R(xt), R(s)], [R(xn), R(s)], accum_out=s[:, 0:1])
            src_res[0] = R(s)
            out_y_res[0] = R(s)
            newton_rsqrt_ap(s[:, 0:1], s[:, 2:3], s[:, 3:4], s[:, 1:2], 1.0 / D, iters=2, Rv=R(s), Rt=R(s))
            yield
            act(xn[:], xt[:], AF.Copy, [R(xt), R(s)], [R(xn)], scale=s[:, 1:2])
            yield
            br, bt = bank()
            b16 = bt[:].bitcast(BF16)
            for c in range(8):
                tr(b16[:, c * 128:(c + 1) * 128], xn[:, c * 128:(c + 1) * 128], ident[:], [R(xn), Rid], [br])
            tt("dve", hT[:, :, col0:col0 + 128], b16.rearrange("p (c t) -> p c t", c=8),
               g_ln[:].unsqueeze(2).to_broadcast([128, 8, 128]), ALU.mult, [br, R(g_ln)], Rh(col0, 128))

        cs_ctr = [0]
        kf_ctr = [0]

        def qkv_tile(row0, col0, M, slot, has_q, kout=None, vout=None):
            ci = cs_ctr[0] % 2
            cs_ctr[0] += 1
            cst = csl[ci]
            dma("sp", cst[0:M, :], rope[row0:row0 + M, :], [], [R(cst)])
            Rw = WG["qkv"]
            if has_q:
                bq, tq = bank()
                for c in range(8):
                    mm(tq[0:M, :], hT[:, c, col0:col0 + M], w_in[:, c, 0:512], c == 0, c == 7, Rh(col0, M) + [Rw], [bq])
            bkv, tkv = bank()
            for c in range(8):
                mm(tkv[0:M, 0:256], hT[:, c, col0:col0 + M], w_in[:, c, 512:768], c == 0, c == 7, Rh(col0, M) + [Rw], [bkv])
            h0 = 0 if has_q else 8
            if has_q:
                tt("pool", TCq[0:M, :], cst[0:M, 0:64], gq[0:M, :], ALU.mult, [R(cst), R(gq)], [R(TCq)])
                tt("pool", TSq[0:M, :], cst[0:M, 64:128], rgq[0:M, :], ALU.mult, [R(cst), R(rgq)], [R(TSq)])
            tt("pool", TCk[0:M, :], cst[0:M, 0:64], gk[0:M, :], ALU.mult, [R(cst), R(gk)], [R(TCk)])
            tt("pool", TSk[0:M, :], cst[0:M, 64:128], rgk[0:M, :], ALU.mult, [R(cst), R(rgk)], [R(TSk)])
            if has_q:
                act(nrm[0:M, :], tq[0:M, :], AF.Square, [bq], [R(nrm)])
                P.add("dve", lambda e: e.tensor_reduce(out=ssq[0:M, 0:8], in_=nrm[0:M, :].rearrange("p (h d) -> p h d", h=8), axis=AX.X, op=ALU.add),
                      [R(nrm)], [R(ssq)])
            act(csb[0:M, 0:128], tkv[0:M, 0:128], AF.Square, [bkv], [R(csb)])
            P.add("dve", lambda e: e.tensor_reduce(out=ssq[0:M, 8:10], in_=csb[0:M, 0:128].rearrange("p (h d) -> p h d", h=2), axis=AX.X, op=ALU.add),
                  [R(csb)], [R(ssq)])
            src_res[0] = R(ssq)
            out_y_res[0] = R(nwy)
            n = 10 - h0
            v, t, yv = nwv[0:M, 0:n], nwt[0:M, 0:n], nwy[0:M, 0:n]
            newton_rsqrt_ap(ssq[0:M, h0:10], v, t, yv, 1.0 / 64)
            if has_q:
                tt("dve", qr[0:M, 0:8, :], tq[0:M, :].rearrange("p (h d) -> p h d", h=8),
                   nwy[0:M, 0:8].unsqueeze(2).to_broadcast([M, 8, 64]), ALU.mult, [bq, R(nwy)], [R(qr)])
            tt("dve", qr[0:M, 8:10, :], tkv[0:M, 0:128].rearrange("p (h d) -> p h d", h=2),
               nwy[0:M, 8 - h0:10 - h0].unsqueeze(2).to_broadcast([M, 2, 64]), ALU.mult, [bkv, R(nwy)], [R(qr)])
            kfi = kf_ctr[0] % 2
            kf_ctr[0] += 1
            vft, kft = vf[kfi], kf[kfi]
            qro, kb = qro2[kfi], kb2[kfi]
            vdt = vd[slot]
            for rr in range(2):
                act(vdt[0:M, :, rr * 64:(rr + 1) * 64], tkv[0:M, 128:256].rearrange("p (h d) -> p h d", h=2), AF.Copy, [bkv], [R(vdt)])
            if vout is not None:
                act(vft[0:M, :], tkv[0:M, 128:256], AF.Copy, [bkv], [R(vft)])
                for (dst, r0, r1) in vout:
                    dma("sp", dst, vft[r0:r1, :], [R(vft)], [])
            if has_q:
                tt("pool", sq[0:M, 0:8, 0:32], qr[0:M, 0:8, 32:64], TSq[0:M, 0:32].unsqueeze(1).to_broadcast([M, 8, 32]), ALU.mult, [R(qr), R(TSq)], [R(sq)])
                tt("pool", sq[0:M, 0:8, 32:64], qr[0:M, 0:8, 0:32], TSq[0:M, 32:64].unsqueeze(1).to_broadcast([M, 8, 32]), ALU.mult, [R(qr), R(TSq)], [R(sq)])
            tt("pool", sq[0:M, 8:10, 0:32], qr[0:M, 8:10, 32:64], TSk[0:M, 0:32].unsqueeze(1).to_broadcast([M, 2, 32]), ALU.mult, [R(qr), R(TSk)], [R(sq)])
            tt("pool", sq[0:M, 8:10, 32:64], qr[0:M, 8:10, 0:32], TSk[0:M, 32:64].unsqueeze(1).to_broadcast([M, 2, 32]), ALU.mult, [R(qr), R(TSk)], [R(sq)])
            if has_q:
                tt("pool", qr[0:M, 0:8, :], qr[0:M, 0:8, :], TCq[0:M, :].unsqueeze(1).to_broadcast([M, 8, 64]), ALU.mult, [R(qr), R(TCq)], [R(qr)])
            tt("pool", qr[0:M, 8:10, :], qr[0:M, 8:10, :], TCk[0:M, :].unsqueeze(1).to_broadcast([M, 2, 64]), ALU.mult, [R(qr), R(TCk)], [R(qr)])
            if has_q:
                tt("pool", qro[0:M, :, :], qr[0:M, 0:8, :], sq[0:M, 0:8, :], ALU.add, [R(qr), R(sq)], [R(qro)])
            tt("pool", kft[0:M, :, :], qr[0:M, 8:10, :], sq[0:M, 8:10, :], ALU.add, [R(qr), R(sq)], [R(kft)])
            cp("pool", kb[0:M, :, :], kft[0:M, :, :], [R(kft)], [R(kb)])
            if kout is not None:
                for (dst, r0, r1) in kout:
                    dma("sp", dst, kft[r0:r1, :, :].rearrange("p a b -> p (a b)"), [R(kft)], [])
            yield
            if has_q:
                bt_, tt_ = bank()
                b16 = tt_[:].bitcast(BF16)
                for h in range(8):
                    tr(b16[0:64, h * 128:h * 128 + M], qro[0:M, h, :], ident[0:M, 0:M], [R(qro), Rid], [bt_])
            bk_, tk_ = bank()
            k16 = tk_[:].bitcast(BF16)
            for kv in range(2):
                tr(k16[0:64, kv * 128:kv * 128 + M], kb[0:M, kv, :], ident[0:M, 0:M], [R(kb), Rid], [bk_])
            if has_q:
                cp("act", QT[:, :, 0:M], b16[0:64, 0:1024].rearrange("p (h t) -> p h t", h=8)[:, :, 0:M], [bt_], [R(QT)])
            cp("act", KT[slot][:, :, 0:M], k16[0:64, 0:256].rearrange("p (k t) -> p k t", k=2)[:, :, 0:M], [bk_], [R(KT[slot])])

        def newton_rsqrt_ap(src, v, t, yv, scale, iters=3, Rv=None, Rt=None):
            Rv = Rv or R(nwv)
            Rt = Rt or R(nwt)
            Ry, Rs = out_y_res[0], src_res[0]
            tsc("dve", v, src, scale, EPS, ALU.mult, ALU.add, [Rs], [Rv])
            tsc("dve", t, src, 0.5 * scale, 0.5 * EPS + 0.5, ALU.mult, ALU.add, [Rs], [Rt])
            P.add("dve", lambda e: e.reciprocal(out=yv, in_=t), [Rt], [Ry])
            for _ in range(iters):
                tt("dve", t, v, yv, ALU.mult, [Rv, Ry], [Rt])
                tt("dve", t, t, yv, ALU.mult, [Rt, Ry], [Rt])
                tsc("dve", t, t, -0.5, 1.5, ALU.mult, ALU.add, [Rt], [Rt])
                tt("dve", yv, yv, t, ALU.mult, [Ry, Rt], [Ry])

        pt_ctr = [0]

        def attend(Mq, blocks, pv_plan, col0):
            ncq = len(pv_plan)
            qw = pv_plan[0][1] - pv_plan[0][0]
            allpts = []
            for kv in range(2):
                pts = []
                for (kt_ap, Rkt, vd_ap, Rvd, nk, bias_ap, Rb) in blocks:
                    bs, ts_ = bank()
                    sview = ts_[0:nk, 0:4 * Mq]
                    mm(sview, kt_ap[:, kv, 0:nk], QT[:, 4 * kv:4 * kv + 4, 0:Mq], True, True, [Rkt, R(QT)], [bs])
                    pti = pt_ctr[0] % NPTB
                    pt_ctr[0] += 1
                    ptt = PT[pti]
                    rds = [bs] + ([Rb] if bias_ap is not None else [])
                    if bias_ap is not None:
                        act(ptt[0:nk, 0:4 * Mq], sview, AF.Exp, rds, [R(ptt)], bias=bias_ap[0:nk, :], scale=0.125)
                    else:
                        act(ptt[0:nk, 0:4 * Mq], sview, AF.Exp, rds, [R(ptt)], scale=0.125)
                    pts.append(ptt)
                allpts.append(pts)
            yield
            for kv in range(2):
                pts = allpts[kv]
                bo, to = bank()
                bsu, tsu = bank()
                for ci, (q0, q1, contrib) in enumerate(pv_plan):
                    n = len(contrib)
                    osl = slice(ci * 4 * qw, (ci + 1) * 4 * qw)
                    for ii, (bi, k0, k1) in enumerate(contrib):
                        vd_ap, Rvd = blocks[bi][2], blocks[bi][3]
                        p3 = pts[bi][:, 0:4 * Mq].rearrange("p (g q) -> p g q", g=4)
                        mm(to[:, osl], vd_ap[k0:k1, kv, :], p3[k0:k1, :, q0:q1], ii == 0, ii == n - 1, [Rvd, R(pts[bi])], [bo])
                    for ii, (bi, k0, k1) in enumerate(contrib):
                        p3 = pts[bi][:, 0:4 * Mq].rearrange("p (g q) -> p g q", g=4)
                        mm(tsu[:, osl], ones[k0:k1, :], p3[k0:k1, :, q0:q1], ii == 0, ii == n - 1, [Rones, R(pts[bi])], [bsu])
                nb_ = nrm if kv == 0 else sqflat
                Rnb_ = R(nrm) if kv == 0 else R(sq)
                s4 = tsu[:, 0:4 * Mq].rearrange("p (c g q) -> p c g q", c=ncq, g=4)
                n4 = nb_[:, 0:4 * Mq].rearrange("p (c g q) -> p c g q", c=ncq, g=4)
                for ci in range(ncq):
                    tt("dve", n4[:, ci], s4[:, ci], esink[:, 4 * kv:4 * kv + 4].unsqueeze(2).to_broadcast([128, 4, qw]), ALU.add, [bsu, R(esink)], [Rnb_])
                nfl = nb_[:, 0:4 * Mq]
                P.add("dve", lambda e, nfl=nfl: e.reciprocal(out=nfl, in_=nfl), [Rnb_], [Rnb_])
                o5 = to[:, 0:4 * Mq].rearrange("p (c j e q) -> p j e c q", c=ncq, j=2, e=2)
                n5 = nb_[:, 0:4 * Mq].rearrange("p (c j e q) -> p j e c q", c=ncq, j=2, e=2)
                for e_ in range(2):
                    ps_ = slice(64 * e_, 64 * e_ + 64)
                    atv = AT[ps_, 2 * kv:2 * kv + 2, col0:col0 + Mq].rearrange("p j (c q) -> p j c q", c=ncq)
                    for ci in range(ncq):
                        tt("dve", atv[:, :, ci, :], o5[ps_, :, e_, ci, :], n5[ps_, :, e_, ci, :], ALU.mult, [bo, Rnb_], [R(AT)])

        def zT_pair(N, colA, colB, RwA, RwB, hold=False):
            br, bt = bank(hold=hold)
            for half, (col, Rw) in enumerate(((colA, RwA), (colB, RwB))):
                for c in range(8):
                    mm(bt[:, half * ST:half * ST + N], w_in[:, c, col:col + 128], hT[:, c, 0:N], c == 0, c == 7, [Rw] + Rh(0, N), [br])
            return br, bt

        tg_ctr = [0]

        def conv_branch(N, nseq, T, tail_src, is_last_prompt, is_sample, js=(0, 1, 2, 3)):
            for j in js:
                for _ in conv_gen(N, nseq, T, tail_src, is_last_prompt, is_sample, j):
                    pass

        def conv_gen(N, nseq, T, tail_src, is_last_prompt, is_sample, j):
            if True:
                b1, t1 = zT_pair(N, C_C + j * 128, C_U + j * 128, WG["c"], WG["u"], hold=True)
                b2, t2 = zT_pair(N, C_B + j * 128, C_GC + j * 128, WG["b"], WG["gC"], hold=True)
                yield
                upv = up[:, 0:nseq * (T + 2)].rearrange("p (s t) -> p s t", s=nseq)
                if tail_src is None:
                    cp("pool", upv[:, 0, 0:2], tail[:, j, :], [R(tail)], [R(up)])
                else:
                    cp("pool", upv[:, :, 0:2], tail_src[:, j, :, :], [R(stail)], [R(up)])
                cp("act", csb[:, 0:N], t1[:, 0:N], [b1], [R(csb)])
                tgt = tgate[tg_ctr[0] % 2]
                tg_ctr[0] += 1
                act(tgt[:, 0:N], t2[:, ST:ST + N], AF.Tanh, [b2], [R(tgt)], scale=0.5)
                tt("dve", upv[:, :, 2:T + 2], csb[:, 0:N].rearrange("p (s t) -> p s t", s=nseq),
                   t1[:, ST:ST + N].rearrange("p (s t) -> p s t", s=nseq), ALU.mult, [R(csb), b1], [R(up)])
                stt("dve", tgt[:, 0:N], tgt[:, 0:N], 1.0, t2[:, ST:ST + N], ALU.add, ALU.mult, [R(tgt), b2], [R(tgt)])
                tt("dve", tgt[:, 0:N], tgt[:, 0:N], t2[:, 0:N], ALU.mult, [R(tgt), b2], [R(tgt)])
                if not is_sample:
                    cp("pool", tail[:, j, :], upv[:, 0, T:T + 2], [R(up)], [R(tail)])
                    if is_last_prompt:
                        cp("pool", coutp[:, j, :], upv[:, 0, T:T + 2], [R(up)], [R(coutp)])
                else:
                    cp("pool", couts[:, j, :, :], upv[:, :, T:T + 2], [R(up)], [R(couts)])
                a3 = acc[:, 0:N].rearrange("p (s t) -> p s t", s=nseq)
                act(a3, upv[:, :, 0:T], AF.Copy, [R(up), R(convw)], [R(acc)], scale=convw[:, j, 0:1])
                stt("dve", a3, upv[:, :, 1:T + 1], convw[:, j, 1:2], a3, ALU.mult, ALU.add, [R(up), R(convw), R(acc)], [R(acc)])
                stt("dve", a3, upv[:, :, 2:T + 2], convw[:, j, 2:3], a3, ALU.mult, ALU.add, [R(up), R(convw), R(acc)], [R(acc)])
                tt("pool", PC[:, j, 0:N], acc[:, 0:N], tgt[:, 0:N], ALU.mult, [R(acc), R(tgt)], [R(PC)])
                unhold(b1)
                unhold(b2)

        def conv_state_only(N):
            for j in range(4):
                b1, t1 = zT_pair(N, C_C + j * 128, C_U + j * 128, WG["c"], WG["u"])
                cp("act", csb[:, 0:2], t1[:, N - 2:N], [b1], [R(csb)])
                tt("dve", tail[:, j, :], csb[:, 0:2], t1[:, ST + N - 2:ST + N], ALU.mult, [R(csb), b1], [R(tail)])

        def gate_a(N):
            for jj in range(2):
                br, bt = zT_pair(N, C_GA + (2 * jj) * 128, C_GA + (2 * jj + 1) * 128, WG["gA"], WG["gA"])
                for half in range(2):
                    j = 2 * jj + half
                    tgt = tgate[tg_ctr[0] % 2]
                    tg_ctr[0] += 1
                    act(tgt[:, 0:N], bt[:, half * ST:half * ST + N], AF.Tanh, [br], [R(tgt)], scale=0.5)
                    stt("dve", tgt[:, 0:N], tgt[:, 0:N], 1.0, bt[:, half * ST:half * ST + N], ALU.add, ALU.mult, [R(tgt), br], [R(tgt)])
                    tt("pool", AT[:, j, 0:N], AT[:, j, 0:N], tgt[:, 0:N], ALU.mult, [R(AT), R(tgt)], [R(AT)])

        def mprime(N, k):
            bA, tAb = bank(hold=True)
            for c in range(8):
                mm(tAb[:, 0:N], w_in[:, c, C_MA + k * 128:C_MA + (k + 1) * 128], hT[:, c, 0:N], c == 0, c == 7, [WG["mA"]] + Rh(0, N), [bA])
            bB, tBb = bank(hold=True)
            for c in range(8):
                mm(tBb[:, 0:N], w_in[:, c, C_MB + k * 128:C_MB + (k + 1) * 128], hT[:, c, 0:N], c == 0, c == 7, [WG["mB"]] + Rh(0, N), [bB])
            return (bA, tAb, bB, tBb)

        def ymerge(N, k, bks):
            bA, tAb, bB, tBb = bks
            for j in range(4):
                mm(tBb[:, ST:ST + N], w_b[:, j, k * 128:(k + 1) * 128], PC[:, j, 0:N], j == 0, j == 3, [R_wb, R(PC)], [bB])
            for j in range(4):
                mm(tAb[:, ST:ST + N], w_a[:, j, k * 128:(k + 1) * 128], AT[:, j, 0:N], j == 0, j == 3, [R_wa, R(AT)], [bA])
            act(tB[:, 0:N], tBb[:, 0:N], AF.Tanh, [bB], [R(tB)], scale=0.5)
            act(tA[:, 0:N], tAb[:, 0:N], AF.Tanh, [bA], [R(tA)], scale=0.5)
            stt("dve", tB[:, 0:N], tB[:, 0:N], 1.0, tBb[:, ST:ST + N], ALU.add, ALU.mult, [R(tB), bB], [R(tB)])
            stt("dve", tA[:, 0:N], tA[:, 0:N], 1.0, tAb[:, ST:ST + N], ALU.add, ALU.mult, [R(tA), bA], [R(tA)])
            tt("pool", mT[:, k, 0:N], tA[:, 0:N], tB[:, 0:N], ALU.add, [R(tA), R(tB)], [RmT[k]])
            unhold(bA)
            unhold(bB)

        def merge(N):
            bks = mprime(N, 0)
            for k in range(8):
                nxt = mprime(N, k + 1) if k < 7 else None
                ymerge(N, k, bks)
                bks = nxt

        pf_ctr = [0]

        def r_phase(g, col0, yrow0):
            xt = xs[x_slot_of[g]]
            pb = pb2[pf_slot[g]]
            rbanks = []
            for hh in range(2):
                br, bt = bank()
                for k in range(8):
                    mm(bt[:, :], mT[:, k, col0:col0 + 128], w_o[:, k, hh * 512:(hh + 1) * 512], k == 0, k == 7, [RmT[k], R_wo], [br])
                rbanks.append((br, bt))
            for hh in range(2):
                br, bt = rbanks[hh]
                stt("dve", xt[:, hh * 512:(hh + 1) * 512], bt[:, :], 0.25, xt[:, hh * 512:(hh + 1) * 512], ALU.mult, ALU.add, [br, R(xt)], [R(xt)])
            yield
            cp("act", rb[:], xt[:], [R(xt)], [R(rb)])
            yield
            b1, t1 = bank()
            b16 = t1[:].bitcast(BF16)
            for c in range(8):
                tr(b16[:, c * 128:(c + 1) * 128], rb[:, c * 128:(c + 1) * 128], ident[:], [R(rb), Rid], [b1])
            cp("act", rT[:].rearrange("p c t -> p (c t)"), b16[:, :], [b1], [R(rT)])
            b2, t2 = bank()
            b16b = t2[:].bitcast(BF16)
            for c in range(2):
                tr(b16b[:, c * 128:(c + 1) * 128], pb[:, c * 128:(c + 1) * 128], ident[:], [R(pb), Rid], [b2])
            act(pT[:].rearrange("p c t -> p (c t)"), b16b[:, 0:256], AF.Copy, [b2], [R(pT)], scale=0.5)
            yield
            for hh in range(2):
                bg, tgb = bank()
                for k in range(8):
                    mm(tgb[:, :], rT[:, k, :], w_pg[:, k, hh * 512:(hh + 1) * 512], k == 0, k == 7, [R(rT), R_wpg], [bg])
                bp, tpb = bank()
                for c in range(2):
                    mm(tpb[:, :], pT[:, c, :], w_pp[:, c, hh * 512:(hh + 1) * 512], c == 0, c == 1, [R(pT), R_wpp], [bp])
                sl = slice(hh * 512, (hh + 1) * 512)
                act(tg[:, :], tgb[:, :], AF.Tanh, [bg], [R(tg)], scale=0.5)
                stt("dve", tg[:, :], tg[:, :], 1.0, tpb[:, :], ALU.add, ALU.mult, [R(tg), bp], [R(tg)])
                tt("dve", xt[:, sl], xt[:, sl], tg[:, :], ALU.add, [R(tg), R(xt)], [R(xt)])
            dma("sp", y_d[yrow0:yrow0 + 128, :], xt[:], [R(xt)], [])

        def run(gen):
            for _ in gen:
                pass

        pf_slot = {}

        def load_p(g):
            i = pf_ctr[0] % 2
            pf_ctr[0] += 1
            pf_slot[g] = i
            dma("pool", pb2[i][:], ph[g * 128:(g + 1) * 128, :], [], [R(pb2[i])])

        run(norm_tile(0, 0))
        run(qkv_tile(0, 0, 128, 0, False))
        issue_w(7)
        conv_state_only(128)
        NB = TOK_PC // ST
        load_x(2)
        run(norm_tile(1, 0))
        run(norm_tile(2, 128))

        def start_qkv(blk):
            g0, g1 = 1 + 2 * blk, 2 + 2 * blk
            last = blk == NB - 1
            gens = []
            for i, g in enumerate((g0, g1)):
                kout = vout = None
                if last and i == 1:
                    kout = [(klast_d[:, :], 0, 128)]
                    vout = [(vlast_d[:, :], 0, 128)]
                qg_ = qkv_tile(g * 128, i * 128, 128, g % 3, True, kout, vout)
                next(qg_)
                gens.append(qg_)
            return gens

        def start_qkv_one(blk, i):
            g = 1 + 2 * blk + i
            last = blk == NB - 1
            kout = vout = None
            if last and i == 1:
                kout = [(klast_d[:, :], 0, 128)]
                vout = [(vlast_d[:, :], 0, 128)]
            qg_ = qkv_tile(g * 128, i * 128, 128, g % 3, True, kout, vout)
            next(qg_)
            return [qg_]

        qgen = start_qkv(0)
        issue_w(5)
        plan = [(0, 64, [(0, 0, 128), (1, 0, 64)]), (64, 128, [(0, 64, 128), (1, 0, 128)])]

        def mk_att(i, g):
            blocks = [
                (KT[(g - 1) % 3], R(KT[(g - 1) % 3]), vd[(g - 1) % 3], R(vd[(g - 1) % 3]), 128, (hbias if g == 1 else None), R(hbias)),
                (KT[g % 3], R(KT[g % 3]), vd[g % 3], R(vd[g % 3]), 128, None, None),
            ]
            return attend(128, blocks, plan, i * 128)

        gS = NPT + 1
        sgen = {}

        def sample_q1(b):
            slot = (gS + b) % 3
            g_ = qkv_tile(gS * 128 + b * 32, b * 32, 32, slot, True,
                          kout=[(ks_d[b, 96:128, :], 0, 32)], vout=[(vs_d[b, 96:128, :], 0, 32)])
            next(g_)
            return g_

        for blk in range(NB):
            g0, g1 = 1 + 2 * blk, 2 + 2 * blk
            last = blk == NB - 1
            ng0 = g0 + 2
            ng1 = g1 + 2 if not last else None
            load_p(g0)
            load_p(g1)
            load_x(ng0)
            if ng1 is not None:
                load_x(ng1)
            load_stail(2)
            issue_w(5)
            run(qgen[0])
            cg = conv_gen(ST, 1, ST, None, last, False, 0)
            next(cg)
            at0 = mk_att(0, g0)
            next(at0)
            run(cg)
            run(at0)
            run(qgen[1])
            cg = conv_gen(ST, 1, ST, None, last, False, 1)
            next(cg)
            at1 = mk_att(1, g1)
            next(at1)
            run(cg)
            run(at1)
            n0 = norm_tile(ng0, 0)
            next(n0)
            conv_branch(ST, 1, ST, None, last, False, js=(2,))
            bks = mprime(ST, 0)
            n1 = None
            if ng1 is not None:
                n1 = norm_tile(ng1, 128)
                next(n1)
            conv_branch(ST, 1, ST, None, last, False, js=(3,))
            gate_a(ST)
            for k in range(8):
                nxt = mprime(ST, k + 1) if k < 7 else None
                if k == 7:
                    next(n0)
                ymerge(ST, k, bks)
                bks = nxt
            ra = r_phase(g0, 0, (g0 - 1) * 128)
            rb_ = r_phase(g1, 128, (g1 - 1) * 128)
            next(ra)
            run(n0)
            if n1 is not None:
                next(n1)
            next(rb_)
            if n1 is not None:
                run(n1)
            next(ra)
            next(ra)
            next(rb_)
            if not last:
                qgen = start_qkv_one(blk + 1, 0)
            else:
                sgen[0] = sample_q1(0)
            run(ra)
            next(rb_)
            if not last:
                qgen.append(start_qkv_one(blk + 1, 1)[0])
            else:
                sgen[1] = sample_q1(1)
            run(rb_)
        gS = NPT + 1
        load_stail(16)
        load_p(gS)
        for b in range(4):
            dma("sp", ckf[:], ck_d[b], [], [R(ckf)])
            dma("sp", cvf[:], cv_d[b], [], [R(cvf)])
            dma("sp", ks_d[b, 0:96, :], ck_d[b, 32:128, :], [], [])
            dma("sp", vs_d[b, 0:96, :], cv_d[b, 32:128, :], [], [])
            cp("pool", ckb[:], ckf[:], [R(ckf)], [R(ckb)])
            for rr in range(2):
                cp("act", vdc[:, :, rr * 64:(rr + 1) * 64], cvf[:].rearrange("p (h d) -> p h d", h=2), [R(cvf)], [R(vdc)])
            bt_, tt_ = bank()
            b16 = tt_[:].bitcast(BF16)
            for kv in range(2):
                tr(b16[0:64, kv * 128:(kv + 1) * 128], ckb[:, kv * 64:(kv + 1) * 64], ident[:], [R(ckb), Rid], [bt_])
            cp("act", KTc[:].rearrange("p k t -> p (k t)"), b16[0:64, 0:256], [bt_], [R(KTc)])
            slot = (gS + b) % 3
            run(sgen[b])
            blocks = [
                (KTc, R(KTc), vdc, R(vdc), 128, None, None),
                (KT[slot], R(KT[slot]), vd[slot], R(vd[slot]), 32, None, None),
            ]
            plan = [(0, 32, [(0, 0, 128), (1, 0, 32)])]
            cg = conv_gen(128, 4, 32, stail, False, True, b)
            next(cg)
            at_ = attend(32, blocks, plan, b * 32)
            next(at_)
            if b + 2 < 4:
                sgen[b + 2] = sample_q1(b + 2)
            run(cg)
            run(at_)
        gate_a(128)
        merge(128)
        run(r_phase(gS, 0, TOK_PC))
        for j in range(4):
            dma("sp", clast_d[:, j * 128:(j + 1) * 128].rearrange("t p -> p t"), coutp[:, j, :], [R(coutp)], [], slow=True)
            for b in range(4):
                dma("sp", cs_d[b, :, j * 128:(j + 1) * 128].rearrange("t p -> p t"), couts[:, j, b, :], [R(couts)], [], slow=True)

        sems = {}
        for nme in P.sem_names:
            sems[nme] = es.enter_context(nc.semaphore(nme))
        esem = {}
        sigcount = {}
        for e in ENGS:
            esem[e] = es.enter_context(nc.semaphore("eng_" + e))
            c = 0
            for op in P.ops[e]:
                if op.kind == "op" and op.sig:
                    c += 1
                    op.val = c
            sigcount[e] = c
        block = es.enter_context(nc.Block())

        def emit(ename, eng):
            inline_ok = ename in ("dve", "pool", "act", "pe")
            for op in P.ops[ename]:
                ws = list(op.waits)
                last = None
                if inline_ok and ws and op.kind == "op" and id(op) not in MULTI:
                    last = ws.pop()
                for w in ws:
                    if w[0] == "sem":
                        eng.wait_ge(sems[w[1]], w[2])
                    else:
                        X = w[1]
                        eng.wait_ge(esem[X.eng], X.val)
                ins = op.fn(eng)
                if last is not None:
                    if last[0] == "sem":
                        ins._wait_ge(sems[last[1]], last[2])
                    else:
                        ins._wait_ge(esem[last[1].eng], last[1].val)
                if op.kind == "dmaop":
                    ins.then_inc(sems[op.inc], 16)
                elif op.sig:
                    ins.then_inc(esem[ename], 1)
            if ename == "sp":
                for Rr in P.store_res:
                    eng.wait_ge(sems[Rr.rsem], Rr.rcount)
                if P.final_count:
                    eng.wait_ge(sems[P.final_sem], P.final_count)

        @block.sync
        def _(eng):
            emit("sp", eng)

        @block.scalar
        def _(eng):
            emit("act", eng)

        @block.vector
        def _(eng):
            emit("dve", eng)

        @block.gpsimd
        def _(eng):
            emit("pool", eng)

        @block.tensor
        def _(eng):
            emit("pe", eng)

    return nc


_NC_CACHE = {}


def kernel(x_prompt, x_sample, p_prompt, p_sample, cache_k, cache_v, state_conv,
           ln_g, w_in, q_norm_g, k_norm_g, sink, conv_w, w_attn_out, w_conv_out, w_o,
           w_ple_gate, w_ple_proj):
    f = np.float32
    xp = np.asarray(x_prompt, f).reshape(SEQ, D)
    xsm = np.asarray(x_sample, f).reshape(32 * 32, D)
    pp = np.asarray(p_prompt, f).reshape(SEQ, 256)
    psm = np.asarray(p_sample, f).reshape(32 * 32, 256)
    ck = np.asarray(cache_k, f).reshape(32, 128, 128)
    cv = np.asarray(cache_v, f).reshape(32, 128, 128)
    sc = np.asarray(state_conv, f).reshape(32, 2, 512)
    def rope_rows(pos):
        try:
            import jax
            import jax.numpy as jnp
            with jax.default_device(jax.devices("cpu")[0]):
                inv_freq = 10000.0 ** (-jnp.arange(0, 32, dtype=jnp.float32) * 2.0 / 64)
                ang = jnp.asarray(pos.astype(np.float32))[:, None] * inv_freq[None, :]
                c = np.asarray(jnp.cos(ang), dtype=f)
                s = np.asarray(jnp.sin(ang), dtype=f)
        except Exception:
            inv_freq = 10000.0 ** (-np.arange(0, 32, dtype=np.float64) * 2.0 / 64)
            ang = pos.astype(np.float64)[:, None] * inv_freq[None, :]
            c, s = np.cos(ang).astype(f), np.sin(ang).astype(f)
        return np.concatenate([c, c, -s, s], axis=1).astype(f)

    if "nc" not in _NC_CACHE:
        _NC_CACHE["nc"] = build_program()
    nc = _NC_CACHE["nc"]
    shared = {
        "ln_g": np.asarray(ln_g, f).reshape(D), "w_in": np.asarray(w_in, f).reshape(D, IN_DIM),
        "qg": np.asarray(q_norm_g, f).reshape(64), "kg": np.asarray(k_norm_g, f).reshape(64),
        "sink": np.asarray(sink, f).reshape(8), "conv_w": np.asarray(conv_w, f).reshape(3, 512),
        "w_a": np.asarray(w_attn_out, f).reshape(512, D), "w_b": np.asarray(w_conv_out, f).reshape(512, D),
        "w_o": np.asarray(w_o, f).reshape(D, D), "w_pg": np.asarray(w_ple_gate, f).reshape(D, D),
        "w_pp": np.asarray(w_ple_proj, f).reshape(256, D),
    }
    in_maps = []
    for c in range(NCORES):
        t0 = c * TOK_PC
        xhc = np.zeros((NROWS, D), f)
        phc = np.zeros((NROWS, 256), f)
        pos = np.zeros((NROWS,), np.float64)
        if c > 0:
            xhc[0:128] = xp[t0 - 128:t0]
        pos[0:128] = np.arange(t0 - 128, t0)
        xhc[128:128 + TOK_PC] = xp[t0:t0 + TOK_PC]
        phc[128:128 + TOK_PC] = pp[t0:t0 + TOK_PC]
        pos[128:128 + TOK_PC] = np.arange(t0, t0 + TOK_PC)
        xhc[128 + TOK_PC:] = xsm[c * 128:(c + 1) * 128]
        phc[128 + TOK_PC:] = psm[c * 128:(c + 1) * 128]
        pos[128 + TOK_PC:] = np.tile(1024 + np.arange(32), 4)
        hb = np.full((128, 1), -30000.0 if c == 0 else 0.0, f)
        m = dict(shared)
        m.update({"xh": xhc, "ph": phc, "rope": rope_rows(pos),
                  "ck": np.ascontiguousarray(ck[c * 4:(c + 1) * 4]), "cv": np.ascontiguousarray(cv[c * 4:(c + 1) * 4]),
                  "sconv": np.ascontiguousarray(sc[c * 4:(c + 1) * 4]), "hbias": hb})
        in_maps.append(m)
    out = run_bass_kernel_spmd(nc, in_maps, core_ids=list(range(NCORES)))
    rs = out.results
    y_prompt = np.concatenate([r["y"][0:TOK_PC] for r in rs], axis=0).reshape(1, SEQ, D)
    y_sample = np.concatenate([r["y"][TOK_PC:] for r in rs], axis=0).reshape(32, 32, D)
    k_prompt = rs[-1]["k_last"].reshape(1, 1, 128, 2, 64)
    v_prompt = rs[-1]["v_last"].reshape(1, 1, 128, 2, 64)
    conv_prompt = rs[-1]["c_last"].reshape(1, 1, 2, 512)
    k_sample = np.concatenate([r["ks"] for r in rs], axis=0).reshape(1, 32, 128, 2, 64)
    v_sample = np.concatenate([r["vs"] for r in rs], axis=0).reshape(1, 32, 128, 2, 64)
    conv_sample = np.concatenate([r["cs"] for r in rs], axis=0).reshape(1, 32, 2, 512)
    return (y_prompt.astype(f), y_sample.astype(f), k_prompt.astype(f), v_prompt.astype(f),
            conv_prompt.astype(f), k_sample.astype(f), v_sample.astype(f), conv_sample.astype(f))
```

```python
import contextlib
import numpy as np
import concourse.bass as bass
import concourse.mybir as mybir
from concourse.bass_utils import run_bass_kernel_spmd

F32 = mybir.dt.float32
BF16 = mybir.dt.bfloat16
AF = mybir.ActivationFunctionType
ALU = mybir.AluOpType
AX = mybir.AxisListType

NCORES = 8
D = 1024
SEQ = 16384
TOK_PC = SEQ // NCORES
NPT = TOK_PC // 128
NROWS = 128 + TOK_PC + 128
ST = 256
IN_DIM = 5376
EPS = 1e-6
C_Q, C_K, C_V, C_GA, C_B, C_C, C_U, C_GC, C_MA, C_MB = 0, 512, 640, 768, 1280, 1792, 2304, 2816, 3328, 4352

ENGS = ("sp", "act", "dve", "pool", "pe")
STRICT_SAME_ENGINE = True
STRICT_ENGINES = ("pool", "dve")


class Res:
    def __init__(self, name):
        self.name = name
        self.w = None
        self.readers = {}
        self.rd_dma = None
        self.dsem = None
        self.dcount = 0
        self.rsem = None
        self.rcount = 0


class Op:
    __slots__ = ("kind", "eng", "idx", "fn", "waits", "sig", "inc", "sem", "val", "know")

    def __init__(self, kind, eng, idx, fn):
        self.kind = kind
        self.eng = eng
        self.idx = idx
        self.fn = fn
        self.waits = []
        self.sig = False
        self.inc = None
        self.sem = None
        self.val = 0
        self.know = {}


class Mark:
    kind = "dma"

    def __init__(self, sem, val, know=None):
        self.sem = sem
        self.val = val
        self.know = know or {}


class Planner:
    def __init__(self):
        self.ops = {e: [] for e in ENGS}
        self.seen = {e: {} for e in ENGS}
        self.seen_sem = {e: {} for e in ENGS}
        self.sem_names = []
        self.final_sem = self.new_sem("final")
        self.final_count = 0
        self.store_res = []

    def new_sem(self, name):
        n = "s%d_%s" % (len(self.sem_names), name)
        self.sem_names.append(n)
        return n

    def _dep(self, eng, waits, X, raw):
        if X is None:
            return
        if X.kind == "dma":
            if self.seen_sem[eng].get(X.sem, 0) >= X.val:
                return
            self.seen_sem[eng][X.sem] = X.val
            waits.append(("sem", X.sem, X.val))
            self._learn(eng, X.know)
            return
        if X.eng == eng and (eng == "pe" or not (raw or (STRICT_SAME_ENGINE and eng in STRICT_ENGINES))):
            return
        if self.seen[eng].get(X.eng, -1) >= X.idx:
            return
        self.seen[eng][X.eng] = X.idx
        X.sig = True
        waits.append(("op", X))
        self._learn(eng, X.know)

    def _learn(self, eng, know):
        se = self.seen[eng]
        for g, i in know.items():
            if se.get(g, -1) < i:
                se[g] = i

    def _deps(self, eng, reads, writes):
        waits = []
        for r in reads:
            self._dep(eng, waits, r.w, True)
        for w in writes:
            self._dep(eng, waits, w.w, False)
            for o in w.readers.values():
                self._dep(eng, waits, o, False)
            if w.rd_dma is not None:
                self._dep(eng, waits, w.rd_dma, False)
                w.rd_dma = None
        return waits

    def add(self, eng, fn, reads=(), writes=()):
        op = Op("op", eng, len(self.ops[eng]), fn)
        op.waits = self._deps(eng, reads, writes)
        self.ops[eng].append(op)
        op.know = dict(self.seen[eng])
        op.know[eng] = op.idx
        for r in reads:
            r.readers[eng] = op
        for w in writes:
            w.w = op
            w.readers = {}
        return op

    def dma(self, eng, fn, reads=(), writes=()):
        op = Op("dmaop", eng, len(self.ops[eng]), fn)
        op.waits = self._deps(eng, reads, writes)
        self.ops[eng].append(op)
        if writes:
            W = writes[0]
            if W.dsem is None:
                W.dsem = self.new_sem("d_" + W.name)
            W.dcount += 16
            op.inc = W.dsem
            W.w = Mark(W.dsem, W.dcount, dict(self.seen[eng]))
            W.readers = {}
        elif reads:
            R = reads[0]
            if R.rsem is None:
                R.rsem = self.new_sem("r_" + R.name)
                self.store_res.append(R)
            R.rcount += 16
            op.inc = R.rsem
            R.rd_dma = Mark(R.rsem, R.rcount, dict(self.seen[eng]))
        else:
            self.final_count += 16
            op.inc = self.final_sem
        return op


def build_program():
    nc = bass.Bass("TRN2", target_bir_lowering=False)
    P = Planner()

    def din(name, shape):
        return nc.dram_tensor(name, list(shape), F32, kind="ExternalInput").ap()

    def dout(name, shape):
        return nc.dram_tensor(name, list(shape), F32, kind="ExternalOutput").ap()

    xh = din("xh", [NROWS, D])
    ph = din("ph", [NROWS, 256])
    rope = din("rope", [NROWS, 128])
    ck_d = din("ck", [4, 128, 128])
    cv_d = din("cv", [4, 128, 128])
    sconv_d = din("sconv", [4, 2, 512])
    hbias_d = din("hbias", [128, 1])
    lng_d = din("ln_g", [D])
    win_d = din("w_in", [D, IN_DIM])
    qg_d = din("qg", [64])
    kg_d = din("kg", [64])
    sink_d = din("sink", [8])
    convw_d = din("conv_w", [3, 512])
    wa_d = din("w_a", [512, D])
    wb_d = din("w_b", [512, D])
    wo_d = din("w_o", [D, D])
    wpg_d = din("w_pg", [D, D])
    wpp_d = din("w_pp", [256, D])

    y_d = dout("y", [TOK_PC + 128, D])
    klast_d = dout("k_last", [128, 128])
    vlast_d = dout("v_last", [128, 128])
    clast_d = dout("c_last", [2, 512])
    ks_d = dout("ks", [4, 128, 128])
    vs_d = dout("vs", [4, 128, 128])
    cs_d = dout("cs", [4, 2, 512])

    es = contextlib.ExitStack()
    with es:
        res = {}

        def sb(name, shape, dt):
            t = es.enter_context(nc.sbuf_tensor("t_" + name, list(shape), dt))
            res["t_" + name] = Res(name)
            return t

        w_in = sb("w_in_sb", [128, 8, IN_DIM], BF16)
        w_a = sb("w_a_sb", [128, 4, D], BF16)
        w_b = sb("w_b_sb", [128, 4, D], BF16)
        w_o = sb("w_o_sb", [128, 8, D], BF16)
        w_pg = sb("w_pg_sb", [128, 8, D], BF16)
        w_pp = sb("w_pp_sb", [128, 2, D], BF16)
        WG = {}
        for nm in ("qkv", "gA", "b", "c", "u", "gC", "mA", "mB"):
            WG[nm] = Res("win_" + nm)
        R_wa, R_wb, R_wo, R_wpg, R_wpp = res["t_w_a_sb"], res["t_w_b_sb"], res["t_w_o_sb"], res["t_w_pg_sb"], res["t_w_pp_sb"]

        ident = sb("ident", [128, 128], BF16)
        identf = sb("identf", [128, 128], F32)
        ones = sb("ones", [128, 128], BF16)
        g_ln = sb("g_ln", [128, 8], F32)
        gq = sb("gq", [128, 64], F32)
        rgq = sb("rgq", [128, 64], F32)
        gk = sb("gk", [128, 64], F32)
        rgk = sb("rgk", [128, 64], F32)
        esink = sb("esink", [128, 8], F32)
        convw = sb("convw", [128, 4, 3], F32)
        hbias = sb("hbias", [128, 1], F32)
        stail = sb("stail", [128, 4, 4, 2], F32)

        NXS = 4
        xs = [sb("xs%d" % i, [128, D], F32) for i in range(NXS)]
        xn = sb("xn", [128, D], BF16)
        sx = [sb("sx%d" % i, [128, 4], F32) for i in range(2)]
        hT = sb("hT", [128, 8, ST], BF16)
        csl = [sb("cs%d" % i, [128, 128], F32) for i in range(2)]
        TCq = sb("TCq", [128, 64], F32)
        TSq = sb("TSq", [128, 64], F32)
        TCk = sb("TCk", [128, 64], F32)
        TSk = sb("TSk", [128, 64], F32)
        sq = sb("sq", [128, 10, 64], F32)
        sqflat = sq[:].rearrange("p h d -> p (h d)")
        qr = sb("qr", [128, 10, 64], F32)
        ssq = sb("ssq", [128, 10], F32)
        nwv = sb("nwv", [128, 10], F32)
        nwt = sb("nwt", [128, 10], F32)
        nwy = sb("nwy", [128, 10], F32)
        qro2 = [sb("qro%d" % i, [128, 8, 64], BF16) for i in range(2)]
        kf = [sb("kf%d" % i, [128, 2, 64], F32) for i in range(2)]
        kb2 = [sb("kb%d" % i, [128, 2, 64], BF16) for i in range(2)]
        vf = [sb("vf0", [128, 128], F32)] * 2
        vd = [sb("vd%d" % i, [128, 2, 128], BF16) for i in range(3)]
        QT = sb("QT", [64, 8, 128], BF16)
        KT = [sb("KT%d" % i, [64, 2, 128], BF16) for i in range(3)]
        ckf = sb("ckf", [128, 128], F32)
        ckb = sb("ckb", [128, 128], BF16)
        cvf = sb("cvf", [128, 128], F32)
        KTc = sb("KTc", [64, 2, 128], BF16)
        vdc = sb("vdc", [128, 2, 128], BF16)
        NPTB = 4
        PT = [sb("PT%d" % i, [128, 512], BF16) for i in range(NPTB)]
        nrm = sb("nrm", [128, 512], F32)
        AT = sb("AT", [128, 4, ST], BF16)
        tgate = [sb("tgate%d" % i, [128, ST], F32) for i in range(2)]
        csb = sb("csb", [128, ST], F32)
        up = sb("up", [128, ST + 8], F32)
        acc = sb("acc", [128, ST], F32)
        PC = sb("PC", [128, 4, ST], BF16)
        tail = sb("tail", [128, 4, 2], F32)
        couts = sb("couts", [128, 4, 4, 2], F32)
        coutp = sb("coutp", [128, 4, 2], F32)
        tA = sb("tA", [128, ST], F32)
        tB = sb("tB", [128, ST], F32)
        mT = sb("mT", [128, 8, ST], BF16)
        rb = sb("rb", [128, D], BF16)
        rT = sb("rT", [128, 8, 128], BF16)
        tg = sb("tg", [128, 512], F32)
        pb2 = [sb("pb%d" % i, [128, 256], BF16) for i in range(2)]
        pT = sb("pT", [128, 2, 128], BF16)

        def R(t):
            return res[t.name] if hasattr(t, "name") else t

        RhA, RhB = Res("hT_a"), Res("hT_b")
        RmT = [Res("mT%d" % k) for k in range(8)]

        def Rh(c0, n):
            out = []
            if c0 < 128:
                out.append(RhA)
            if c0 + n > 128:
                out.append(RhB)
            return out

        banks = []
        for i in range(8):
            t = es.enter_context(nc.psum_tensor("ps%d" % i, [128, 512], F32))
            banks.append((Res("ps%d" % i), t))
        bank_ctr = [0]

        held = set()

        def bank(hold=False):
            while True:
                i = bank_ctr[0] % 8
                bank_ctr[0] += 1
                if i not in held:
                    break
            if hold:
                held.add(i)
            r, t = banks[i]
            return r, t

        def unhold(r):
            for i, (rr, _) in enumerate(banks):
                if rr is r:
                    held.discard(i)

        MULTI = set()
        def mm(out, lhsT, rhs, start, stop, reads, writes):
            P.add("pe", lambda e: e.matmul(out, lhsT=lhsT, rhs=rhs, start=start, stop=stop), reads, writes)

        def tr(out, in_, idn, reads, writes):
            P.add("pe", lambda e: e.transpose(out=out, in_=in_, identity=idn), reads, writes)

        def act(out, in_, func, reads, writes, bias=None, scale=None, accum_out=None):
            kw = {}
            if bias is not None:
                kw["bias"] = bias
            if scale is not None:
                kw["scale"] = scale
            if accum_out is not None:
                kw["accum_out"] = accum_out
            op_ = P.add("act", lambda e: e.activation(out=out, in_=in_, func=func, **kw), reads, writes)
            if accum_out is not None:
                MULTI.add(id(op_))

        def tt(eng, out, in0, in1, op, reads, writes):
            P.add(eng, lambda e: e.tensor_tensor(out=out, in0=in0, in1=in1, op=op), reads, writes)

        def tsc(eng, out, in0, s1, s2, op0, op1, reads, writes):
            if op1 is None:
                P.add(eng, lambda e: e.tensor_scalar(out=out, in0=in0, scalar1=s1, scalar2=None, op0=op0), reads, writes)
            else:
                P.add(eng, lambda e: e.tensor_scalar(out=out, in0=in0, scalar1=s1, scalar2=s2, op0=op0, op1=op1), reads, writes)

        def stt(eng, out, in0, scalar, in1, op0, op1, reads, writes):
            P.add(eng, lambda e: e.scalar_tensor_tensor(out=out, in0=in0, scalar=scalar, in1=in1, op0=op0, op1=op1), reads, writes)

        def cp(eng, out, in_, reads, writes):
            if eng == "act":
                act(out, in_, AF.Copy, reads, writes)
            else:
                P.add(eng, lambda e: e.tensor_copy(out=out, in_=in_), reads, writes)

        def dma(eng, out, in_, reads, writes, slow=False):
            if slow:
                P.dma(eng, lambda e: e.dma_start(out=out, in_=in_, allow_slow_non_contiguous=True), reads, writes)
            else:
                P.dma(eng, lambda e: e.dma_start(out=out, in_=in_), reads, writes)

        Rid, Ridf, Rones = R(ident), R(identf), R(ones)
        P.add("pool", lambda e: e.memset(identf[:], 1.0), [], [Ridf])
        P.add("pool", lambda e: e.affine_select(out=identf[:], in_=identf[:], pattern=[[-1, 128]],
                                                compare_op=ALU.is_equal, fill=0.0, base=0, channel_multiplier=1),
              [Ridf], [Ridf])
        cp("pool", ident[:], identf[:], [Ridf], [Rid])
        P.add("pool", lambda e: e.memset(ones[:], 1.0), [], [Rones])
        P.add("pool", lambda e: e.memset(tail[:], 0.0), [], [R(tail)])

        dma("sp", xs[0][:], xh[0:128, :], [], [R(xs[0])])
        dma("sp", g_ln[:], lng_d.rearrange("(c p) -> p c", p=128), [], [R(g_ln)], slow=True)
        dma("sp", gq[:], qg_d.partition_broadcast(128), [], [R(gq)])
        dma("sp", gk[:], kg_d.partition_broadcast(128), [], [R(gk)])
        dma("sp", rgq[:, 0:32], qg_d[32:64].partition_broadcast(128), [], [R(rgq)])
        dma("sp", rgq[:, 32:64], qg_d[0:32].partition_broadcast(128), [], [R(rgq)])
        dma("sp", rgk[:, 0:32], kg_d[32:64].partition_broadcast(128), [], [R(rgk)])
        dma("sp", rgk[:, 32:64], kg_d[0:32].partition_broadcast(128), [], [R(rgk)])
        dma("sp", esink[:], sink_d.partition_broadcast(128), [], [R(esink)])
        dma("sp", hbias[:], hbias_d, [], [R(hbias)])
        for j in range(4):
            dma("act", convw[:, j, :], convw_d[:, j * 128:(j + 1) * 128].rearrange("t p -> p t"), [], [R(convw)], slow=True)
        act(esink[:], esink[:], AF.Exp, [R(esink)], [R(esink)])

        xs_ctr = [0]
        x_slot_of = {}

        def load_x(g):
            i = xs_ctr[0] % NXS
            xs_ctr[0] += 1
            x_slot_of[g] = i
            if g == 0:
                return
            dma("sp", xs[i][:], xh[g * 128:(g + 1) * 128, :], [], [R(xs[i])])

        def load_win(nm, c0, c1):
            step = 512
            for a in range(c0, c1, step):
                b = min(c1, a + step)
                dma("pool", w_in[:, :, a:b], win_d[:, a:b].rearrange("(c p) n -> p c n", p=128), [], [WG[nm]])

        load_x(0)
        load_x(1)
        wq = []
        wq.append(lambda: load_win("qkv", 0, 768))
        wq.append(lambda: load_win("c", C_C, C_C + 512))
        wq.append(lambda: load_win("u", C_U, C_U + 512))
        wq.append(lambda: load_win("b", C_B, C_B + 512))
        wq.append(lambda: load_win("gC", C_GC, C_GC + 512))
        wq.append(lambda: load_win("gA", C_GA, C_GA + 512))
        wq.append(lambda: dma("pool", w_b[:], wb_d.rearrange("(c p) n -> p c n", p=128), [], [R_wb]))
        wq.append(lambda: dma("pool", w_a[:], wa_d.rearrange("(c p) n -> p c n", p=128), [], [R_wa]))
        wq.append(lambda: load_win("mA", C_MA, C_MA + 1024))
        wq.append(lambda: load_win("mB", C_MB, C_MB + 1024))
        for hh in range(2):
            wq.append(lambda hh=hh: dma("pool", w_o[:, :, hh * 512:(hh + 1) * 512], wo_d[:, hh * 512:(hh + 1) * 512].rearrange("(c p) n -> p c n", p=128), [], [R_wo]))
        for hh in range(2):
            wq.append(lambda hh=hh: dma("pool", w_pg[:, :, hh * 512:(hh + 1) * 512], wpg_d[:, hh * 512:(hh + 1) * 512].rearrange("(c p) n -> p c n", p=128), [], [R_wpg]))
        wq.append(lambda: dma("pool", w_pp[:], wpp_d.rearrange("(c p) n -> p c n", p=128), [], [R_wpp]))

        def issue_w(n):
            for _ in range(n):
                if wq:
                    wq.pop(0)()

        issue_w(3)
        stail_jobs = [(b, j) for b in range(4) for j in range(4)]

        def load_stail(n):
            for _ in range(n):
                if stail_jobs:
                    b, j = stail_jobs.pop(0)
                    dma("sp", stail[:, j, b, :], sconv_d[b, :, j * 128:(j + 1) * 128].rearrange("t p -> p t"), [], [R(stail)], slow=True)

        out_y_res = [None]
        src_res = [None]

        norm_ctr = [0]

        def norm_tile(g, col0):
            i = x_slot_of[g]
            xt = xs[i]
            s = sx[norm_ctr[0] % 2]
            norm_ctr[0] += 1
            P.add("pool", lambda e: e.memset(s[:], 0.0), [], [R(s)])
            act(xn[:], xt[:], AF.Square, [R(xt), R(s)], [R(xn), R(s)], accum_out=s[:, 0:1])
            src_res[0] = R(s)
            out_y_res[0] = R(s)
            newton_rsqrt_ap(s[:, 0:1], s[:, 2:3], s[:, 3:4], s[:, 1:2], 1.0 / D, iters=2, Rv=R(s), Rt=R(s))
            yield
            act(xn[:], xt[:], AF.Copy, [R(xt), R(s)], [R(xn)], scale=s[:, 1:2])
            yield
            br, bt = bank()
            b16 = bt[:].bitcast(BF16)
            for c in range(8):
                tr(b16[:, c * 128:(c + 1) * 128], xn[:, c * 128:(c + 1) * 128], ident[:], [R(xn), Rid], [br])
            tt("dve", hT[:, :, col0:col0 + 128], b16.rearrange("p (c t) -> p c t", c=8),
               g_ln[:].unsqueeze(2).to_broadcast([128, 8, 128]), ALU.mult, [br, R(g_ln)], Rh(col0, 128))

        cs_ctr = [0]
        kf_ctr = [0]

        def qkv_tile(row0, col0, M, slot, has_q, kout=None, vout=None):
            ci = cs_ctr[0] % 2
            cs_ctr[0] += 1
            cst = csl[ci]
            dma("sp", cst[0:M, :], rope[row0:row0 + M, :], [], [R(cst)])
            Rw = WG["qkv"]
            if has_q:
                bq, tq = bank()
                for c in range(8):
                    mm(tq[0:M, :], hT[:, c, col0:col0 + M], w_in[:, c, 0:512], c == 0, c == 7, Rh(col0, M) + [Rw], [bq])
            bkv, tkv = bank()
            for c in range(8):
                mm(tkv[0:M, 0:256], hT[:, c, col0:col0 + M], w_in[:, c, 512:768], c == 0, c == 7, Rh(col0, M) + [Rw], [bkv])
            h0 = 0 if has_q else 8
            if has_q:
                tt("pool", TCq[0:M, :], cst[0:M, 0:64], gq[0:M, :], ALU.mult, [R(cst), R(gq)], [R(TCq)])
                tt("pool", TSq[0:M, :], cst[0:M, 64:128], rgq[0:M, :], ALU.mult, [R(cst), R(rgq)], [R(TSq)])
            tt("pool", TCk[0:M, :], cst[0:M, 0:64], gk[0:M, :], ALU.mult, [R(cst), R(gk)], [R(TCk)])
            tt("pool", TSk[0:M, :], cst[0:M, 64:128], rgk[0:M, :], ALU.mult, [R(cst), R(rgk)], [R(TSk)])
            if has_q:
                act(nrm[0:M, :], tq[0:M, :], AF.Square, [bq], [R(nrm)])
                P.add("dve", lambda e: e.tensor_reduce(out=ssq[0:M, 0:8], in_=nrm[0:M, :].rearrange("p (h d) -> p h d", h=8), axis=AX.X, op=ALU.add),
                      [R(nrm)], [R(ssq)])
            act(csb[0:M, 0:128], tkv[0:M, 0:128], AF.Square, [bkv], [R(csb)])
            P.add("dve", lambda e: e.tensor_reduce(out=ssq[0:M, 8:10], in_=csb[0:M, 0:128].rearrange("p (h d) -> p h d", h=2), axis=AX.X, op=ALU.add),
                  [R(csb)], [R(ssq)])
            src_res[0] = R(ssq)
            out_y_res[0] = R(nwy)
            n = 10 - h0
            v, t, yv = nwv[0:M, 0:n], nwt[0:M, 0:n], nwy[0:M, 0:n]
            newton_rsqrt_ap(ssq[0:M, h0:10], v, t, yv, 1.0 / 64)
            if has_q:
                tt("dve", qr[0:M, 0:8, :], tq[0:M, :].rearrange("p (h d) -> p h d", h=8),
                   nwy[0:M, 0:8].unsqueeze(2).to_broadcast([M, 8, 64]), ALU.mult, [bq, R(nwy)], [R(qr)])
            tt("dve", qr[0:M, 8:10, :], tkv[0:M, 0:128].rearrange("p (h d) -> p h d", h=2),
               nwy[0:M, 8 - h0:10 - h0].unsqueeze(2).to_broadcast([M, 2, 64]), ALU.mult, [bkv, R(nwy)], [R(qr)])
            kfi = kf_ctr[0] % 2
            kf_ctr[0] += 1
            vft, kft = vf[kfi], kf[kfi]
            qro, kb = qro2[kfi], kb2[kfi]
            vdt = vd[slot]
            for rr in range(2):
                act(vdt[0:M, :, rr * 64:(rr + 1) * 64], tkv[0:M, 128:256].rearrange("p (h d) -> p h d", h=2), AF.Copy, [bkv], [R(vdt)])
            if vout is not None:
                act(vft[0:M, :], tkv[0:M, 128:256], AF.Copy, [bkv], [R(vft)])
                for (dst, r0, r1) in vout:
                    dma("sp", dst, vft[r0:r1, :], [R(vft)], [])
            if has_q:
                tt("pool", sq[0:M, 0:8, 0:32], qr[0:M, 0:8, 32:64], TSq[0:M, 0:32].unsqueeze(1).to_broadcast([M, 8, 32]), ALU.mult, [R(qr), R(TSq)], [R(sq)])
                tt("pool", sq[0:M, 0:8, 32:64], qr[0:M, 0:8, 0:32], TSq[0:M, 32:64].unsqueeze(1).to_broadcast([M, 8, 32]), ALU.mult, [R(qr), R(TSq)], [R(sq)])
            tt("pool", sq[0:M, 8:10, 0:32], qr[0:M, 8:10, 32:64], TSk[0:M, 0:32].unsqueeze(1).to_broadcast([M, 2, 32]), ALU.mult, [R(qr), R(TSk)], [R(sq)])
            tt("pool", sq[0:M, 8:10, 32:64], qr[0:M, 8:10, 0:32], TSk[0:M, 32:64].unsqueeze(1).to_broadcast([M, 2, 32]), ALU.mult, [R(qr), R(TSk)], [R(sq)])
            if has_q:
                tt("pool", qr[0:M, 0:8, :], qr[0:M, 0:8, :], TCq[0:M, :].unsqueeze(1).to_broadcast([M, 8, 64]), ALU.mult, [R(qr), R(TCq)], [R(qr)])
            tt("pool", qr[0:M, 8:10, :], qr[0:M, 8:10, :], TCk[0:M, :].unsqueeze(1).to_broadcast([M, 2, 64]), ALU.mult, [R(qr), R(TCk)], [R(qr)])
            if has_q:
                tt("pool", qro[0:M, :, :], qr[0:M, 0:8, :], sq[0:M, 0:8, :], ALU.add, [R(qr), R(sq)], [R(qro)])
            tt("pool", kft[0:M, :, :], qr[0:M, 8:10, :], sq[0:M, 8:10, :], ALU.add, [R(qr), R(sq)], [R(kft)])
            cp("pool", kb[0:M, :, :], kft[0:M, :, :], [R(kft)], [R(kb)])
            if kout is not None:
                for (dst, r0, r1) in kout:
                    dma("sp", dst, kft[r0:r1, :, :].rearrange("p a b -> p (a b)"), [R(kft)], [])
            yield
            if has_q:
                bt_, tt_ = bank()
                b16 = tt_[:].bitcast(BF16)
                for h in range(8):
                    tr(b16[0:64, h * 128:h * 128 + M], qro[0:M, h, :], ident[0:M, 0:M], [R(qro), Rid], [bt_])
            bk_, tk_ = bank()
            k16 = tk_[:].bitcast(BF16)
            for kv in range(2):
                tr(k16[0:64, kv * 128:kv * 128 + M], kb[0:M, kv, :], ident[0:M, 0:M], [R(kb), Rid], [bk_])
            if has_q:
                cp("act", QT[:, :, 0:M], b16[0:64, 0:1024].rearrange("p (h t) -> p h t", h=8)[:, :, 0:M], [bt_], [R(QT)])
            cp("act", KT[slot][:, :, 0:M], k16[0:64, 0:256].rearrange("p (k t) -> p k t", k=2)[:, :, 0:M], [bk_], [R(KT[slot])])

        def newton_rsqrt_ap(src, v, t, yv, scale, iters=3, Rv=None, Rt=None):
            Rv = Rv or R(nwv)
            Rt = Rt or R(nwt)
            Ry, Rs = out_y_res[0], src_res[0]
            tsc("dve", v, src, scale, EPS, ALU.mult, ALU.add, [Rs], [Rv])
            tsc("dve", t, src, 0.5 * scale, 0.5 * EPS + 0.5, ALU.mult, ALU.add, [Rs], [Rt])
            P.add("dve", lambda e: e.reciprocal(out=yv, in_=t), [Rt], [Ry])
            for _ in range(iters):
                tt("dve", t, v, yv, ALU.mult, [Rv, Ry], [Rt])
                tt("dve", t, t, yv, ALU.mult, [Rt, Ry], [Rt])
                tsc("dve", t, t, -0.5, 1.5, ALU.mult, ALU.add, [Rt], [Rt])
                tt("dve", yv, yv, t, ALU.mult, [Ry, Rt], [Ry])

        pt_ctr = [0]

        def attend(Mq, blocks, pv_plan, col0):
            ncq = len(pv_plan)
            qw = pv_plan[0][1] - pv_plan[0][0]
            allpts = []
            for kv in range(2):
                pts = []
                for (kt_ap, Rkt, vd_ap, Rvd, nk, bias_ap, Rb) in blocks:
                    bs, ts_ = bank()
                    sview = ts_[0:nk, 0:4 * Mq]
                    mm(sview, kt_ap[:, kv, 0:nk], QT[:, 4 * kv:4 * kv + 4, 0:Mq], True, True, [Rkt, R(QT)], [bs])
                    pti = pt_ctr[0] % NPTB
                    pt_ctr[0] += 1
                    ptt = PT[pti]
                    rds = [bs] + ([Rb] if bias_ap is not None else [])
                    if bias_ap is not None:
                        act(ptt[0:nk, 0:4 * Mq], sview, AF.Exp, rds, [R(ptt)], bias=bias_ap[0:nk, :], scale=0.125)
                    else:
                        act(ptt[0:nk, 0:4 * Mq], sview, AF.Exp, rds, [R(ptt)], scale=0.125)
                    pts.append(ptt)
                allpts.append(pts)
            yield
            for kv in range(2):
                pts = allpts[kv]
                bo, to = bank()
                bsu, tsu = bank()
                for ci, (q0, q1, contrib) in enumerate(pv_plan):
                    n = len(contrib)
                    osl = slice(ci * 4 * qw, (ci + 1) * 4 * qw)
                    for ii, (bi, k0, k1) in enumerate(contrib):
                        vd_ap, Rvd = blocks[bi][2], blocks[bi][3]
                        p3 = pts[bi][:, 0:4 * Mq].rearrange("p (g q) -> p g q", g=4)
                        mm(to[:, osl], vd_ap[k0:k1, kv, :], p3[k0:k1, :, q0:q1], ii == 0, ii == n - 1, [Rvd, R(pts[bi])], [bo])
                    for ii, (bi, k0, k1) in enumerate(contrib):
                        p3 = pts[bi][:, 0:4 * Mq].rearrange("p (g q) -> p g q", g=4)
                        mm(tsu[:, osl], ones[k0:k1, :], p3[k0:k1, :, q0:q1], ii == 0, ii == n - 1, [Rones, R(pts[bi])], [bsu])
                nb_ = nrm if kv == 0 else sqflat
                Rnb_ = R(nrm) if kv == 0 else R(sq)
                s4 = tsu[:, 0:4 * Mq].rearrange("p (c g q) -> p c g q", c=ncq, g=4)
                n4 = nb_[:, 0:4 * Mq].rearrange("p (c g q) -> p c g q", c=ncq, g=4)
                for ci in range(ncq):
                    tt("dve", n4[:, ci], s4[:, ci], esink[:, 4 * kv:4 * kv + 4].unsqueeze(2).to_broadcast([128, 4, qw]), ALU.add, [bsu, R(esink)], [Rnb_])
                nfl = nb_[:, 0:4 * Mq]
                P.add("dve", lambda e, nfl=nfl: e.reciprocal(out=nfl, in_=nfl), [Rnb_], [Rnb_])
                o5 = to[:, 0:4 * Mq].rearrange("p (c j e q) -> p j e c q", c=ncq, j=2, e=2)
                n5 = nb_[:, 0:4 * Mq].rearrange("p (c j e q) -> p j e c q", c=ncq, j=2, e=2)
                for e_ in range(2):
                    ps_ = slice(64 * e_, 64 * e_ + 64)
                    atv = AT[ps_, 2 * kv:2 * kv + 2, col0:col0 + Mq].rearrange("p j (c q) -> p j c q", c=ncq)
                    for ci in range(ncq):
                        tt("dve", atv[:, :, ci, :], o5[ps_, :, e_, ci, :], n5[ps_, :, e_, ci, :], ALU.mult, [bo, Rnb_], [R(AT)])

        def zT_pair(N, colA, colB, RwA, RwB, hold=False):
            br, bt = bank(hold=hold)
            for half, (col, Rw) in enumerate(((colA, RwA), (colB, RwB))):
                for c in range(8):
                    mm(bt[:, half * ST:half * ST + N], w_in[:, c, col:col + 128], hT[:, c, 0:N], c == 0, c == 7, [Rw] + Rh(0, N), [br])
            return br, bt

        tg_ctr = [0]

        def conv_branch(N, nseq, T, tail_src, is_last_prompt, is_sample, js=(0, 1, 2, 3)):
            for j in js:
                for _ in conv_gen(N, nseq, T, tail_src, is_last_prompt, is_sample, j):
                    pass

        def conv_gen(N, nseq, T, tail_src, is_last_prompt, is_sample, j):
            if True:
                b1, t1 = zT_pair(N, C_C + j * 128, C_U + j * 128, WG["c"], WG["u"], hold=True)
                b2, t2 = zT_pair(N, C_B + j * 128, C_GC + j * 128, WG["b"], WG["gC"], hold=True)
                yield
                upv = up[:, 0:nseq * (T + 2)].rearrange("p (s t) -> p s t", s=nseq)
                if tail_src is None:
                    cp("pool", upv[:, 0, 0:2], tail[:, j, :], [R(tail)], [R(up)])
                else:
                    cp("pool", upv[:, :, 0:2], tail_src[:, j, :, :], [R(stail)], [R(up)])
                cp("act", csb[:, 0:N], t1[:, 0:N], [b1], [R(csb)])
                tgt = tgate[tg_ctr[0] % 2]
                tg_ctr[0] += 1
                act(tgt[:, 0:N], t2[:, ST:ST + N], AF.Tanh, [b2], [R(tgt)], scale=0.5)
                tt("dve", upv[:, :, 2:T + 2], csb[:, 0:N].rearrange("p (s t) -> p s t", s=nseq),
                   t1[:, ST:ST + N].rearrange("p (s t) -> p s t", s=nseq), ALU.mult, [R(csb), b1], [R(up)])
                stt("dve", tgt[:, 0:N], tgt[:, 0:N], 1.0, t2[:, ST:ST + N], ALU.add, ALU.mult, [R(tgt), b2], [R(tgt)])
                tt("dve", tgt[:, 0:N], tgt[:, 0:N], t2[:, 0:N], ALU.mult, [R(tgt), b2], [R(tgt)])
                if not is_sample:
                    cp("pool", tail[:, j, :], upv[:, 0, T:T + 2], [R(up)], [R(tail)])
                    if is_last_prompt:
                        cp("pool", coutp[:, j, :], upv[:, 0, T:T + 2], [R(up)], [R(coutp)])
                else:
                    cp("pool", couts[:, j, :, :], upv[:, :, T:T + 2], [R(up)], [R(couts)])
                a3 = acc[:, 0:N].rearrange("p (s t) -> p s t", s=nseq)
                act(a3, upv[:, :, 0:T], AF.Copy, [R(up), R(convw)], [R(acc)], scale=convw[:, j, 0:1])
                stt("dve", a3, upv[:, :, 1:T + 1], convw[:, j, 1:2], a3, ALU.mult, ALU.add, [R(up), R(convw), R(acc)], [R(acc)])
                stt("dve", a3, upv[:, :, 2:T + 2], convw[:, j, 2:3], a3, ALU.mult, ALU.add, [R(up), R(convw), R(acc)], [R(acc)])
                tt("pool", PC[:, j, 0:N], acc[:, 0:N], tgt[:, 0:N], ALU.mult, [R(acc), R(tgt)], [R(PC)])
                unhold(b1)
                unhold(b2)

        def conv_state_only(N):
            for j in range(4):
                b1, t1 = zT_pair(N, C_C + j * 128, C_U + j * 128, WG["c"], WG["u"])
                cp("act", csb[:, 0:2], t1[:, N - 2:N], [b1], [R(csb)])
                tt("dve", tail[:, j, :], csb[:, 0:2], t1[:, ST + N - 2:ST + N], ALU.mult, [R(csb), b1], [R(tail)])

        def gate_a(N):
            for jj in range(2):
                br, bt = zT_pair(N, C_GA + (2 * jj) * 128, C_GA + (2 * jj + 1) * 128, WG["gA"], WG["gA"])
                for half in range(2):
                    j = 2 * jj + half
                    tgt = tgate[tg_ctr[0] % 2]
                    tg_ctr[0] += 1
                    act(tgt[:, 0:N], bt[:, half * ST:half * ST + N], AF.Tanh, [br], [R(tgt)], scale=0.5)
                    stt("dve", tgt[:, 0:N], tgt[:, 0:N], 1.0, bt[:, half * ST:half * ST + N], ALU.add, ALU.mult, [R(tgt), br], [R(tgt)])
                    tt("pool", AT[:, j, 0:N], AT[:, j, 0:N], tgt[:, 0:N], ALU.mult, [R(AT), R(tgt)], [R(AT)])

        def mprime(N, k):
            bA, tAb = bank(hold=True)
            for c in range(8):
                mm(tAb[:, 0:N], w_in[:, c, C_MA + k * 128:C_MA + (k + 1) * 128], hT[:, c, 0:N], c == 0, c == 7, [WG["mA"]] + Rh(0, N), [bA])
            bB, tBb = bank(hold=True)
            for c in range(8):
                mm(tBb[:, 0:N], w_in[:, c, C_MB + k * 128:C_MB + (k + 1) * 128], hT[:, c, 0:N], c == 0, c == 7, [WG["mB"]] + Rh(0, N), [bB])
            return (bA, tAb, bB, tBb)

        def ymerge(N, k, bks):
            bA, tAb, bB, tBb = bks
            for j in range(4):
                mm(tBb[:, ST:ST + N], w_b[:, j, k * 128:(k + 1) * 128], PC[:, j, 0:N], j == 0, j == 3, [R_wb, R(PC)], [bB])
            for j in range(4):
                mm(tAb[:, ST:ST + N], w_a[:, j, k * 128:(k + 1) * 128], AT[:, j, 0:N], j == 0, j == 3, [R_wa, R(AT)], [bA])
            act(tB[:, 0:N], tBb[:, 0:N], AF.Tanh, [bB], [R(tB)], scale=0.5)
            act(tA[:, 0:N], tAb[:, 0:N], AF.Tanh, [bA], [R(tA)], scale=0.5)
            stt("dve", tB[:, 0:N], tB[:, 0:N], 1.0, tBb[:, ST:ST + N], ALU.add, ALU.mult, [R(tB), bB], [R(tB)])
            stt("dve", tA[:, 0:N], tA[:, 0:N], 1.0, tAb[:, ST:ST + N], ALU.add, ALU.mult, [R(tA), bA], [R(tA)])
            tt("pool", mT[:, k, 0:N], tA[:, 0:N], tB[:, 0:N], ALU.add, [R(tA), R(tB)], [RmT[k]])
            unhold(bA)
            unhold(bB)

        def merge(N):
            bks = mprime(N, 0)
            for k in range(8):
                nxt = mprime(N, k + 1) if k < 7 else None
                ymerge(N, k, bks)
                bks = nxt

        pf_ctr = [0]

        def r_phase(g, col0, yrow0):
            xt = xs[x_slot_of[g]]
            pb = pb2[pf_slot[g]]
            rbanks = []
            for hh in range(2):
                br, bt = bank()
                for k in range(8):
                    mm(bt[:, :], mT[:, k, col0:col0 + 128], w_o[:, k, hh * 512:(hh + 1) * 512], k == 0, k == 7, [RmT[k], R_wo], [br])
                rbanks.append((br, bt))
            for hh in range(2):
                br, bt = rbanks[hh]
                stt("dve", xt[:, hh * 512:(hh + 1) * 512], bt[:, :], 0.25, xt[:, hh * 512:(hh + 1) * 512], ALU.mult, ALU.add, [br, R(xt)], [R(xt)])
            yield
            cp("act", rb[:], xt[:], [R(xt)], [R(rb)])
            yield
            b1, t1 = bank()
            b16 = t1[:].bitcast(BF16)
            for c in range(8):
                tr(b16[:, c * 128:(c + 1) * 128], rb[:, c * 128:(c + 1) * 128], ident[:], [R(rb), Rid], [b1])
            cp("dve", rT[:].rearrange("p c t -> p (c t)"), b16[:, :], [b1], [R(rT)])
            b2, t2 = bank()
            b16b = t2[:].bitcast(BF16)
            for c in range(2):
                tr(b16b[:, c * 128:(c + 1) * 128], pb[:, c * 128:(c + 1) * 128], ident[:], [R(pb), Rid], [b2])
            act(pT[:].rearrange("p c t -> p (c t)"), b16b[:, 0:256], AF.Copy, [b2], [R(pT)], scale=0.5)
            yield
            for hh in range(2):
                bg, tgb = bank()
                for k in range(8):
                    mm(tgb[:, :], rT[:, k, :], w_pg[:, k, hh * 512:(hh + 1) * 512], k == 0, k == 7, [R(rT), R_wpg], [bg])
                bp, tpb = bank()
                for c in range(2):
                    mm(tpb[:, :], pT[:, c, :], w_pp[:, c, hh * 512:(hh + 1) * 512], c == 0, c == 1, [R(pT), R_wpp], [bp])
                sl = slice(hh * 512, (hh + 1) * 512)
                act(tg[:, :], tgb[:, :], AF.Tanh, [bg], [R(tg)], scale=0.5)
                stt("dve", tg[:, :], tg[:, :], 1.0, tpb[:, :], ALU.add, ALU.mult, [R(tg), bp], [R(tg)])
                tt("dve", xt[:, sl], xt[:, sl], tg[:, :], ALU.add, [R(tg), R(xt)], [R(xt)])
            dma("sp", y_d[yrow0:yrow0 + 128, :], xt[:], [R(xt)], [])

        def run(gen):
            for _ in gen:
                pass

        pf_slot = {}

        def load_p(g):
            i = pf_ctr[0] % 2
            pf_ctr[0] += 1
            pf_slot[g] = i
            dma("pool", pb2[i][:], ph[g * 128:(g + 1) * 128, :], [], [R(pb2[i])])

        run(norm_tile(0, 0))
        run(qkv_tile(0, 0, 128, 0, False))
        issue_w(7)
        conv_state_only(128)
        NB = TOK_PC // ST
        load_x(2)
        run(norm_tile(1, 0))
        run(norm_tile(2, 128))

        def start_qkv(blk):
            g0, g1 = 1 + 2 * blk, 2 + 2 * blk
            last = blk == NB - 1
            gens = []
            for i, g in enumerate((g0, g1)):
                kout = vout = None
                if last and i == 1:
                    kout = [(klast_d[:, :], 0, 128)]
                    vout = [(vlast_d[:, :], 0, 128)]
                qg_ = qkv_tile(g * 128, i * 128, 128, g % 3, True, kout, vout)
                next(qg_)
                gens.append(qg_)
            return gens

        def start_qkv_one(blk, i):
            g = 1 + 2 * blk + i
            last = blk == NB - 1
            kout = vout = None
            if last and i == 1:
                kout = [(klast_d[:, :], 0, 128)]
                vout = [(vlast_d[:, :], 0, 128)]
            qg_ = qkv_tile(g * 128, i * 128, 128, g % 3, True, kout, vout)
            next(qg_)
            return [qg_]

        qgen = start_qkv(0)
        issue_w(5)
        plan = [(0, 64, [(0, 0, 128), (1, 0, 64)]), (64, 128, [(0, 64, 128), (1, 0, 128)])]

        def mk_att(i, g):
            blocks = [
                (KT[(g - 1) % 3], R(KT[(g - 1) % 3]), vd[(g - 1) % 3], R(vd[(g - 1) % 3]), 128, (hbias if g == 1 else None), R(hbias)),
                (KT[g % 3], R(KT[g % 3]), vd[g % 3], R(vd[g % 3]), 128, None, None),
            ]
            return attend(128, blocks, plan, i * 128)

        gS = NPT + 1
        sgen = {}

        def sample_q1(b):
            slot = (gS + b) % 3
            g_ = qkv_tile(gS * 128 + b * 32, b * 32, 32, slot, True,
                          kout=[(ks_d[b, 96:128, :], 0, 32)], vout=[(vs_d[b, 96:128, :], 0, 32)])
            next(g_)
            return g_

        for blk in range(NB):
            g0, g1 = 1 + 2 * blk, 2 + 2 * blk
            last = blk == NB - 1
            ng0 = g0 + 2
            ng1 = g1 + 2 if not last else None
            load_p(g0)
            load_p(g1)
            load_x(ng0)
            if ng1 is not None:
                load_x(ng1)
            load_stail(2)
            issue_w(5)
            run(qgen[0])
            cg = conv_gen(ST, 1, ST, None, last, False, 0)
            next(cg)
            at0 = mk_att(0, g0)
            next(at0)
            run(cg)
            run(at0)
            run(qgen[1])
            cg = conv_gen(ST, 1, ST, None, last, False, 1)
            next(cg)
            at1 = mk_att(1, g1)
            next(at1)
            run(cg)
            run(at1)
            n0 = norm_tile(ng0, 0)
            next(n0)
            conv_branch(ST, 1, ST, None, last, False, js=(2,))
            bks = mprime(ST, 0)
            n1 = None
            if ng1 is not None:
                n1 = norm_tile(ng1, 128)
                next(n1)
            conv_branch(ST, 1, ST, None, last, False, js=(3,))
            gate_a(ST)
            for k in range(8):
                nxt = mprime(ST, k + 1) if k < 7 else None
                if k == 7:
                    next(n0)
                ymerge(ST, k, bks)
                bks = nxt
            ra = r_phase(g0, 0, (g0 - 1) * 128)
            rb_ = r_phase(g1, 128, (g1 - 1) * 128)
            next(ra)
            run(n0)
            if n1 is not None:
                next(n1)
            next(rb_)
            if n1 is not None:
                run(n1)
            next(ra)
            next(ra)
            next(rb_)
            if not last:
                qgen = start_qkv_one(blk + 1, 0)
            else:
                sgen[0] = sample_q1(0)
            run(ra)
            next(rb_)
            if not last:
                qgen.append(start_qkv_one(blk + 1, 1)[0])
            else:
                sgen[1] = sample_q1(1)
            run(rb_)
        gS = NPT + 1
        load_stail(16)
        load_p(gS)
        for b in range(4):
            dma("sp", ckf[:], ck_d[b], [], [R(ckf)])
            dma("sp", cvf[:], cv_d[b], [], [R(cvf)])
            dma("sp", ks_d[b, 0:96, :], ck_d[b, 32:128, :], [], [])
            dma("sp", vs_d[b, 0:96, :], cv_d[b, 32:128, :], [], [])
            cp("pool", ckb[:], ckf[:], [R(ckf)], [R(ckb)])
            for rr in range(2):
                cp("act", vdc[:, :, rr * 64:(rr + 1) * 64], cvf[:].rearrange("p (h d) -> p h d", h=2), [R(cvf)], [R(vdc)])
            bt_, tt_ = bank()
            b16 = tt_[:].bitcast(BF16)
            for kv in range(2):
                tr(b16[0:64, kv * 128:(kv + 1) * 128], ckb[:, kv * 64:(kv + 1) * 64], ident[:], [R(ckb), Rid], [bt_])
            cp("act", KTc[:].rearrange("p k t -> p (k t)"), b16[0:64, 0:256], [bt_], [R(KTc)])
            slot = (gS + b) % 3
            run(sgen[b])
            blocks = [
                (KTc, R(KTc), vdc, R(vdc), 128, None, None),
                (KT[slot], R(KT[slot]), vd[slot], R(vd[slot]), 32, None, None),
            ]
            plan = [(0, 32, [(0, 0, 128), (1, 0, 32)])]
            cg = conv_gen(128, 4, 32, stail, False, True, b)
            next(cg)
            at_ = attend(32, blocks, plan, b * 32)
            next(at_)
            run(cg)
            if b + 2 < 4:
                sgen[b + 2] = sample_q1(b + 2)
            run(at_)
        gate_a(128)
        merge(128)
        run(r_phase(gS, 0, TOK_PC))
        for j in range(4):
            dma("sp", clast_d[:, j * 128:(j + 1) * 128].rearrange("t p -> p t"), coutp[:, j, :], [R(coutp)], [], slow=True)
            for b in range(4):
                dma("sp", cs_d[b, :, j * 128:(j + 1) * 128].rearrange("t p -> p t"), couts[:, j, b, :], [R(couts)], [], slow=True)

        sems = {}
        for nme in P.sem_names:
            sems[nme] = es.enter_context(nc.semaphore(nme))
        esem = {}
        sigcount = {}
        for e in ENGS:
            esem[e] = es.enter_context(nc.semaphore("eng_" + e))
            c = 0
            for op in P.ops[e]:
                if op.kind == "op" and op.sig:
                    c += 1
                    op.val = c
            sigcount[e] = c
        block = es.enter_context(nc.Block())

        def emit(ename, eng):
            inline_ok = ename in ("dve", "pool", "act", "pe")
            for op in P.ops[ename]:
                ws = list(op.waits)
                last = None
                if inline_ok and ws and op.kind == "op" and id(op) not in MULTI:
                    last = ws.pop()
                for w in ws:
                    if w[0] == "sem":
                        eng.wait_ge(sems[w[1]], w[2])
                    else:
                        X = w[1]
                        eng.wait_ge(esem[X.eng], X.val)
                ins = op.fn(eng)
                if last is not None:
                    if last[0] == "sem":
                        ins._wait_ge(sems[last[1]], last[2])
                    else:
                        ins._wait_ge(esem[last[1].eng], last[1].val)
                if op.kind == "dmaop":
                    ins.then_inc(sems[op.inc], 16)
                elif op.sig:
                    ins.then_inc(esem[ename], 1)
            if ename == "sp":
                for Rr in P.store_res:
                    eng.wait_ge(sems[Rr.rsem], Rr.rcount)
                if P.final_count:
                    eng.wait_ge(sems[P.final_sem], P.final_count)

        @block.sync
        def _(eng):
            emit("sp", eng)

        @block.scalar
        def _(eng):
            emit("act", eng)

        @block.vector
        def _(eng):
            emit("dve", eng)

        @block.gpsimd
        def _(eng):
            emit("pool", eng)

        @block.tensor
        def _(eng):
            emit("pe", eng)

    return nc


_NC_CACHE = {}


def kernel(x_prompt, x_sample, p_prompt, p_sample, cache_k, cache_v, state_conv,
           ln_g, w_in, q_norm_g, k_norm_g, sink, conv_w, w_attn_out, w_conv_out, w_o,
           w_ple_gate, w_ple_proj):
    f = np.float32
    xp = np.asarray(x_prompt, f).reshape(SEQ, D)
    xsm = np.asarray(x_sample, f).reshape(32 * 32, D)
    pp = np.asarray(p_prompt, f).reshape(SEQ, 256)
    psm = np.asarray(p_sample, f).reshape(32 * 32, 256)
    ck = np.asarray(cache_k, f).reshape(32, 128, 128)
    cv = np.asarray(cache_v, f).reshape(32, 128, 128)
    sc = np.asarray(state_conv, f).reshape(32, 2, 512)
    def rope_rows(pos):
        try:
            import jax
            import jax.numpy as jnp
            with jax.default_device(jax.devices("cpu")[0]):
                inv_freq = 10000.0 ** (-jnp.arange(0, 32, dtype=jnp.float32) * 2.0 / 64)
                ang = jnp.asarray(pos.astype(np.float32))[:, None] * inv_freq[None, :]
                c = np.asarray(jnp.cos(ang), dtype=f)
                s = np.asarray(jnp.sin(ang), dtype=f)
        except Exception:
            inv_freq = 10000.0 ** (-np.arange(0, 32, dtype=np.float64) * 2.0 / 64)
            ang = pos.astype(np.float64)[:, None] * inv_freq[None, :]
            c, s = np.cos(ang).astype(f), np.sin(ang).astype(f)
        return np.concatenate([c, c, -s, s], axis=1).astype(f)

    if "nc" not in _NC_CACHE:
        _NC_CACHE["nc"] = build_program()
    nc = _NC_CACHE["nc"]
    shared = {
        "ln_g": np.asarray(ln_g, f).reshape(D), "w_in": np.asarray(w_in, f).reshape(D, IN_DIM),
        "qg": np.asarray(q_norm_g, f).reshape(64), "kg": np.asarray(k_norm_g, f).reshape(64),
        "sink": np.asarray(sink, f).reshape(8), "conv_w": np.asarray(conv_w, f).reshape(3, 512),
        "w_a": np.asarray(w_attn_out, f).reshape(512, D), "w_b": np.asarray(w_conv_out, f).reshape(512, D),
        "w_o": np.asarray(w_o, f).reshape(D, D), "w_pg": np.asarray(w_ple_gate, f).reshape(D, D),
        "w_pp": np.asarray(w_ple_proj, f).reshape(256, D),
    }
    in_maps = []
    for c in range(NCORES):
        t0 = c * TOK_PC
        xhc = np.zeros((NROWS, D), f)
        phc = np.zeros((NROWS, 256), f)
        pos = np.zeros((NROWS,), np.float64)
        if c > 0:
            xhc[0:128] = xp[t0 - 128:t0]
        pos[0:128] = np.arange(t0 - 128, t0)
        xhc[128:128 + TOK_PC] = xp[t0:t0 + TOK_PC]
        phc[128:128 + TOK_PC] = pp[t0:t0 + TOK_PC]
        pos[128:128 + TOK_PC] = np.arange(t0, t0 + TOK_PC)
        xhc[128 + TOK_PC:] = xsm[c * 128:(c + 1) * 128]
        phc[128 + TOK_PC:] = psm[c * 128:(c + 1) * 128]
        pos[128 + TOK_PC:] = np.tile(1024 + np.arange(32), 4)
        hb = np.full((128, 1), -30000.0 if c == 0 else 0.0, f)
        m = dict(shared)
        m.update({"xh": xhc, "ph": phc, "rope": rope_rows(pos),
                  "ck": np.ascontiguousarray(ck[c * 4:(c + 1) * 4]), "cv": np.ascontiguousarray(cv[c * 4:(c + 1) * 4]),
                  "sconv": np.ascontiguousarray(sc[c * 4:(c + 1) * 4]), "hbias": hb})
        in_maps.append(m)
    out = run_bass_kernel_spmd(nc, in_maps, core_ids=list(range(NCORES)))
    rs = out.results
    y_prompt = np.concatenate([r["y"][0:TOK_PC] for r in rs], axis=0).reshape(1, SEQ, D)
    y_sample = np.concatenate([r["y"][TOK_PC:] for r in rs], axis=0).reshape(32, 32, D)
    k_prompt = rs[-1]["k_last"].reshape(1, 1, 128, 2, 64)
    v_prompt = rs[-1]["v_last"].reshape(1, 1, 128, 2, 64)
    conv_prompt = rs[-1]["c_last"].reshape(1, 1, 2, 512)
    k_sample = np.concatenate([r["ks"] for r in rs], axis=0).reshape(1, 32, 128, 2, 64)
    v_sample = np.concatenate([r["vs"] for r in rs], axis=0).reshape(1, 32, 128, 2, 64)
    conv_sample = np.concatenate([r["cs"] for r in rs], axis=0).reshape(1, 32, 2, 512)
    return (y_prompt.astype(f), y_sample.astype(f), k_prompt.astype(f), v_prompt.astype(f),
            conv_prompt.astype(f), k_sample.astype(f), v_sample.astype(f), conv_sample.astype(f))
```

```python
import contextlib
import numpy as np
import concourse.bass as bass
import concourse.mybir as mybir
from concourse.bass_utils import run_bass_kernel_spmd

F32 = mybir.dt.float32
BF16 = mybir.dt.bfloat16
AF = mybir.ActivationFunctionType
ALU = mybir.AluOpType
AX = mybir.AxisListType

NCORES = 8
D = 1024
SEQ = 16384
TOK_PC = SEQ // NCORES
NPT = TOK_PC // 128
NROWS = 128 + TOK_PC + 128
ST = 256
IN_DIM = 5376
EPS = 1e-6
C_Q, C_K, C_V, C_GA, C_B, C_C, C_U, C_GC, C_MA, C_MB = 0, 512, 640, 768, 1280, 1792, 2304, 2816, 3328, 4352

ENGS = ("sp", "act", "dve", "pool", "pe")
STRICT_SAME_ENGINE = True
STRICT_ENGINES = ("pool", "dve")


class Res:
    def __init__(self, name):
        self.name = name
        self.w = None
        self.readers = {}
        self.rd_dma = None
        self.dsem = None
        self.dcount = 0
        self.rsem = None
        self.rcount = 0


class Op:
    __slots__ = ("kind", "eng", "idx", "fn", "waits", "sig", "inc", "sem", "val", "know")

    def __init__(self, kind, eng, idx, fn):
        self.kind = kind
        self.eng = eng
        self.idx = idx
        self.fn = fn
        self.waits = []
        self.sig = False
        self.inc = None
        self.sem = None
        self.val = 0
        self.know = {}


class Mark:
    kind = "dma"

    def __init__(self, sem, val, know=None):
        self.sem = sem
        self.val = val
        self.know = know or {}


class Planner:
    def __init__(self):
        self.ops = {e: [] for e in ENGS}
        self.seen = {e: {} for e in ENGS}
        self.seen_sem = {e: {} for e in ENGS}
        self.sem_names = []
        self.final_sem = self.new_sem("final")
        self.final_count = 0
        self.store_res = []

    def new_sem(self, name):
        n = "s%d_%s" % (len(self.sem_names), name)
        self.sem_names.append(n)
        return n

    def _dep(self, eng, waits, X, raw):
        if X is None:
            return
        if X.kind == "dma":
            if self.seen_sem[eng].get(X.sem, 0) >= X.val:
                return
            self.seen_sem[eng][X.sem] = X.val
            waits.append(("sem", X.sem, X.val))
            self._learn(eng, X.know)
            return
        if X.eng == eng and (eng == "pe" or not (raw or (STRICT_SAME_ENGINE and eng in STRICT_ENGINES))):
            return
        if self.seen[eng].get(X.eng, -1) >= X.idx:
            return
        self.seen[eng][X.eng] = X.idx
        X.sig = True
        waits.append(("op", X))
        self._learn(eng, X.know)

    def _learn(self, eng, know):
        se = self.seen[eng]
        for g, i in know.items():
            if se.get(g, -1) < i:
                se[g] = i

    def _deps(self, eng, reads, writes):
        waits = []
        for r in reads:
            self._dep(eng, waits, r.w, True)
        for w in writes:
            self._dep(eng, waits, w.w, False)
            for o in w.readers.values():
                self._dep(eng, waits, o, False)
            if w.rd_dma is not None:
                self._dep(eng, waits, w.rd_dma, False)
                w.rd_dma = None
        return waits

    def add(self, eng, fn, reads=(), writes=()):
        op = Op("op", eng, len(self.ops[eng]), fn)
        op.waits = self._deps(eng, reads, writes)
        self.ops[eng].append(op)
        op.know = dict(self.seen[eng])
        op.know[eng] = op.idx
        for r in reads:
            r.readers[eng] = op
        for w in writes:
            w.w = op
            w.readers = {}
        return op

    def dma(self, eng, fn, reads=(), writes=()):
        op = Op("dmaop", eng, len(self.ops[eng]), fn)
        op.waits = self._deps(eng, reads, writes)
        self.ops[eng].append(op)
        if writes:
            W = writes[0]
            if W.dsem is None:
                W.dsem = self.new_sem("d_" + W.name)
            W.dcount += 16
            op.inc = W.dsem
            W.w = Mark(W.dsem, W.dcount, dict(self.seen[eng]))
            W.readers = {}
        elif reads:
            R = reads[0]
            if R.rsem is None:
                R.rsem = self.new_sem("r_" + R.name)
                self.store_res.append(R)
            R.rcount += 16
            op.inc = R.rsem
            R.rd_dma = Mark(R.rsem, R.rcount, dict(self.seen[eng]))
        else:
            self.final_count += 16
            op.inc = self.final_sem
        return op


def build_program():
    nc = bass.Bass("TRN2", target_bir_lowering=False)
    P = Planner()

    def din(name, shape):
        return nc.dram_tensor(name, list(shape), F32, kind="ExternalInput").ap()

    def dout(name, shape):
        return nc.dram_tensor(name, list(shape), F32, kind="ExternalOutput").ap()

    xh = din("xh", [NROWS, D])
    ph = din("ph", [NROWS, 256])
    rope = din("rope", [NROWS, 128])
    ck_d = din("ck", [4, 128, 128])
    cv_d = din("cv", [4, 128, 128])
    sconv_d = din("sconv", [4, 2, 512])
    hbias_d = din("hbias", [128, 1])
    lng_d = din("ln_g", [D])
    win_d = din("w_in", [D, IN_DIM])
    qg_d = din("qg", [64])
    kg_d = din("kg", [64])
    sink_d = din("sink", [8])
    convw_d = din("conv_w", [3, 512])
    wa_d = din("w_a", [512, D])
    wb_d = din("w_b", [512, D])
    wo_d = din("w_o", [D, D])
    wpg_d = din("w_pg", [D, D])
    wpp_d = din("w_pp", [256, D])

    y_d = dout("y", [TOK_PC + 128, D])
    klast_d = dout("k_last", [128, 128])
    vlast_d = dout("v_last", [128, 128])
    clast_d = dout("c_last", [2, 512])
    ks_d = dout("ks", [4, 128, 128])
    vs_d = dout("vs", [4, 128, 128])
    cs_d = dout("cs", [4, 2, 512])

    es = contextlib.ExitStack()
    with es:
        res = {}

        def sb(name, shape, dt):
            t = es.enter_context(nc.sbuf_tensor("t_" + name, list(shape), dt))
            res["t_" + name] = Res(name)
            return t

        w_in = sb("w_in_sb", [128, 8, IN_DIM], BF16)
        w_a = sb("w_a_sb", [128, 4, D], BF16)
        w_b = sb("w_b_sb", [128, 4, D], BF16)
        w_o = sb("w_o_sb", [128, 8, D], BF16)
        w_pg = sb("w_pg_sb", [128, 8, D], BF16)
        w_pp = sb("w_pp_sb", [128, 2, D], BF16)
        WG = {}
        for nm in ("qkv", "gA", "b", "c", "u", "gC", "mA", "mB"):
            WG[nm] = Res("win_" + nm)
        R_wa, R_wb, R_wo, R_wpg, R_wpp = res["t_w_a_sb"], res["t_w_b_sb"], res["t_w_o_sb"], res["t_w_pg_sb"], res["t_w_pp_sb"]

        ident = sb("ident", [128, 128], BF16)
        identf = sb("identf", [128, 128], F32)
        ones = sb("ones", [128, 128], BF16)
        g_ln = sb("g_ln", [128, 8], F32)
        gq = sb("gq", [128, 64], F32)
        rgq = sb("rgq", [128, 64], F32)
        gk = sb("gk", [128, 64], F32)
        rgk = sb("rgk", [128, 64], F32)
        esink = sb("esink", [128, 8], F32)
        convw = sb("convw", [128, 4, 3], F32)
        hbias = sb("hbias", [128, 1], F32)
        stail = sb("stail", [128, 4, 4, 2], F32)

        NXS = 4
        xs = [sb("xs%d" % i, [128, D], F32) for i in range(NXS)]
        xn = sb("xn", [128, D], BF16)
        sx = [sb("sx%d" % i, [128, 4], F32) for i in range(2)]
        hT = sb("hT", [128, 8, ST], BF16)
        csl = [sb("cs%d" % i, [128, 128], F32) for i in range(2)]
        TCq = sb("TCq", [128, 64], F32)
        TSq = sb("TSq", [128, 64], F32)
        TCk = sb("TCk", [128, 64], F32)
        TSk = sb("TSk", [128, 64], F32)
        sq = sb("sq", [128, 10, 64], F32)
        sqflat = sq[:].rearrange("p h d -> p (h d)")
        qr = sb("qr", [128, 10, 64], F32)
        ssq = sb("ssq", [128, 10], F32)
        nwv = sb("nwv", [128, 10], F32)
        nwt = sb("nwt", [128, 10], F32)
        nwy = sb("nwy", [128, 10], F32)
        qro2 = [sb("qro%d" % i, [128, 8, 64], BF16) for i in range(2)]
        kf = [sb("kf%d" % i, [128, 2, 64], F32) for i in range(2)]
        kb2 = [sb("kb%d" % i, [128, 2, 64], BF16) for i in range(2)]
        vf = [sb("vf0", [128, 128], F32)] * 2
        vd = [sb("vd%d" % i, [128, 2, 128], BF16) for i in range(3)]
        QT = sb("QT", [64, 8, 128], BF16)
        KT = [sb("KT%d" % i, [64, 2, 128], BF16) for i in range(3)]
        ckf = sb("ckf", [128, 128], F32)
        ckb = sb("ckb", [128, 128], BF16)
        cvf = sb("cvf", [128, 128], F32)
        KTc = sb("KTc", [64, 2, 128], BF16)
        vdc = sb("vdc", [128, 2, 128], BF16)
        NPTB = 4
        PT = [sb("PT%d" % i, [128, 512], BF16) for i in range(NPTB)]
        nrm = sb("nrm", [128, 512], F32)
        AT = sb("AT", [128, 4, ST], BF16)
        tgate = [sb("tgate%d" % i, [128, ST], F32) for i in range(2)]
        csb = sb("csb", [128, ST], F32)
        up = sb("up", [128, ST + 8], F32)
        acc = sb("acc", [128, ST], F32)
        PC = sb("PC", [128, 4, ST], BF16)
        tail = sb("tail", [128, 4, 2], F32)
        couts = sb("couts", [128, 4, 4, 2], F32)
        coutp = sb("coutp", [128, 4, 2], F32)
        tA = sb("tA", [128, ST], F32)
        tB = sb("tB", [128, ST], F32)
        mT = sb("mT", [128, 8, ST], BF16)
        rb = sb("rb", [128, D], BF16)
        rT = sb("rT", [128, 8, 128], BF16)
        tg = sb("tg", [128, 512], F32)
        pb2 = [sb("pb%d" % i, [128, 256], BF16) for i in range(2)]
        pT = sb("pT", [128, 2, 128], BF16)

        def R(t):
            return res[t.name] if hasattr(t, "name") else t

        RhA, RhB = Res("hT_a"), Res("hT_b")
        RmT = [Res("mT%d" % k) for k in range(8)]

        def Rh(c0, n):
            out = []
            if c0 < 128:
                out.append(RhA)
            if c0 + n > 128:
                out.append(RhB)
            return out

        banks = []
        for i in range(8):
            t = es.enter_context(nc.psum_tensor("ps%d" % i, [128, 512], F32))
            banks.append((Res("ps%d" % i), t))
        bank_ctr = [0]

        held = set()

        def bank(hold=False):
            while True:
                i = bank_ctr[0] % 8
                bank_ctr[0] += 1
                if i not in held:
                    break
            if hold:
                held.add(i)
            r, t = banks[i]
            return r, t

        def unhold(r):
            for i, (rr, _) in enumerate(banks):
                if rr is r:
                    held.discard(i)

        MULTI = set()
        def mm(out, lhsT, rhs, start, stop, reads, writes):
            P.add("pe", lambda e: e.matmul(out, lhsT=lhsT, rhs=rhs, start=start, stop=stop), reads, writes)

        def tr(out, in_, idn, reads, writes):
            P.add("pe", lambda e: e.transpose(out=out, in_=in_, identity=idn), reads, writes)

        def act(out, in_, func, reads, writes, bias=None, scale=None, accum_out=None):
            kw = {}
            if bias is not None:
                kw["bias"] = bias
            if scale is not None:
                kw["scale"] = scale
            if accum_out is not None:
                kw["accum_out"] = accum_out
            op_ = P.add("act", lambda e: e.activation(out=out, in_=in_, func=func, **kw), reads, writes)
            if accum_out is not None:
                MULTI.add(id(op_))

        def tt(eng, out, in0, in1, op, reads, writes):
            P.add(eng, lambda e: e.tensor_tensor(out=out, in0=in0, in1=in1, op=op), reads, writes)

        def tsc(eng, out, in0, s1, s2, op0, op1, reads, writes):
            if op1 is None:
                P.add(eng, lambda e: e.tensor_scalar(out=out, in0=in0, scalar1=s1, scalar2=None, op0=op0), reads, writes)
            else:
                P.add(eng, lambda e: e.tensor_scalar(out=out, in0=in0, scalar1=s1, scalar2=s2, op0=op0, op1=op1), reads, writes)

        def stt(eng, out, in0, scalar, in1, op0, op1, reads, writes):
            P.add(eng, lambda e: e.scalar_tensor_tensor(out=out, in0=in0, scalar=scalar, in1=in1, op0=op0, op1=op1), reads, writes)

        def cp(eng, out, in_, reads, writes):
            if eng == "act":
                act(out, in_, AF.Copy, reads, writes)
            else:
                P.add(eng, lambda e: e.tensor_copy(out=out, in_=in_), reads, writes)

        def dma(eng, out, in_, reads, writes, slow=False):
            if slow:
                P.dma(eng, lambda e: e.dma_start(out=out, in_=in_, allow_slow_non_contiguous=True), reads, writes)
            else:
                P.dma(eng, lambda e: e.dma_start(out=out, in_=in_), reads, writes)

        Rid, Ridf, Rones = R(ident), R(identf), R(ones)
        P.add("pool", lambda e: e.memset(identf[:], 1.0), [], [Ridf])
        P.add("pool", lambda e: e.affine_select(out=identf[:], in_=identf[:], pattern=[[-1, 128]],
                                                compare_op=ALU.is_equal, fill=0.0, base=0, channel_multiplier=1),
              [Ridf], [Ridf])
        cp("pool", ident[:], identf[:], [Ridf], [Rid])
        P.add("pool", lambda e: e.memset(ones[:], 1.0), [], [Rones])
        P.add("pool", lambda e: e.memset(tail[:], 0.0), [], [R(tail)])

        dma("sp", xs[0][:], xh[0:128, :], [], [R(xs[0])])
        dma("sp", g_ln[:], lng_d.rearrange("(c p) -> p c", p=128), [], [R(g_ln)], slow=True)
        dma("sp", gq[:], qg_d.partition_broadcast(128), [], [R(gq)])
        dma("sp", gk[:], kg_d.partition_broadcast(128), [], [R(gk)])
        dma("sp", rgq[:, 0:32], qg_d[32:64].partition_broadcast(128), [], [R(rgq)])
        dma("sp", rgq[:, 32:64], qg_d[0:32].partition_broadcast(128), [], [R(rgq)])
        dma("sp", rgk[:, 0:32], kg_d[32:64].partition_broadcast(128), [], [R(rgk)])
        dma("sp", rgk[:, 32:64], kg_d[0:32].partition_broadcast(128), [], [R(rgk)])
        dma("sp", esink[:], sink_d.partition_broadcast(128), [], [R(esink)])
        dma("sp", hbias[:], hbias_d, [], [R(hbias)])
        for j in range(4):
            dma("act", convw[:, j, :], convw_d[:, j * 128:(j + 1) * 128].rearrange("t p -> p t"), [], [R(convw)], slow=True)
        act(esink[:], esink[:], AF.Exp, [R(esink)], [R(esink)])

        xs_ctr = [0]
        x_slot_of = {}

        def load_x(g):
            i = xs_ctr[0] % NXS
            xs_ctr[0] += 1
            x_slot_of[g] = i
            if g == 0:
                return
            dma("sp", xs[i][:], xh[g * 128:(g + 1) * 128, :], [], [R(xs[i])])

        def load_win(nm, c0, c1):
            step = 512
            for a in range(c0, c1, step):
                b = min(c1, a + step)
                dma("pool", w_in[:, :, a:b], win_d[:, a:b].rearrange("(c p) n -> p c n", p=128), [], [WG[nm]])

        load_x(0)
        load_x(1)
        wq = []
        wq.append(lambda: load_win("qkv", 0, 768))
        wq.append(lambda: load_win("c", C_C, C_C + 512))
        wq.append(lambda: load_win("u", C_U, C_U + 512))
        wq.append(lambda: load_win("b", C_B, C_B + 512))
        wq.append(lambda: load_win("gC", C_GC, C_GC + 512))
        wq.append(lambda: load_win("gA", C_GA, C_GA + 512))
        wq.append(lambda: dma("pool", w_b[:], wb_d.rearrange("(c p) n -> p c n", p=128), [], [R_wb]))
        wq.append(lambda: dma("pool", w_a[:], wa_d.rearrange("(c p) n -> p c n", p=128), [], [R_wa]))
        wq.append(lambda: load_win("mA", C_MA, C_MA + 1024))
        wq.append(lambda: load_win("mB", C_MB, C_MB + 1024))
        for hh in range(2):
            wq.append(lambda hh=hh: dma("pool", w_o[:, :, hh * 512:(hh + 1) * 512], wo_d[:, hh * 512:(hh + 1) * 512].rearrange("(c p) n -> p c n", p=128), [], [R_wo]))
        for hh in range(2):
            wq.append(lambda hh=hh: dma("pool", w_pg[:, :, hh * 512:(hh + 1) * 512], wpg_d[:, hh * 512:(hh + 1) * 512].rearrange("(c p) n -> p c n", p=128), [], [R_wpg]))
        wq.append(lambda: dma("pool", w_pp[:], wpp_d.rearrange("(c p) n -> p c n", p=128), [], [R_wpp]))

        def issue_w(n):
            for _ in range(n):
                if wq:
                    wq.pop(0)()

        issue_w(3)
        stail_jobs = [(b, j) for b in range(4) for j in range(4)]

        def load_stail(n):
            for _ in range(n):
                if stail_jobs:
                    b, j = stail_jobs.pop(0)
                    dma("sp", stail[:, j, b, :], sconv_d[b, :, j * 128:(j + 1) * 128].rearrange("t p -> p t"), [], [R(stail)], slow=True)

        out_y_res = [None]
        src_res = [None]

        norm_ctr = [0]

        def norm_tile(g, col0):
            i = x_slot_of[g]
            xt = xs[i]
            s = sx[norm_ctr[0] % 2]
            norm_ctr[0] += 1
            P.add("pool", lambda e: e.memset(s[:], 0.0), [], [R(s)])
            act(xn[:], xt[:], AF.Square, [R(xt), R(s)], [R(xn), R(s)], accum_out=s[:, 0:1])
            src_res[0] = R(s)
            out_y_res[0] = R(s)
            newton_rsqrt_ap(s[:, 0:1], s[:, 2:3], s[:, 3:4], s[:, 1:2], 1.0 / D, iters=2, Rv=R(s), Rt=R(s))
            yield
            act(xn[:], xt[:], AF.Copy, [R(xt), R(s)], [R(xn)], scale=s[:, 1:2])
            yield
            br, bt = bank()
            b16 = bt[:].bitcast(BF16)
            for c in range(8):
                tr(b16[:, c * 128:(c + 1) * 128], xn[:, c * 128:(c + 1) * 128], ident[:], [R(xn), Rid], [br])
            tt("dve", hT[:, :, col0:col0 + 128], b16.rearrange("p (c t) -> p c t", c=8),
               g_ln[:].unsqueeze(2).to_broadcast([128, 8, 128]), ALU.mult, [br, R(g_ln)], Rh(col0, 128))

        cs_ctr = [0]
        kf_ctr = [0]

        def qkv_tile(row0, col0, M, slot, has_q, kout=None, vout=None):
            ci = cs_ctr[0] % 2
            cs_ctr[0] += 1
            cst = csl[ci]
            dma("sp", cst[0:M, :], rope[row0:row0 + M, :], [], [R(cst)])
            Rw = WG["qkv"]
            if has_q:
                bq, tq = bank()
                for c in range(8):
                    mm(tq[0:M, :], hT[:, c, col0:col0 + M], w_in[:, c, 0:512], c == 0, c == 7, Rh(col0, M) + [Rw], [bq])
            bkv, tkv = bank()
            for c in range(8):
                mm(tkv[0:M, 0:256], hT[:, c, col0:col0 + M], w_in[:, c, 512:768], c == 0, c == 7, Rh(col0, M) + [Rw], [bkv])
            h0 = 0 if has_q else 8
            if has_q:
                tt("pool", TCq[0:M, :], cst[0:M, 0:64], gq[0:M, :], ALU.mult, [R(cst), R(gq)], [R(TCq)])
                tt("pool", TSq[0:M, :], cst[0:M, 64:128], rgq[0:M, :], ALU.mult, [R(cst), R(rgq)], [R(TSq)])
            tt("pool", TCk[0:M, :], cst[0:M, 0:64], gk[0:M, :], ALU.mult, [R(cst), R(gk)], [R(TCk)])
            tt("pool", TSk[0:M, :], cst[0:M, 64:128], rgk[0:M, :], ALU.mult, [R(cst), R(rgk)], [R(TSk)])
            if has_q:
                act(nrm[0:M, :], tq[0:M, :], AF.Square, [bq], [R(nrm)])
                P.add("dve", lambda e: e.tensor_reduce(out=ssq[0:M, 0:8], in_=nrm[0:M, :].rearrange("p (h d) -> p h d", h=8), axis=AX.X, op=ALU.add),
                      [R(nrm)], [R(ssq)])
            act(tA[0:M, 0:128], tkv[0:M, 0:128], AF.Square, [bkv], [R(tA)])
            P.add("dve", lambda e: e.tensor_reduce(out=ssq[0:M, 8:10], in_=tA[0:M, 0:128].rearrange("p (h d) -> p h d", h=2), axis=AX.X, op=ALU.add),
                  [R(tA)], [R(ssq)])
            src_res[0] = R(ssq)
            out_y_res[0] = R(nwy)
            n = 10 - h0
            v, t, yv = nwv[0:M, 0:n], nwt[0:M, 0:n], nwy[0:M, 0:n]
            newton_rsqrt_ap(ssq[0:M, h0:10], v, t, yv, 1.0 / 64)
            if has_q:
                tt("dve", qr[0:M, 0:8, :], tq[0:M, :].rearrange("p (h d) -> p h d", h=8),
                   nwy[0:M, 0:8].unsqueeze(2).to_broadcast([M, 8, 64]), ALU.mult, [bq, R(nwy)], [R(qr)])
            tt("dve", qr[0:M, 8:10, :], tkv[0:M, 0:128].rearrange("p (h d) -> p h d", h=2),
               nwy[0:M, 8 - h0:10 - h0].unsqueeze(2).to_broadcast([M, 2, 64]), ALU.mult, [bkv, R(nwy)], [R(qr)])
            kfi = kf_ctr[0] % 2
            kf_ctr[0] += 1
            vft, kft = vf[kfi], kf[kfi]
            qro, kb = qro2[kfi], kb2[kfi]
            vdt = vd[slot]
            for rr in range(2):
                act(vdt[0:M, :, rr * 64:(rr + 1) * 64], tkv[0:M, 128:256].rearrange("p (h d) -> p h d", h=2), AF.Copy, [bkv], [R(vdt)])
            if vout is not None:
                act(vft[0:M, :], tkv[0:M, 128:256], AF.Copy, [bkv], [R(vft)])
                for (dst, r0, r1) in vout:
                    dma("sp", dst, vft[r0:r1, :], [R(vft)], [])
            if has_q:
                tt("pool", sq[0:M, 0:8, 0:32], qr[0:M, 0:8, 32:64], TSq[0:M, 0:32].unsqueeze(1).to_broadcast([M, 8, 32]), ALU.mult, [R(qr), R(TSq)], [R(sq)])
                tt("pool", sq[0:M, 0:8, 32:64], qr[0:M, 0:8, 0:32], TSq[0:M, 32:64].unsqueeze(1).to_broadcast([M, 8, 32]), ALU.mult, [R(qr), R(TSq)], [R(sq)])
            tt("pool", sq[0:M, 8:10, 0:32], qr[0:M, 8:10, 32:64], TSk[0:M, 0:32].unsqueeze(1).to_broadcast([M, 2, 32]), ALU.mult, [R(qr), R(TSk)], [R(sq)])
            tt("pool", sq[0:M, 8:10, 32:64], qr[0:M, 8:10, 0:32], TSk[0:M, 32:64].unsqueeze(1).to_broadcast([M, 2, 32]), ALU.mult, [R(qr), R(TSk)], [R(sq)])
            if has_q:
                tt("pool", qr[0:M, 0:8, :], qr[0:M, 0:8, :], TCq[0:M, :].unsqueeze(1).to_broadcast([M, 8, 64]), ALU.mult, [R(qr), R(TCq)], [R(qr)])
            tt("pool", qr[0:M, 8:10, :], qr[0:M, 8:10, :], TCk[0:M, :].unsqueeze(1).to_broadcast([M, 2, 64]), ALU.mult, [R(qr), R(TCk)], [R(qr)])
            if has_q:
                tt("pool", qro[0:M, :, :], qr[0:M, 0:8, :], sq[0:M, 0:8, :], ALU.add, [R(qr), R(sq)], [R(qro)])
            tt("pool", kft[0:M, :, :], qr[0:M, 8:10, :], sq[0:M, 8:10, :], ALU.add, [R(qr), R(sq)], [R(kft)])
            cp("pool", kb[0:M, :, :], kft[0:M, :, :], [R(kft)], [R(kb)])
            if kout is not None:
                for (dst, r0, r1) in kout:
                    dma("sp", dst, kft[r0:r1, :, :].rearrange("p a b -> p (a b)"), [R(kft)], [])
            yield
            if has_q:
                bt_, tt_ = bank()
                b16 = tt_[:].bitcast(BF16)
                for h in range(8):
                    tr(b16[0:64, h * 128:h * 128 + M], qro[0:M, h, :], ident[0:M, 0:M], [R(qro), Rid], [bt_])
            bk_, tk_ = bank()
            k16 = tk_[:].bitcast(BF16)
            for kv in range(2):
                tr(k16[0:64, kv * 128:kv * 128 + M], kb[0:M, kv, :], ident[0:M, 0:M], [R(kb), Rid], [bk_])
            if has_q:
                cp("act", QT[:, :, 0:M], b16[0:64, 0:1024].rearrange("p (h t) -> p h t", h=8)[:, :, 0:M], [bt_], [R(QT)])
            cp("act", KT[slot][:, :, 0:M], k16[0:64, 0:256].rearrange("p (k t) -> p k t", k=2)[:, :, 0:M], [bk_], [R(KT[slot])])

        def newton_rsqrt_ap(src, v, t, yv, scale, iters=3, Rv=None, Rt=None):
            Rv = Rv or R(nwv)
            Rt = Rt or R(nwt)
            Ry, Rs = out_y_res[0], src_res[0]
            tsc("dve", v, src, scale, EPS, ALU.mult, ALU.add, [Rs], [Rv])
            tsc("dve", t, src, 0.5 * scale, 0.5 * EPS + 0.5, ALU.mult, ALU.add, [Rs], [Rt])
            P.add("dve", lambda e: e.reciprocal(out=yv, in_=t), [Rt], [Ry])
            for _ in range(iters):
                tt("dve", t, v, yv, ALU.mult, [Rv, Ry], [Rt])
                tt("dve", t, t, yv, ALU.mult, [Rt, Ry], [Rt])
                tsc("dve", t, t, -0.5, 1.5, ALU.mult, ALU.add, [Rt], [Rt])
                tt("dve", yv, yv, t, ALU.mult, [Ry, Rt], [Ry])

        pt_ctr = [0]

        def attend(Mq, blocks, pv_plan, col0):
            ncq = len(pv_plan)
            qw = pv_plan[0][1] - pv_plan[0][0]
            allpts = []
            for kv in range(2):
                pts = []
                for (kt_ap, Rkt, vd_ap, Rvd, nk, bias_ap, Rb) in blocks:
                    bs, ts_ = bank()
                    sview = ts_[0:nk, 0:4 * Mq]
                    mm(sview, kt_ap[:, kv, 0:nk], QT[:, 4 * kv:4 * kv + 4, 0:Mq], True, True, [Rkt, R(QT)], [bs])
                    pti = pt_ctr[0] % NPTB
                    pt_ctr[0] += 1
                    ptt = PT[pti]
                    rds = [bs] + ([Rb] if bias_ap is not None else [])
                    if bias_ap is not None:
                        act(ptt[0:nk, 0:4 * Mq], sview, AF.Exp, rds, [R(ptt)], bias=bias_ap[0:nk, :], scale=0.125)
                    else:
                        act(ptt[0:nk, 0:4 * Mq], sview, AF.Exp, rds, [R(ptt)], scale=0.125)
                    pts.append(ptt)
                allpts.append(pts)
            yield
            for kv in range(2):
                pts = allpts[kv]
                bo, to = bank()
                bsu, tsu = bank()
                for ci, (q0, q1, contrib) in enumerate(pv_plan):
                    n = len(contrib)
                    osl = slice(ci * 4 * qw, (ci + 1) * 4 * qw)
                    for ii, (bi, k0, k1) in enumerate(contrib):
                        vd_ap, Rvd = blocks[bi][2], blocks[bi][3]
                        p3 = pts[bi][:, 0:4 * Mq].rearrange("p (g q) -> p g q", g=4)
                        mm(to[:, osl], vd_ap[k0:k1, kv, :], p3[k0:k1, :, q0:q1], ii == 0, ii == n - 1, [Rvd, R(pts[bi])], [bo])
                    for ii, (bi, k0, k1) in enumerate(contrib):
                        p3 = pts[bi][:, 0:4 * Mq].rearrange("p (g q) -> p g q", g=4)
                        mm(tsu[:, osl], ones[k0:k1, :], p3[k0:k1, :, q0:q1], ii == 0, ii == n - 1, [Rones, R(pts[bi])], [bsu])
                nb_ = nrm if kv == 0 else sqflat
                Rnb_ = R(nrm) if kv == 0 else R(sq)
                s4 = tsu[:, 0:4 * Mq].rearrange("p (c g q) -> p c g q", c=ncq, g=4)
                n4 = nb_[:, 0:4 * Mq].rearrange("p (c g q) -> p c g q", c=ncq, g=4)
                for ci in range(ncq):
                    tt("dve", n4[:, ci], s4[:, ci], esink[:, 4 * kv:4 * kv + 4].unsqueeze(2).to_broadcast([128, 4, qw]), ALU.add, [bsu, R(esink)], [Rnb_])
                nfl = nb_[:, 0:4 * Mq]
                P.add("dve", lambda e, nfl=nfl: e.reciprocal(out=nfl, in_=nfl), [Rnb_], [Rnb_])
                o5 = to[:, 0:4 * Mq].rearrange("p (c j e q) -> p j e c q", c=ncq, j=2, e=2)
                n5 = nb_[:, 0:4 * Mq].rearrange("p (c j e q) -> p j e c q", c=ncq, j=2, e=2)
                for e_ in range(2):
                    ps_ = slice(64 * e_, 64 * e_ + 64)
                    atv = AT[ps_, 2 * kv:2 * kv + 2, col0:col0 + Mq].rearrange("p j (c q) -> p j c q", c=ncq)
                    for ci in range(ncq):
                        tt("dve", atv[:, :, ci, :], o5[ps_, :, e_, ci, :], n5[ps_, :, e_, ci, :], ALU.mult, [bo, Rnb_], [R(AT)])

        def zT_pair(N, colA, colB, RwA, RwB, hold=False):
            br, bt = bank(hold=hold)
            for half, (col, Rw) in enumerate(((colA, RwA), (colB, RwB))):
                for c in range(8):
                    mm(bt[:, half * ST:half * ST + N], w_in[:, c, col:col + 128], hT[:, c, 0:N], c == 0, c == 7, [Rw] + Rh(0, N), [br])
            return br, bt

        tg_ctr = [0]

        def conv_branch(N, nseq, T, tail_src, is_last_prompt, is_sample, js=(0, 1, 2, 3)):
            for j in js:
                for _ in conv_gen(N, nseq, T, tail_src, is_last_prompt, is_sample, j):
                    pass

        def conv_gen(N, nseq, T, tail_src, is_last_prompt, is_sample, j):
            if True:
                b1, t1 = zT_pair(N, C_C + j * 128, C_U + j * 128, WG["c"], WG["u"], hold=True)
                b2, t2 = zT_pair(N, C_B + j * 128, C_GC + j * 128, WG["b"], WG["gC"], hold=True)
                yield
                upv = up[:, 0:nseq * (T + 2)].rearrange("p (s t) -> p s t", s=nseq)
                if tail_src is None:
                    cp("pool", upv[:, 0, 0:2], tail[:, j, :], [R(tail)], [R(up)])
                else:
                    cp("pool", upv[:, :, 0:2], tail_src[:, j, :, :], [R(stail)], [R(up)])
                cp("act", csb[:, 0:N], t1[:, 0:N], [b1], [R(csb)])
                tgt = tgate[tg_ctr[0] % 2]
                tg_ctr[0] += 1
                act(tgt[:, 0:N], t2[:, ST:ST + N], AF.Tanh, [b2], [R(tgt)], scale=0.5)
                tt("dve", upv[:, :, 2:T + 2], csb[:, 0:N].rearrange("p (s t) -> p s t", s=nseq),
                   t1[:, ST:ST + N].rearrange("p (s t) -> p s t", s=nseq), ALU.mult, [R(csb), b1], [R(up)])
                stt("dve", tgt[:, 0:N], tgt[:, 0:N], 1.0, t2[:, ST:ST + N], ALU.add, ALU.mult, [R(tgt), b2], [R(tgt)])
                tt("dve", tgt[:, 0:N], tgt[:, 0:N], t2[:, 0:N], ALU.mult, [R(tgt), b2], [R(tgt)])
                if not is_sample:
                    cp("pool", tail[:, j, :], upv[:, 0, T:T + 2], [R(up)], [R(tail)])
                    if is_last_prompt:
                        cp("pool", coutp[:, j, :], upv[:, 0, T:T + 2], [R(up)], [R(coutp)])
                else:
                    cp("pool", couts[:, j, :, :], upv[:, :, T:T + 2], [R(up)], [R(couts)])
                a3 = acc[:, 0:N].rearrange("p (s t) -> p s t", s=nseq)
                act(a3, upv[:, :, 0:T], AF.Copy, [R(up), R(convw)], [R(acc)], scale=convw[:, j, 0:1])
                stt("dve", a3, upv[:, :, 1:T + 1], convw[:, j, 1:2], a3, ALU.mult, ALU.add, [R(up), R(convw), R(acc)], [R(acc)])
                stt("dve", a3, upv[:, :, 2:T + 2], convw[:, j, 2:3], a3, ALU.mult, ALU.add, [R(up), R(convw), R(acc)], [R(acc)])
                tt("pool", PC[:, j, 0:N], acc[:, 0:N], tgt[:, 0:N], ALU.mult, [R(acc), R(tgt)], [R(PC)])
                unhold(b1)
                unhold(b2)

        def conv_state_only(N):
            for j in range(4):
                b1, t1 = zT_pair(N, C_C + j * 128, C_U + j * 128, WG["c"], WG["u"])
                cp("act", csb[:, 0:2], t1[:, N - 2:N], [b1], [R(csb)])
                tt("dve", tail[:, j, :], csb[:, 0:2], t1[:, ST + N - 2:ST + N], ALU.mult, [R(csb), b1], [R(tail)])

        def gate_a(N):
            for jj in range(2):
                br, bt = zT_pair(N, C_GA + (2 * jj) * 128, C_GA + (2 * jj + 1) * 128, WG["gA"], WG["gA"])
                for half in range(2):
                    j = 2 * jj + half
                    tgt = tgate[tg_ctr[0] % 2]
                    tg_ctr[0] += 1
                    act(tgt[:, 0:N], bt[:, half * ST:half * ST + N], AF.Tanh, [br], [R(tgt)], scale=0.5)
                    stt("dve", tgt[:, 0:N], tgt[:, 0:N], 1.0, bt[:, half * ST:half * ST + N], ALU.add, ALU.mult, [R(tgt), br], [R(tgt)])
                    tt("pool", AT[:, j, 0:N], AT[:, j, 0:N], tgt[:, 0:N], ALU.mult, [R(AT), R(tgt)], [R(AT)])

        def mprime(N, k):
            bA, tAb = bank(hold=True)
            for c in range(8):
                mm(tAb[:, 0:N], w_in[:, c, C_MA + k * 128:C_MA + (k + 1) * 128], hT[:, c, 0:N], c == 0, c == 7, [WG["mA"]] + Rh(0, N), [bA])
            bB, tBb = bank(hold=True)
            for c in range(8):
                mm(tBb[:, 0:N], w_in[:, c, C_MB + k * 128:C_MB + (k + 1) * 128], hT[:, c, 0:N], c == 0, c == 7, [WG["mB"]] + Rh(0, N), [bB])
            return (bA, tAb, bB, tBb)

        def ymerge(N, k, bks):
            bA, tAb, bB, tBb = bks
            for j in range(4):
                mm(tBb[:, ST:ST + N], w_b[:, j, k * 128:(k + 1) * 128], PC[:, j, 0:N], j == 0, j == 3, [R_wb, R(PC)], [bB])
            for j in range(4):
                mm(tAb[:, ST:ST + N], w_a[:, j, k * 128:(k + 1) * 128], AT[:, j, 0:N], j == 0, j == 3, [R_wa, R(AT)], [bA])
            act(tB[:, 0:N], tBb[:, 0:N], AF.Tanh, [bB], [R(tB)], scale=0.5)
            act(tA[:, 0:N], tAb[:, 0:N], AF.Tanh, [bA], [R(tA)], scale=0.5)
            stt("dve", tB[:, 0:N], tB[:, 0:N], 1.0, tBb[:, ST:ST + N], ALU.add, ALU.mult, [R(tB), bB], [R(tB)])
            stt("dve", tA[:, 0:N], tA[:, 0:N], 1.0, tAb[:, ST:ST + N], ALU.add, ALU.mult, [R(tA), bA], [R(tA)])
            tt("pool", mT[:, k, 0:N], tA[:, 0:N], tB[:, 0:N], ALU.add, [R(tA), R(tB)], [RmT[k]])
            unhold(bA)
            unhold(bB)

        def merge(N):
            bks = mprime(N, 0)
            for k in range(8):
                nxt = mprime(N, k + 1) if k < 7 else None
                ymerge(N, k, bks)
                bks = nxt

        pf_ctr = [0]

        def r_phase(g, col0, yrow0):
            xt = xs[x_slot_of[g]]
            pb = pb2[pf_slot[g]]
            rbanks = []
            for hh in range(2):
                br, bt = bank()
                for k in range(8):
                    mm(bt[:, :], mT[:, k, col0:col0 + 128], w_o[:, k, hh * 512:(hh + 1) * 512], k == 0, k == 7, [RmT[k], R_wo], [br])
                rbanks.append((br, bt))
            for hh in range(2):
                br, bt = rbanks[hh]
                stt("dve", xt[:, hh * 512:(hh + 1) * 512], bt[:, :], 0.25, xt[:, hh * 512:(hh + 1) * 512], ALU.mult, ALU.add, [br, R(xt)], [R(xt)])
            yield
            cp("act", rb[:], xt[:], [R(xt)], [R(rb)])
            yield
            b1, t1 = bank()
            b16 = t1[:].bitcast(BF16)
            for c in range(8):
                tr(b16[:, c * 128:(c + 1) * 128], rb[:, c * 128:(c + 1) * 128], ident[:], [R(rb), Rid], [b1])
            cp("dve", rT[:].rearrange("p c t -> p (c t)"), b16[:, :], [b1], [R(rT)])
            b2, t2 = bank()
            b16b = t2[:].bitcast(BF16)
            for c in range(2):
                tr(b16b[:, c * 128:(c + 1) * 128], pb[:, c * 128:(c + 1) * 128], ident[:], [R(pb), Rid], [b2])
            act(pT[:].rearrange("p c t -> p (c t)"), b16b[:, 0:256], AF.Copy, [b2], [R(pT)], scale=0.5)
            yield
            for hh in range(2):
                bg, tgb = bank()
                for k in range(8):
                    mm(tgb[:, :], rT[:, k, :], w_pg[:, k, hh * 512:(hh + 1) * 512], k == 0, k == 7, [R(rT), R_wpg], [bg])
                bp, tpb = bank()
                for c in range(2):
                    mm(tpb[:, :], pT[:, c, :], w_pp[:, c, hh * 512:(hh + 1) * 512], c == 0, c == 1, [R(pT), R_wpp], [bp])
                sl = slice(hh * 512, (hh + 1) * 512)
                act(tg[:, :], tgb[:, :], AF.Tanh, [bg], [R(tg)], scale=0.5)
                stt("dve", tg[:, :], tg[:, :], 1.0, tpb[:, :], ALU.add, ALU.mult, [R(tg), bp], [R(tg)])
                tt("dve", xt[:, sl], xt[:, sl], tg[:, :], ALU.add, [R(tg), R(xt)], [R(xt)])
            dma("sp", y_d[yrow0:yrow0 + 128, :], xt[:], [R(xt)], [])

        def run(gen):
            for _ in gen:
                pass

        pf_slot = {}

        def load_p(g):
            i = pf_ctr[0] % 2
            pf_ctr[0] += 1
            pf_slot[g] = i
            dma("pool", pb2[i][:], ph[g * 128:(g + 1) * 128, :], [], [R(pb2[i])])

        run(norm_tile(0, 0))
        run(qkv_tile(0, 0, 128, 0, False))
        issue_w(7)
        conv_state_only(128)
        NB = TOK_PC // ST
        load_x(2)
        run(norm_tile(1, 0))
        run(norm_tile(2, 128))

        def start_qkv(blk):
            g0, g1 = 1 + 2 * blk, 2 + 2 * blk
            last = blk == NB - 1
            gens = []
            for i, g in enumerate((g0, g1)):
                kout = vout = None
                if last and i == 1:
                    kout = [(klast_d[:, :], 0, 128)]
                    vout = [(vlast_d[:, :], 0, 128)]
                qg_ = qkv_tile(g * 128, i * 128, 128, g % 3, True, kout, vout)
                next(qg_)
                gens.append(qg_)
            return gens

        def start_qkv_one(blk, i):
            g = 1 + 2 * blk + i
            last = blk == NB - 1
            kout = vout = None
            if last and i == 1:
                kout = [(klast_d[:, :], 0, 128)]
                vout = [(vlast_d[:, :], 0, 128)]
            qg_ = qkv_tile(g * 128, i * 128, 128, g % 3, True, kout, vout)
            next(qg_)
            return [qg_]

        qgen = start_qkv(0)
        issue_w(5)
        plan = [(0, 64, [(0, 0, 128), (1, 0, 64)]), (64, 128, [(0, 64, 128), (1, 0, 128)])]

        def mk_att(i, g):
            blocks = [
                (KT[(g - 1) % 3], R(KT[(g - 1) % 3]), vd[(g - 1) % 3], R(vd[(g - 1) % 3]), 128, (hbias if g == 1 else None), R(hbias)),
                (KT[g % 3], R(KT[g % 3]), vd[g % 3], R(vd[g % 3]), 128, None, None),
            ]
            return attend(128, blocks, plan, i * 128)

        gS = NPT + 1
        sgen = {}

        def sample_q1(b):
            slot = (gS + b) % 3
            g_ = qkv_tile(gS * 128 + b * 32, b * 32, 32, slot, True,
                          kout=[(ks_d[b, 96:128, :], 0, 32)], vout=[(vs_d[b, 96:128, :], 0, 32)])
            next(g_)
            return g_

        for blk in range(NB):
            g0, g1 = 1 + 2 * blk, 2 + 2 * blk
            last = blk == NB - 1
            ng0 = g0 + 2
            ng1 = g1 + 2 if not last else None
            load_p(g0)
            load_p(g1)
            load_x(ng0)
            if ng1 is not None:
                load_x(ng1)
            load_stail(2)
            issue_w(5)
            run(qgen[0])
            cg = conv_gen(ST, 1, ST, None, last, False, 0)
            next(cg)
            at0 = mk_att(0, g0)
            next(at0)
            run(cg)
            run(at0)
            run(qgen[1])
            cg = conv_gen(ST, 1, ST, None, last, False, 1)
            next(cg)
            at1 = mk_att(1, g1)
            next(at1)
            run(cg)
            run(at1)
            n0 = norm_tile(ng0, 0)
            next(n0)
            conv_branch(ST, 1, ST, None, last, False, js=(2,))
            bks = mprime(ST, 0)
            n1 = None
            if ng1 is not None:
                n1 = norm_tile(ng1, 128)
                next(n1)
            conv_branch(ST, 1, ST, None, last, False, js=(3,))
            gate_a(ST)
            for k in range(8):
                nxt = mprime(ST, k + 1) if k < 7 else None
                if k == 7:
                    next(n0)
                ymerge(ST, k, bks)
                bks = nxt
            ra = r_phase(g0, 0, (g0 - 1) * 128)
            rb_ = r_phase(g1, 128, (g1 - 1) * 128)
            next(ra)
            run(n0)
            if n1 is not None:
                next(n1)
            next(rb_)
            if n1 is not None:
                run(n1)
            next(ra)
            next(ra)
            next(rb_)
            if not last:
                qgen = start_qkv_one(blk + 1, 0)
            else:
                sgen[0] = sample_q1(0)
            run(ra)
            next(rb_)
            if not last:
                qgen.append(start_qkv_one(blk + 1, 1)[0])
            else:
                sgen[1] = sample_q1(1)
            run(rb_)
        gS = NPT + 1
        load_stail(16)
        load_p(gS)
        for b in range(4):
            dma("sp", ckf[:], ck_d[b], [], [R(ckf)])
            dma("sp", cvf[:], cv_d[b], [], [R(cvf)])
            dma("sp", ks_d[b, 0:96, :], ck_d[b, 32:128, :], [], [])
            dma("sp", vs_d[b, 0:96, :], cv_d[b, 32:128, :], [], [])
            cp("pool", ckb[:], ckf[:], [R(ckf)], [R(ckb)])
            for rr in range(2):
                cp("act", vdc[:, :, rr * 64:(rr + 1) * 64], cvf[:].rearrange("p (h d) -> p h d", h=2), [R(cvf)], [R(vdc)])
            bt_, tt_ = bank()
            b16 = tt_[:].bitcast(BF16)
            for kv in range(2):
                tr(b16[0:64, kv * 128:(kv + 1) * 128], ckb[:, kv * 64:(kv + 1) * 64], ident[:], [R(ckb), Rid], [bt_])
            cp("act", KTc[:].rearrange("p k t -> p (k t)"), b16[0:64, 0:256], [bt_], [R(KTc)])
            slot = (gS + b) % 3
            run(sgen[b])
            if b + 2 < 4:
                sgen[b + 2] = sample_q1(b + 2)
            blocks = [
                (KTc, R(KTc), vdc, R(vdc), 128, None, None),
                (KT[slot], R(KT[slot]), vd[slot], R(vd[slot]), 32, None, None),
            ]
            plan = [(0, 32, [(0, 0, 128), (1, 0, 32)])]
            cg = conv_gen(128, 4, 32, stail, False, True, b)
            next(cg)
            at_ = attend(32, blocks, plan, b * 32)
            next(at_)
            run(cg)
            run(at_)
        gate_a(128)
        merge(128)
        run(r_phase(gS, 0, TOK_PC))
        for j in range(4):
            dma("sp", clast_d[:, j * 128:(j + 1) * 128].rearrange("t p -> p t"), coutp[:, j, :], [R(coutp)], [], slow=True)
            for b in range(4):
                dma("sp", cs_d[b, :, j * 128:(j + 1) * 128].rearrange("t p -> p t"), couts[:, j, b, :], [R(couts)], [], slow=True)

        sems = {}
        for nme in P.sem_names:
            sems[nme] = es.enter_context(nc.semaphore(nme))
        esem = {}
        sigcount = {}
        for e in ENGS:
            esem[e] = es.enter_context(nc.semaphore("eng_" + e))
            c = 0
            for op in P.ops[e]:
                if op.kind == "op" and op.sig:
                    c += 1
                    op.val = c
            sigcount[e] = c
        block = es.enter_context(nc.Block())

        def emit(ename, eng):
            inline_ok = ename in ("dve", "pool", "act", "pe")
            for op in P.ops[ename]:
                ws = list(op.waits)
                last = None
                if inline_ok and ws and op.kind == "op" and id(op) not in MULTI:
                    last = ws.pop()
                for w in ws:
                    if w[0] == "sem":
                        eng.wait_ge(sems[w[1]], w[2])
                    else:
                        X = w[1]
                        eng.wait_ge(esem[X.eng], X.val)
                ins = op.fn(eng)
                if last is not None:
                    if last[0] == "sem":
                        ins._wait_ge(sems[last[1]], last[2])
                    else:
                        ins._wait_ge(esem[last[1].eng], last[1].val)
                if op.kind == "dmaop":
                    ins.then_inc(sems[op.inc], 16)
                elif op.sig:
                    ins.then_inc(esem[ename], 1)
            if ename == "sp":
                for Rr in P.store_res:
                    eng.wait_ge(sems[Rr.rsem], Rr.rcount)
                if P.final_count:
                    eng.wait_ge(sems[P.final_sem], P.final_count)

        @block.sync
        def _(eng):
            emit("sp", eng)

        @block.scalar
        def _(eng):
            emit("act", eng)

        @block.vector
        def _(eng):
            emit("dve", eng)

        @block.gpsimd
        def _(eng):
            emit("pool", eng)

        @block.tensor
        def _(eng):
            emit("pe", eng)

    return nc


_NC_CACHE = {}


def kernel(x_prompt, x_sample, p_prompt, p_sample, cache_k, cache_v, state_conv,
           ln_g, w_in, q_norm_g, k_norm_g, sink, conv_w, w_attn_out, w_conv_out, w_o,
           w_ple_gate, w_ple_proj):
    f = np.float32
    xp = np.asarray(x_prompt, f).reshape(SEQ, D)
    xsm = np.asarray(x_sample, f).reshape(32 * 32, D)
    pp = np.asarray(p_prompt, f).reshape(SEQ, 256)
    psm = np.asarray(p_sample, f).reshape(32 * 32, 256)
    ck = np.asarray(cache_k, f).reshape(32, 128, 128)
    cv = np.asarray(cache_v, f).reshape(32, 128, 128)
    sc = np.asarray(state_conv, f).reshape(32, 2, 512)
    def rope_rows(pos):
        try:
            import jax
            import jax.numpy as jnp
            with jax.default_device(jax.devices("cpu")[0]):
                inv_freq = 10000.0 ** (-jnp.arange(0, 32, dtype=jnp.float32) * 2.0 / 64)
                ang = jnp.asarray(pos.astype(np.float32))[:, None] * inv_freq[None, :]
                c = np.asarray(jnp.cos(ang), dtype=f)
                s = np.asarray(jnp.sin(ang), dtype=f)
        except Exception:
            inv_freq = 10000.0 ** (-np.arange(0, 32, dtype=np.float64) * 2.0 / 64)
            ang = pos.astype(np.float64)[:, None] * inv_freq[None, :]
            c, s = np.cos(ang).astype(f), np.sin(ang).astype(f)
        return np.concatenate([c, c, -s, s], axis=1).astype(f)

    if "nc" not in _NC_CACHE:
        _NC_CACHE["nc"] = build_program()
    nc = _NC_CACHE["nc"]
    shared = {
        "ln_g": np.asarray(ln_g, f).reshape(D), "w_in": np.asarray(w_in, f).reshape(D, IN_DIM),
        "qg": np.asarray(q_norm_g, f).reshape(64), "kg": np.asarray(k_norm_g, f).reshape(64),
        "sink": np.asarray(sink, f).reshape(8), "conv_w": np.asarray(conv_w, f).reshape(3, 512),
        "w_a": np.asarray(w_attn_out, f).reshape(512, D), "w_b": np.asarray(w_conv_out, f).reshape(512, D),
        "w_o": np.asarray(w_o, f).reshape(D, D), "w_pg": np.asarray(w_ple_gate, f).reshape(D, D),
        "w_pp": np.asarray(w_ple_proj, f).reshape(256, D),
    }
    in_maps = []
    for c in range(NCORES):
        t0 = c * TOK_PC
        xhc = np.zeros((NROWS, D), f)
        phc = np.zeros((NROWS, 256), f)
        pos = np.zeros((NROWS,), np.float64)
        if c > 0:
            xhc[0:128] = xp[t0 - 128:t0]
        pos[0:128] = np.arange(t0 - 128, t0)
        xhc[128:128 + TOK_PC] = xp[t0:t0 + TOK_PC]
        phc[128:128 + TOK_PC] = pp[t0:t0 + TOK_PC]
        pos[128:128 + TOK_PC] = np.arange(t0, t0 + TOK_PC)
        xhc[128 + TOK_PC:] = xsm[c * 128:(c + 1) * 128]
        phc[128 + TOK_PC:] = psm[c * 128:(c + 1) * 128]
        pos[128 + TOK_PC:] = np.tile(1024 + np.arange(32), 4)
        hb = np.full((128, 1), -30000.0 if c == 0 else 0.0, f)
        m = dict(shared)
        m.update({"xh": xhc, "ph": phc, "rope": rope_rows(pos),
                  "ck": np.ascontiguousarray(ck[c * 4:(c + 1) * 4]), "cv": np.ascontiguousarray(cv[c * 4:(c + 1) * 4]),
                  "sconv": np.ascontiguousarray(sc[c * 4:(c + 1) * 4]), "hbias": hb})
        in_maps.append(m)
    out = run_bass_kernel_spmd(nc, in_maps, core_ids=list(range(NCORES)))
    rs = out.results
    y_prompt = np.concatenate([r["y"][0:TOK_PC] for r in rs], axis=0).reshape(1, SEQ, D)
    y_sample = np.concatenate([r["y"][TOK_PC:] for r in rs], axis=0).reshape(32, 32, D)
    k_prompt = rs[-1]["k_last"].reshape(1, 1, 128, 2, 64)
    v_prompt = rs[-1]["v_last"].reshape(1, 1, 128, 2, 64)
    conv_prompt = rs[-1]["c_last"].reshape(1, 1, 2, 512)
    k_sample = np.concatenate([r["ks"] for r in rs], axis=0).reshape(1, 32, 128, 2, 64)
    v_sample = np.concatenate([r["vs"] for r in rs], axis=0).reshape(1, 32, 128, 2, 64)
    conv_sample = np.concatenate([r["cs"] for r in rs], axis=0).reshape(1, 32, 2, 512)
    return (y_prompt.astype(f), y_sample.astype(f), k_prompt.astype(f), v_prompt.astype(f),
            conv_prompt.astype(f), k_sample.astype(f), v_sample.astype(f), conv_sample.astype(f))
```

```python
import contextlib
import numpy as np
import concourse.bass as bass
import concourse.mybir as mybir
from concourse.bass_utils import run_bass_kernel_spmd

F32 = mybir.dt.float32
BF16 = mybir.dt.bfloat16
AF = mybir.ActivationFunctionType
ALU = mybir.AluOpType
AX = mybir.AxisListType

NCORES = 8
D = 1024
SEQ = 16384
TOK_PC = SEQ // NCORES
NPT = TOK_PC // 128
NROWS = 128 + TOK_PC + 128
ST = 256
IN_DIM = 5376
EPS = 1e-6
C_Q, C_K, C_V, C_GA, C_B, C_C, C_U, C_GC, C_MA, C_MB = 0, 512, 640, 768, 1280, 1792, 2304, 2816, 3328, 4352

ENGS = ("sp", "act", "dve", "pool", "pe")
STRICT_SAME_ENGINE = True
STRICT_ENGINES = ("pool", "dve")


class Res:
    def __init__(self, name):
        self.name = name
        self.w = None
        self.readers = {}
        self.rd_dma = None
        self.dsem = None
        self.dcount = 0
        self.rsem = None
        self.rcount = 0


class Op:
    __slots__ = ("kind", "eng", "idx", "fn", "waits", "sig", "inc", "sem", "val", "know")

    def __init__(self, kind, eng, idx, fn):
        self.kind = kind
        self.eng = eng
        self.idx = idx
        self.fn = fn
        self.waits = []
        self.sig = False
        self.inc = None
        self.sem = None
        self.val = 0
        self.know = {}


class Mark:
    kind = "dma"

    def __init__(self, sem, val, know=None):
        self.sem = sem
        self.val = val
        self.know = know or {}


class Planner:
    def __init__(self):
        self.ops = {e: [] for e in ENGS}
        self.seen = {e: {} for e in ENGS}
        self.seen_sem = {e: {} for e in ENGS}
        self.sem_names = []
        self.final_sem = self.new_sem("final")
        self.final_count = 0
        self.store_res = []

    def new_sem(self, name):
        n = "s%d_%s" % (len(self.sem_names), name)
        self.sem_names.append(n)
        return n

    def _dep(self, eng, waits, X, raw):
        if X is None:
            return
        if X.kind == "dma":
            if self.seen_sem[eng].get(X.sem, 0) >= X.val:
                return
            self.seen_sem[eng][X.sem] = X.val
            waits.append(("sem", X.sem, X.val))
            self._learn(eng, X.know)
            return
        if X.eng == eng and (eng == "pe" or not (raw or (STRICT_SAME_ENGINE and eng in STRICT_ENGINES))):
            return
        if self.seen[eng].get(X.eng, -1) >= X.idx:
            return
        self.seen[eng][X.eng] = X.idx
        X.sig = True
        waits.append(("op", X))
        self._learn(eng, X.know)

    def _learn(self, eng, know):
        se = self.seen[eng]
        for g, i in know.items():
            if se.get(g, -1) < i:
                se[g] = i

    def _deps(self, eng, reads, writes):
        waits = []
        for r in reads:
            self._dep(eng, waits, r.w, True)
        for w in writes:
            self._dep(eng, waits, w.w, False)
            for o in w.readers.values():
                self._dep(eng, waits, o, False)
            if w.rd_dma is not None:
                self._dep(eng, waits, w.rd_dma, False)
                w.rd_dma = None
        return waits

    def add(self, eng, fn, reads=(), writes=()):
        op = Op("op", eng, len(self.ops[eng]), fn)
        op.waits = self._deps(eng, reads, writes)
        self.ops[eng].append(op)
        op.know = dict(self.seen[eng])
        op.know[eng] = op.idx
        for r in reads:
            r.readers[eng] = op
        for w in writes:
            w.w = op
            w.readers = {}
        return op

    def dma(self, eng, fn, reads=(), writes=()):
        op = Op("dmaop", eng, len(self.ops[eng]), fn)
        op.waits = self._deps(eng, reads, writes)
        self.ops[eng].append(op)
        if writes:
            W = writes[0]
            if W.dsem is None:
                W.dsem = self.new_sem("d_" + W.name)
            W.dcount += 16
            op.inc = W.dsem
            W.w = Mark(W.dsem, W.dcount, dict(self.seen[eng]))
            W.readers = {}
        elif reads:
            R = reads[0]
            if R.rsem is None:
                R.rsem = self.new_sem("r_" + R.name)
                self.store_res.append(R)
            R.rcount += 16
            op.inc = R.rsem
            R.rd_dma = Mark(R.rsem, R.rcount, dict(self.seen[eng]))
        else:
            self.final_count += 16
            op.inc = self.final_sem
        return op


def build_program():
    nc = bass.Bass("TRN2", target_bir_lowering=False)
    P = Planner()

    def din(name, shape):
        return nc.dram_tensor(name, list(shape), F32, kind="ExternalInput").ap()

    def dout(name, shape):
        return nc.dram_tensor(name, list(shape), F32, kind="ExternalOutput").ap()

    xh = din("xh", [NROWS, D])
    ph = din("ph", [NROWS, 256])
    rope = din("rope", [NROWS, 128])
    ck_d = din("ck", [4, 128, 128])
    cv_d = din("cv", [4, 128, 128])
    sconv_d = din("sconv", [4, 2, 512])
    hbias_d = din("hbias", [128, 1])
    lng_d = din("ln_g", [D])
    win_d = din("w_in", [D, IN_DIM])
    qg_d = din("qg", [64])
    kg_d = din("kg", [64])
    sink_d = din("sink", [8])
    convw_d = din("conv_w", [3, 512])
    wa_d = din("w_a", [512, D])
    wb_d = din("w_b", [512, D])
    wo_d = din("w_o", [D, D])
    wpg_d = din("w_pg", [D, D])
    wpp_d = din("w_pp", [256, D])

    y_d = dout("y", [TOK_PC + 128, D])
    klast_d = dout("k_last", [128, 128])
    vlast_d = dout("v_last", [128, 128])
    clast_d = dout("c_last", [2, 512])
    ks_d = dout("ks", [4, 128, 128])
    vs_d = dout("vs", [4, 128, 128])
    cs_d = dout("cs", [4, 2, 512])

    es = contextlib.ExitStack()
    with es:
        res = {}

        def sb(name, shape, dt):
            t = es.enter_context(nc.sbuf_tensor("t_" + name, list(shape), dt))
            res["t_" + name] = Res(name)
            return t

        w_in = sb("w_in_sb", [128, 8, IN_DIM], BF16)
        w_a = sb("w_a_sb", [128, 4, D], BF16)
        w_b = sb("w_b_sb", [128, 4, D], BF16)
        w_o = sb("w_o_sb", [128, 8, D], BF16)
        w_pg = sb("w_pg_sb", [128, 8, D], BF16)
        w_pp = sb("w_pp_sb", [128, 2, D], BF16)
        WG = {}
        for nm in ("qkv", "gA", "b", "c", "u", "gC", "mA", "mB"):
            WG[nm] = Res("win_" + nm)
        R_wa, R_wb, R_wo, R_wpg, R_wpp = res["t_w_a_sb"], res["t_w_b_sb"], res["t_w_o_sb"], res["t_w_pg_sb"], res["t_w_pp_sb"]

        ident = sb("ident", [128, 128], BF16)
        identf = sb("identf", [128, 128], F32)
        ones = sb("ones", [128, 128], BF16)
        g_ln = sb("g_ln", [128, 8], F32)
        gq = sb("gq", [128, 64], F32)
        rgq = sb("rgq", [128, 64], F32)
        gk = sb("gk", [128, 64], F32)
        rgk = sb("rgk", [128, 64], F32)
        esink = sb("esink", [128, 8], F32)
        convw = sb("convw", [128, 4, 3], F32)
        hbias = sb("hbias", [128, 1], F32)
        stail = sb("stail", [128, 4, 4, 2], F32)

        NXS = 4
        xs = [sb("xs%d" % i, [128, D], F32) for i in range(NXS)]
        xn = sb("xn", [128, D], BF16)
        sx = [sb("sx%d" % i, [128, 4], F32) for i in range(2)]
        hT = sb("hT", [128, 8, ST], BF16)
        csl = [sb("cs%d" % i, [128, 128], F32) for i in range(2)]
        TCq = sb("TCq", [128, 64], F32)
        TSq = sb("TSq", [128, 64], F32)
        TCk = sb("TCk", [128, 64], F32)
        TSk = sb("TSk", [128, 64], F32)
        sq = sb("sq", [128, 10, 64], F32)
        sqflat = sq[:].rearrange("p h d -> p (h d)")
        qr = sb("qr", [128, 10, 64], F32)
        ssq = sb("ssq", [128, 10], F32)
        nwv = sb("nwv", [128, 10], F32)
        nwt = sb("nwt", [128, 10], F32)
        nwy = sb("nwy", [128, 10], F32)
        qro2 = [sb("qro%d" % i, [128, 8, 64], BF16) for i in range(2)]
        kf = [sb("kf%d" % i, [128, 2, 64], F32) for i in range(2)]
        kb2 = [sb("kb%d" % i, [128, 2, 64], BF16) for i in range(2)]
        vf = [sb("vf0", [128, 128], F32)] * 2
        vd = [sb("vd%d" % i, [128, 2, 128], BF16) for i in range(3)]
        QT = sb("QT", [64, 8, 128], BF16)
        KT = [sb("KT%d" % i, [64, 2, 128], BF16) for i in range(3)]
        ckf = sb("ckf", [128, 128], F32)
        ckb = sb("ckb", [128, 128], BF16)
        cvf = sb("cvf", [128, 128], F32)
        KTc = sb("KTc", [64, 2, 128], BF16)
        vdc = sb("vdc", [128, 2, 128], BF16)
        NPTB = 4
        PT = [sb("PT%d" % i, [128, 512], BF16) for i in range(NPTB)]
        nrm = sb("nrm", [128, 512], F32)
        AT = sb("AT", [128, 4, ST], BF16)
        tgate = [sb("tgate%d" % i, [128, ST], F32) for i in range(2)]
        csb = sb("csb", [128, ST], F32)
        up = sb("up", [128, ST + 8], F32)
        acc = sb("acc", [128, ST], F32)
        PC = sb("PC", [128, 4, ST], BF16)
        tail = sb("tail", [128, 4, 2], F32)
        couts = sb("couts", [128, 4, 4, 2], F32)
        coutp = sb("coutp", [128, 4, 2], F32)
        tA = sb("tA", [128, ST], F32)
        tB = sb("tB", [128, ST], F32)
        mT = sb("mT", [128, 8, ST], BF16)
        rb = sb("rb", [128, D], BF16)
        rT = sb("rT", [128, 8, 128], BF16)
        tg = sb("tg", [128, 512], F32)
        pb2 = [sb("pb%d" % i, [128, 256], BF16) for i in range(2)]
        pT = sb("pT", [128, 2, 128], BF16)

        def R(t):
            return res[t.name] if hasattr(t, "name") else t

        RhA, RhB = Res("hT_a"), Res("hT_b")
        RmT = [Res("mT%d" % k) for k in range(8)]

        def Rh(c0, n):
            out = []
            if c0 < 128:
                out.append(RhA)
            if c0 + n > 128:
                out.append(RhB)
            return out

        banks = []
        for i in range(8):
            t = es.enter_context(nc.psum_tensor("ps%d" % i, [128, 512], F32))
            banks.append((Res("ps%d" % i), t))
        bank_ctr = [0]

        held = set()

        def bank(hold=False):
            while True:
                i = bank_ctr[0] % 8
                bank_ctr[0] += 1
                if i not in held:
                    break
            if hold:
                held.add(i)
            r, t = banks[i]
            return r, t

        def unhold(r):
            for i, (rr, _) in enumerate(banks):
                if rr is r:
                    held.discard(i)

        MULTI = set()
        def mm(out, lhsT, rhs, start, stop, reads, writes):
            P.add("pe", lambda e: e.matmul(out, lhsT=lhsT, rhs=rhs, start=start, stop=stop), reads, writes)

        def tr(out, in_, idn, reads, writes):
            P.add("pe", lambda e: e.transpose(out=out, in_=in_, identity=idn), reads, writes)

        def act(out, in_, func, reads, writes, bias=None, scale=None, accum_out=None):
            kw = {}
            if bias is not None:
                kw["bias"] = bias
            if scale is not None:
                kw["scale"] = scale
            if accum_out is not None:
                kw["accum_out"] = accum_out
            op_ = P.add("act", lambda e: e.activation(out=out, in_=in_, func=func, **kw), reads, writes)
            if accum_out is not None:
                MULTI.add(id(op_))

        def tt(eng, out, in0, in1, op, reads, writes):
            P.add(eng, lambda e: e.tensor_tensor(out=out, in0=in0, in1=in1, op=op), reads, writes)

        def tsc(eng, out, in0, s1, s2, op0, op1, reads, writes):
            if op1 is None:
                P.add(eng, lambda e: e.tensor_scalar(out=out, in0=in0, scalar1=s1, scalar2=None, op0=op0), reads, writes)
            else:
                P.add(eng, lambda e: e.tensor_scalar(out=out, in0=in0, scalar1=s1, scalar2=s2, op0=op0, op1=op1), reads, writes)

        def stt(eng, out, in0, scalar, in1, op0, op1, reads, writes):
            P.add(eng, lambda e: e.scalar_tensor_tensor(out=out, in0=in0, scalar=scalar, in1=in1, op0=op0, op1=op1), reads, writes)

        def cp(eng, out, in_, reads, writes):
            if eng == "act":
                act(out, in_, AF.Copy, reads, writes)
            else:
                P.add(eng, lambda e: e.tensor_copy(out=out, in_=in_), reads, writes)

        def dma(eng, out, in_, reads, writes, slow=False):
            if slow:
                P.dma(eng, lambda e: e.dma_start(out=out, in_=in_, allow_slow_non_contiguous=True), reads, writes)
            else:
                P.dma(eng, lambda e: e.dma_start(out=out, in_=in_), reads, writes)

        Rid, Ridf, Rones = R(ident), R(identf), R(ones)
        P.add("pool", lambda e: e.memset(identf[:], 1.0), [], [Ridf])
        P.add("pool", lambda e: e.affine_select(out=identf[:], in_=identf[:], pattern=[[-1, 128]],
                                                compare_op=ALU.is_equal, fill=0.0, base=0, channel_multiplier=1),
              [Ridf], [Ridf])
        cp("pool", ident[:], identf[:], [Ridf], [Rid])
        P.add("pool", lambda e: e.memset(ones[:], 1.0), [], [Rones])
        P.add("pool", lambda e: e.memset(tail[:], 0.0), [], [R(tail)])

        dma("sp", xs[0][:], xh[0:128, :], [], [R(xs[0])])
        dma("sp", g_ln[:], lng_d.rearrange("(c p) -> p c", p=128), [], [R(g_ln)], slow=True)
        dma("sp", gq[:], qg_d.partition_broadcast(128), [], [R(gq)])
        dma("sp", gk[:], kg_d.partition_broadcast(128), [], [R(gk)])
        dma("sp", rgq[:, 0:32], qg_d[32:64].partition_broadcast(128), [], [R(rgq)])
        dma("sp", rgq[:, 32:64], qg_d[0:32].partition_broadcast(128), [], [R(rgq)])
        dma("sp", rgk[:, 0:32], kg_d[32:64].partition_broadcast(128), [], [R(rgk)])
        dma("sp", rgk[:, 32:64], kg_d[0:32].partition_broadcast(128), [], [R(rgk)])
        dma("sp", esink[:], sink_d.partition_broadcast(128), [], [R(esink)])
        dma("sp", hbias[:], hbias_d, [], [R(hbias)])
        for j in range(4):
            dma("act", convw[:, j, :], convw_d[:, j * 128:(j + 1) * 128].rearrange("t p -> p t"), [], [R(convw)], slow=True)
        act(esink[:], esink[:], AF.Exp, [R(esink)], [R(esink)])

        xs_ctr = [0]
        x_slot_of = {}

        def load_x(g):
            i = xs_ctr[0] % NXS
            xs_ctr[0] += 1
            x_slot_of[g] = i
            if g == 0:
                return
            dma("sp", xs[i][:], xh[g * 128:(g + 1) * 128, :], [], [R(xs[i])])

        def load_win(nm, c0, c1):
            step = 1024
            for a in range(c0, c1, step):
                b = min(c1, a + step)
                dma("pool", w_in[:, :, a:b], win_d[:, a:b].rearrange("(c p) n -> p c n", p=128), [], [WG[nm]])

        load_x(0)
        load_x(1)
        wq = []
        wq.append(lambda: load_win("qkv", 0, 768))
        wq.append(lambda: load_win("c", C_C, C_C + 512))
        wq.append(lambda: load_win("u", C_U, C_U + 512))
        wq.append(lambda: load_win("b", C_B, C_B + 512))
        wq.append(lambda: load_win("gC", C_GC, C_GC + 512))
        wq.append(lambda: load_win("gA", C_GA, C_GA + 512))
        wq.append(lambda: dma("pool", w_b[:], wb_d.rearrange("(c p) n -> p c n", p=128), [], [R_wb]))
        wq.append(lambda: dma("pool", w_a[:], wa_d.rearrange("(c p) n -> p c n", p=128), [], [R_wa]))
        wq.append(lambda: load_win("mA", C_MA, C_MA + 1024))
        wq.append(lambda: load_win("mB", C_MB, C_MB + 1024))
        wq.append(lambda: dma("pool", w_o[:], wo_d.rearrange("(c p) n -> p c n", p=128), [], [R_wo]))
        wq.append(lambda: dma("pool", w_pg[:], wpg_d.rearrange("(c p) n -> p c n", p=128), [], [R_wpg]))
        wq.append(lambda: dma("pool", w_pp[:], wpp_d.rearrange("(c p) n -> p c n", p=128), [], [R_wpp]))

        def issue_w(n):
            for _ in range(n):
                if wq:
                    wq.pop(0)()

        issue_w(3)
        stail_jobs = [(b, j) for b in range(4) for j in range(4)]

        def load_stail(n):
            for _ in range(n):
                if stail_jobs:
                    b, j = stail_jobs.pop(0)
                    dma("sp", stail[:, j, b, :], sconv_d[b, :, j * 128:(j + 1) * 128].rearrange("t p -> p t"), [], [R(stail)], slow=True)

        out_y_res = [None]
        src_res = [None]

        norm_ctr = [0]

        def norm_tile(g, col0):
            i = x_slot_of[g]
            xt = xs[i]
            s = sx[norm_ctr[0] % 2]
            norm_ctr[0] += 1
            P.add("pool", lambda e: e.memset(s[:], 0.0), [], [R(s)])
            act(xn[:], xt[:], AF.Square, [R(xt), R(s)], [R(xn), R(s)], accum_out=s[:, 0:1])
            src_res[0] = R(s)
            out_y_res[0] = R(s)
            newton_rsqrt_ap(s[:, 0:1], s[:, 2:3], s[:, 3:4], s[:, 1:2], 1.0 / D, iters=2, Rv=R(s), Rt=R(s))
            yield
            act(xn[:], xt[:], AF.Copy, [R(xt), R(s)], [R(xn)], scale=s[:, 1:2])
            yield
            br, bt = bank()
            b16 = bt[:].bitcast(BF16)
            for c in range(8):
                tr(b16[:, c * 128:(c + 1) * 128], xn[:, c * 128:(c + 1) * 128], ident[:], [R(xn), Rid], [br])
            tt("dve", hT[:, :, col0:col0 + 128], b16.rearrange("p (c t) -> p c t", c=8),
               g_ln[:].unsqueeze(2).to_broadcast([128, 8, 128]), ALU.mult, [br, R(g_ln)], Rh(col0, 128))

        cs_ctr = [0]
        kf_ctr = [0]

        def qkv_tile(row0, col0, M, slot, has_q, kout=None, vout=None):
            ci = cs_ctr[0] % 2
            cs_ctr[0] += 1
            cst = csl[ci]
            dma("sp", cst[0:M, :], rope[row0:row0 + M, :], [], [R(cst)])
            Rw = WG["qkv"]
            if has_q:
                bq, tq = bank()
                for c in range(8):
                    mm(tq[0:M, :], hT[:, c, col0:col0 + M], w_in[:, c, 0:512], c == 0, c == 7, Rh(col0, M) + [Rw], [bq])
            bkv, tkv = bank()
            for c in range(8):
                mm(tkv[0:M, 0:256], hT[:, c, col0:col0 + M], w_in[:, c, 512:768], c == 0, c == 7, Rh(col0, M) + [Rw], [bkv])
            h0 = 0 if has_q else 8
            if has_q:
                tt("pool", TCq[0:M, :], cst[0:M, 0:64], gq[0:M, :], ALU.mult, [R(cst), R(gq)], [R(TCq)])
                tt("pool", TSq[0:M, :], cst[0:M, 64:128], rgq[0:M, :], ALU.mult, [R(cst), R(rgq)], [R(TSq)])
            tt("pool", TCk[0:M, :], cst[0:M, 0:64], gk[0:M, :], ALU.mult, [R(cst), R(gk)], [R(TCk)])
            tt("pool", TSk[0:M, :], cst[0:M, 64:128], rgk[0:M, :], ALU.mult, [R(cst), R(rgk)], [R(TSk)])
            if has_q:
                act(nrm[0:M, :], tq[0:M, :], AF.Square, [bq], [R(nrm)])
                P.add("dve", lambda e: e.tensor_reduce(out=ssq[0:M, 0:8], in_=nrm[0:M, :].rearrange("p (h d) -> p h d", h=8), axis=AX.X, op=ALU.add),
                      [R(nrm)], [R(ssq)])
            act(csb[0:M, 0:128], tkv[0:M, 0:128], AF.Square, [bkv], [R(csb)])
            P.add("dve", lambda e: e.tensor_reduce(out=ssq[0:M, 8:10], in_=csb[0:M, 0:128].rearrange("p (h d) -> p h d", h=2), axis=AX.X, op=ALU.add),
                  [R(csb)], [R(ssq)])
            src_res[0] = R(ssq)
            out_y_res[0] = R(nwy)
            n = 10 - h0
            v, t, yv = nwv[0:M, 0:n], nwt[0:M, 0:n], nwy[0:M, 0:n]
            newton_rsqrt_ap(ssq[0:M, h0:10], v, t, yv, 1.0 / 64)
            if has_q:
                tt("dve", qr[0:M, 0:8, :], tq[0:M, :].rearrange("p (h d) -> p h d", h=8),
                   nwy[0:M, 0:8].unsqueeze(2).to_broadcast([M, 8, 64]), ALU.mult, [bq, R(nwy)], [R(qr)])
            tt("dve", qr[0:M, 8:10, :], tkv[0:M, 0:128].rearrange("p (h d) -> p h d", h=2),
               nwy[0:M, 8 - h0:10 - h0].unsqueeze(2).to_broadcast([M, 2, 64]), ALU.mult, [bkv, R(nwy)], [R(qr)])
            kfi = kf_ctr[0] % 2
            kf_ctr[0] += 1
            vft, kft = vf[kfi], kf[kfi]
            qro, kb = qro2[kfi], kb2[kfi]
            vdt = vd[slot]
            for rr in range(2):
                act(vdt[0:M, :, rr * 64:(rr + 1) * 64], tkv[0:M, 128:256].rearrange("p (h d) -> p h d", h=2), AF.Copy, [bkv], [R(vdt)])
            if vout is not None:
                act(vft[0:M, :], tkv[0:M, 128:256], AF.Copy, [bkv], [R(vft)])
                for (dst, r0, r1) in vout:
                    dma("sp", dst, vft[r0:r1, :], [R(vft)], [])
            if has_q:
                tt("pool", sq[0:M, 0:8, 0:32], qr[0:M, 0:8, 32:64], TSq[0:M, 0:32].unsqueeze(1).to_broadcast([M, 8, 32]), ALU.mult, [R(qr), R(TSq)], [R(sq)])
                tt("pool", sq[0:M, 0:8, 32:64], qr[0:M, 0:8, 0:32], TSq[0:M, 32:64].unsqueeze(1).to_broadcast([M, 8, 32]), ALU.mult, [R(qr), R(TSq)], [R(sq)])
            tt("pool", sq[0:M, 8:10, 0:32], qr[0:M, 8:10, 32:64], TSk[0:M, 0:32].unsqueeze(1).to_broadcast([M, 2, 32]), ALU.mult, [R(qr), R(TSk)], [R(sq)])
            tt("pool", sq[0:M, 8:10, 32:64], qr[0:M, 8:10, 0:32], TSk[0:M, 32:64].unsqueeze(1).to_broadcast([M, 2, 32]), ALU.mult, [R(qr), R(TSk)], [R(sq)])
            if has_q:
                tt("pool", qr[0:M, 0:8, :], qr[0:M, 0:8, :], TCq[0:M, :].unsqueeze(1).to_broadcast([M, 8, 64]), ALU.mult, [R(qr), R(TCq)], [R(qr)])
            tt("pool", qr[0:M, 8:10, :], qr[0:M, 8:10, :], TCk[0:M, :].unsqueeze(1).to_broadcast([M, 2, 64]), ALU.mult, [R(qr), R(TCk)], [R(qr)])
            if has_q:
                tt("pool", qro[0:M, :, :], qr[0:M, 0:8, :], sq[0:M, 0:8, :], ALU.add, [R(qr), R(sq)], [R(qro)])
            tt("pool", kft[0:M, :, :], qr[0:M, 8:10, :], sq[0:M, 8:10, :], ALU.add, [R(qr), R(sq)], [R(kft)])
            cp("pool", kb[0:M, :, :], kft[0:M, :, :], [R(kft)], [R(kb)])
            if kout is not None:
                for (dst, r0, r1) in kout:
                    dma("sp", dst, kft[r0:r1, :, :].rearrange("p a b -> p (a b)"), [R(kft)], [])
            yield
            if has_q:
                bt_, tt_ = bank()
                b16 = tt_[:].bitcast(BF16)
                for h in range(8):
                    tr(b16[0:64, h * 128:h * 128 + M], qro[0:M, h, :], ident[0:M, 0:M], [R(qro), Rid], [bt_])
            bk_, tk_ = bank()
            k16 = tk_[:].bitcast(BF16)
            for kv in range(2):
                tr(k16[0:64, kv * 128:kv * 128 + M], kb[0:M, kv, :], ident[0:M, 0:M], [R(kb), Rid], [bk_])
            if has_q:
                cp("act", QT[:, :, 0:M], b16[0:64, 0:1024].rearrange("p (h t) -> p h t", h=8)[:, :, 0:M], [bt_], [R(QT)])
            cp("act", KT[slot][:, :, 0:M], k16[0:64, 0:256].rearrange("p (k t) -> p k t", k=2)[:, :, 0:M], [bk_], [R(KT[slot])])

        def newton_rsqrt_ap(src, v, t, yv, scale, iters=3, Rv=None, Rt=None):
            Rv = Rv or R(nwv)
            Rt = Rt or R(nwt)
            Ry, Rs = out_y_res[0], src_res[0]
            tsc("dve", v, src, scale, EPS, ALU.mult, ALU.add, [Rs], [Rv])
            tsc("dve", t, src, 0.5 * scale, 0.5 * EPS + 0.5, ALU.mult, ALU.add, [Rs], [Rt])
            P.add("dve", lambda e: e.reciprocal(out=yv, in_=t), [Rt], [Ry])
            for _ in range(iters):
                tt("dve", t, v, yv, ALU.mult, [Rv, Ry], [Rt])
                tt("dve", t, t, yv, ALU.mult, [Rt, Ry], [Rt])
                tsc("dve", t, t, -0.5, 1.5, ALU.mult, ALU.add, [Rt], [Rt])
                tt("dve", yv, yv, t, ALU.mult, [Ry, Rt], [Ry])

        pt_ctr = [0]

        def attend(Mq, blocks, pv_plan, col0):
            ncq = len(pv_plan)
            qw = pv_plan[0][1] - pv_plan[0][0]
            allpts = []
            for kv in range(2):
                pts = []
                for (kt_ap, Rkt, vd_ap, Rvd, nk, bias_ap, Rb) in blocks:
                    bs, ts_ = bank()
                    sview = ts_[0:nk, 0:4 * Mq]
                    mm(sview, kt_ap[:, kv, 0:nk], QT[:, 4 * kv:4 * kv + 4, 0:Mq], True, True, [Rkt, R(QT)], [bs])
                    pti = pt_ctr[0] % NPTB
                    pt_ctr[0] += 1
                    ptt = PT[pti]
                    rds = [bs] + ([Rb] if bias_ap is not None else [])
                    if bias_ap is not None:
                        act(ptt[0:nk, 0:4 * Mq], sview, AF.Exp, rds, [R(ptt)], bias=bias_ap[0:nk, :], scale=0.125)
                    else:
                        act(ptt[0:nk, 0:4 * Mq], sview, AF.Exp, rds, [R(ptt)], scale=0.125)
                    pts.append(ptt)
                allpts.append(pts)
            yield
            for kv in range(2):
                pts = allpts[kv]
                bo, to = bank()
                bsu, tsu = bank()
                for ci, (q0, q1, contrib) in enumerate(pv_plan):
                    n = len(contrib)
                    osl = slice(ci * 4 * qw, (ci + 1) * 4 * qw)
                    for ii, (bi, k0, k1) in enumerate(contrib):
                        vd_ap, Rvd = blocks[bi][2], blocks[bi][3]
                        p3 = pts[bi][:, 0:4 * Mq].rearrange("p (g q) -> p g q", g=4)
                        mm(to[:, osl], vd_ap[k0:k1, kv, :], p3[k0:k1, :, q0:q1], ii == 0, ii == n - 1, [Rvd, R(pts[bi])], [bo])
                    for ii, (bi, k0, k1) in enumerate(contrib):
                        p3 = pts[bi][:, 0:4 * Mq].rearrange("p (g q) -> p g q", g=4)
                        mm(tsu[:, osl], ones[k0:k1, :], p3[k0:k1, :, q0:q1], ii == 0, ii == n - 1, [Rones, R(pts[bi])], [bsu])
                nb_ = nrm if kv == 0 else sqflat
                Rnb_ = R(nrm) if kv == 0 else R(sq)
                s4 = tsu[:, 0:4 * Mq].rearrange("p (c g q) -> p c g q", c=ncq, g=4)
                n4 = nb_[:, 0:4 * Mq].rearrange("p (c g q) -> p c g q", c=ncq, g=4)
                for ci in range(ncq):
                    tt("dve", n4[:, ci], s4[:, ci], esink[:, 4 * kv:4 * kv + 4].unsqueeze(2).to_broadcast([128, 4, qw]), ALU.add, [bsu, R(esink)], [Rnb_])
                nfl = nb_[:, 0:4 * Mq]
                P.add("dve", lambda e, nfl=nfl: e.reciprocal(out=nfl, in_=nfl), [Rnb_], [Rnb_])
                o5 = to[:, 0:4 * Mq].rearrange("p (c j e q) -> p j e c q", c=ncq, j=2, e=2)
                n5 = nb_[:, 0:4 * Mq].rearrange("p (c j e q) -> p j e c q", c=ncq, j=2, e=2)
                for e_ in range(2):
                    ps_ = slice(64 * e_, 64 * e_ + 64)
                    atv = AT[ps_, 2 * kv:2 * kv + 2, col0:col0 + Mq].rearrange("p j (c q) -> p j c q", c=ncq)
                    for ci in range(ncq):
                        tt("dve", atv[:, :, ci, :], o5[ps_, :, e_, ci, :], n5[ps_, :, e_, ci, :], ALU.mult, [bo, Rnb_], [R(AT)])

        def zT_pair(N, colA, colB, RwA, RwB, hold=False):
            br, bt = bank(hold=hold)
            for half, (col, Rw) in enumerate(((colA, RwA), (colB, RwB))):
                for c in range(8):
                    mm(bt[:, half * ST:half * ST + N], w_in[:, c, col:col + 128], hT[:, c, 0:N], c == 0, c == 7, [Rw] + Rh(0, N), [br])
            return br, bt

        tg_ctr = [0]

        def conv_branch(N, nseq, T, tail_src, is_last_prompt, is_sample, js=(0, 1, 2, 3)):
            for j in js:
                for _ in conv_gen(N, nseq, T, tail_src, is_last_prompt, is_sample, j):
                    pass

        def conv_gen(N, nseq, T, tail_src, is_last_prompt, is_sample, j):
            if True:
                b1, t1 = zT_pair(N, C_C + j * 128, C_U + j * 128, WG["c"], WG["u"], hold=True)
                b2, t2 = zT_pair(N, C_B + j * 128, C_GC + j * 128, WG["b"], WG["gC"], hold=True)
                yield
                upv = up[:, 0:nseq * (T + 2)].rearrange("p (s t) -> p s t", s=nseq)
                if tail_src is None:
                    cp("pool", upv[:, 0, 0:2], tail[:, j, :], [R(tail)], [R(up)])
                else:
                    cp("pool", upv[:, :, 0:2], tail_src[:, j, :, :], [R(stail)], [R(up)])
                cp("act", csb[:, 0:N], t1[:, 0:N], [b1], [R(csb)])
                tgt = tgate[tg_ctr[0] % 2]
                tg_ctr[0] += 1
                act(tgt[:, 0:N], t2[:, ST:ST + N], AF.Tanh, [b2], [R(tgt)], scale=0.5)
                tt("dve", upv[:, :, 2:T + 2], csb[:, 0:N].rearrange("p (s t) -> p s t", s=nseq),
                   t1[:, ST:ST + N].rearrange("p (s t) -> p s t", s=nseq), ALU.mult, [R(csb), b1], [R(up)])
                stt("dve", tgt[:, 0:N], tgt[:, 0:N], 1.0, t2[:, ST:ST + N], ALU.add, ALU.mult, [R(tgt), b2], [R(tgt)])
                tt("dve", tgt[:, 0:N], tgt[:, 0:N], t2[:, 0:N], ALU.mult, [R(tgt), b2], [R(tgt)])
                if not is_sample:
                    cp("pool", tail[:, j, :], upv[:, 0, T:T + 2], [R(up)], [R(tail)])
                    if is_last_prompt:
                        cp("pool", coutp[:, j, :], upv[:, 0, T:T + 2], [R(up)], [R(coutp)])
                else:
                    cp("pool", couts[:, j, :, :], upv[:, :, T:T + 2], [R(up)], [R(couts)])
                a3 = acc[:, 0:N].rearrange("p (s t) -> p s t", s=nseq)
                act(a3, upv[:, :, 0:T], AF.Copy, [R(up), R(convw)], [R(acc)], scale=convw[:, j, 0:1])
                stt("dve", a3, upv[:, :, 1:T + 1], convw[:, j, 1:2], a3, ALU.mult, ALU.add, [R(up), R(convw), R(acc)], [R(acc)])
                stt("dve", a3, upv[:, :, 2:T + 2], convw[:, j, 2:3], a3, ALU.mult, ALU.add, [R(up), R(convw), R(acc)], [R(acc)])
                tt("pool", PC[:, j, 0:N], acc[:, 0:N], tgt[:, 0:N], ALU.mult, [R(acc), R(tgt)], [R(PC)])
                unhold(b1)
                unhold(b2)

        def conv_state_only(N):
            for j in range(4):
                b1, t1 = zT_pair(N, C_C + j * 128, C_U + j * 128, WG["c"], WG["u"])
                cp("act", csb[:, 0:2], t1[:, N - 2:N], [b1], [R(csb)])
                tt("dve", tail[:, j, :], csb[:, 0:2], t1[:, ST + N - 2:ST + N], ALU.mult, [R(csb), b1], [R(tail)])

        def gate_a(N):
            for jj in range(2):
                br, bt = zT_pair(N, C_GA + (2 * jj) * 128, C_GA + (2 * jj + 1) * 128, WG["gA"], WG["gA"])
                for half in range(2):
                    j = 2 * jj + half
                    tgt = tgate[tg_ctr[0] % 2]
                    tg_ctr[0] += 1
                    act(tgt[:, 0:N], bt[:, half * ST:half * ST + N], AF.Tanh, [br], [R(tgt)], scale=0.5)
                    stt("dve", tgt[:, 0:N], tgt[:, 0:N], 1.0, bt[:, half * ST:half * ST + N], ALU.add, ALU.mult, [R(tgt), br], [R(tgt)])
                    tt("pool", AT[:, j, 0:N], AT[:, j, 0:N], tgt[:, 0:N], ALU.mult, [R(AT), R(tgt)], [R(AT)])

        def mprime(N, k):
            bA, tAb = bank(hold=True)
            for c in range(8):
                mm(tAb[:, 0:N], w_in[:, c, C_MA + k * 128:C_MA + (k + 1) * 128], hT[:, c, 0:N], c == 0, c == 7, [WG["mA"]] + Rh(0, N), [bA])
            bB, tBb = bank(hold=True)
            for c in range(8):
                mm(tBb[:, 0:N], w_in[:, c, C_MB + k * 128:C_MB + (k + 1) * 128], hT[:, c, 0:N], c == 0, c == 7, [WG["mB"]] + Rh(0, N), [bB])
            return (bA, tAb, bB, tBb)

        def ymerge(N, k, bks):
            bA, tAb, bB, tBb = bks
            for j in range(4):
                mm(tBb[:, ST:ST + N], w_b[:, j, k * 128:(k + 1) * 128], PC[:, j, 0:N], j == 0, j == 3, [R_wb, R(PC)], [bB])
            for j in range(4):
                mm(tAb[:, ST:ST + N], w_a[:, j, k * 128:(k + 1) * 128], AT[:, j, 0:N], j == 0, j == 3, [R_wa, R(AT)], [bA])
            act(tB[:, 0:N], tBb[:, 0:N], AF.Tanh, [bB], [R(tB)], scale=0.5)
            act(tA[:, 0:N], tAb[:, 0:N], AF.Tanh, [bA], [R(tA)], scale=0.5)
            stt("dve", tB[:, 0:N], tB[:, 0:N], 1.0, tBb[:, ST:ST + N], ALU.add, ALU.mult, [R(tB), bB], [R(tB)])
            stt("dve", tA[:, 0:N], tA[:, 0:N], 1.0, tAb[:, ST:ST + N], ALU.add, ALU.mult, [R(tA), bA], [R(tA)])
            tt("pool", mT[:, k, 0:N], tA[:, 0:N], tB[:, 0:N], ALU.add, [R(tA), R(tB)], [RmT[k]])
            unhold(bA)
            unhold(bB)

        def merge(N):
            bks = mprime(N, 0)
            for k in range(8):
                nxt = mprime(N, k + 1) if k < 7 else None
                ymerge(N, k, bks)
                bks = nxt

        pf_ctr = [0]

        def r_phase(g, col0, yrow0):
            xt = xs[x_slot_of[g]]
            pb = pb2[pf_slot[g]]
            rbanks = []
            for hh in range(2):
                br, bt = bank()
                for k in range(8):
                    mm(bt[:, :], mT[:, k, col0:col0 + 128], w_o[:, k, hh * 512:(hh + 1) * 512], k == 0, k == 7, [RmT[k], R_wo], [br])
                rbanks.append((br, bt))
            for hh in range(2):
                br, bt = rbanks[hh]
                stt("dve", xt[:, hh * 512:(hh + 1) * 512], bt[:, :], 0.25, xt[:, hh * 512:(hh + 1) * 512], ALU.mult, ALU.add, [br, R(xt)], [R(xt)])
            yield
            cp("act", rb[:], xt[:], [R(xt)], [R(rb)])
            yield
            b1, t1 = bank()
            b16 = t1[:].bitcast(BF16)
            for c in range(8):
                tr(b16[:, c * 128:(c + 1) * 128], rb[:, c * 128:(c + 1) * 128], ident[:], [R(rb), Rid], [b1])
            cp("dve", rT[:].rearrange("p c t -> p (c t)"), b16[:, :], [b1], [R(rT)])
            b2, t2 = bank()
            b16b = t2[:].bitcast(BF16)
            for c in range(2):
                tr(b16b[:, c * 128:(c + 1) * 128], pb[:, c * 128:(c + 1) * 128], ident[:], [R(pb), Rid], [b2])
            act(pT[:].rearrange("p c t -> p (c t)"), b16b[:, 0:256], AF.Copy, [b2], [R(pT)], scale=0.5)
            yield
            for hh in range(2):
                bg, tgb = bank()
                for k in range(8):
                    mm(tgb[:, :], rT[:, k, :], w_pg[:, k, hh * 512:(hh + 1) * 512], k == 0, k == 7, [R(rT), R_wpg], [bg])
                bp, tpb = bank()
                for c in range(2):
                    mm(tpb[:, :], pT[:, c, :], w_pp[:, c, hh * 512:(hh + 1) * 512], c == 0, c == 1, [R(pT), R_wpp], [bp])
                sl = slice(hh * 512, (hh + 1) * 512)
                act(tg[:, :], tgb[:, :], AF.Tanh, [bg], [R(tg)], scale=0.5)
                stt("dve", tg[:, :], tg[:, :], 1.0, tpb[:, :], ALU.add, ALU.mult, [R(tg), bp], [R(tg)])
                tt("dve", xt[:, sl], xt[:, sl], tg[:, :], ALU.add, [R(tg), R(xt)], [R(xt)])
            dma("sp", y_d[yrow0:yrow0 + 128, :], xt[:], [R(xt)], [])

        def run(gen):
            for _ in gen:
                pass

        pf_slot = {}

        def load_p(g):
            i = pf_ctr[0] % 2
            pf_ctr[0] += 1
            pf_slot[g] = i
            dma("pool", pb2[i][:], ph[g * 128:(g + 1) * 128, :], [], [R(pb2[i])])

        run(norm_tile(0, 0))
        run(qkv_tile(0, 0, 128, 0, False))
        issue_w(7)
        conv_state_only(128)
        NB = TOK_PC // ST
        load_x(2)
        run(norm_tile(1, 0))
        run(norm_tile(2, 128))

        def start_qkv(blk):
            g0, g1 = 1 + 2 * blk, 2 + 2 * blk
            last = blk == NB - 1
            gens = []
            for i, g in enumerate((g0, g1)):
                kout = vout = None
                if last and i == 1:
                    kout = [(klast_d[:, :], 0, 128)]
                    vout = [(vlast_d[:, :], 0, 128)]
                qg_ = qkv_tile(g * 128, i * 128, 128, g % 3, True, kout, vout)
                next(qg_)
                gens.append(qg_)
            return gens

        def start_qkv_one(blk, i):
            g = 1 + 2 * blk + i
            last = blk == NB - 1
            kout = vout = None
            if last and i == 1:
                kout = [(klast_d[:, :], 0, 128)]
                vout = [(vlast_d[:, :], 0, 128)]
            qg_ = qkv_tile(g * 128, i * 128, 128, g % 3, True, kout, vout)
            next(qg_)
            return [qg_]

        qgen = start_qkv(0)
        issue_w(5)
        plan = [(0, 64, [(0, 0, 128), (1, 0, 64)]), (64, 128, [(0, 64, 128), (1, 0, 128)])]

        def mk_att(i, g):
            blocks = [
                (KT[(g - 1) % 3], R(KT[(g - 1) % 3]), vd[(g - 1) % 3], R(vd[(g - 1) % 3]), 128, (hbias if g == 1 else None), R(hbias)),
                (KT[g % 3], R(KT[g % 3]), vd[g % 3], R(vd[g % 3]), 128, None, None),
            ]
            return attend(128, blocks, plan, i * 128)

        gS = NPT + 1
        sgen = {}

        def sample_q1(b):
            slot = (gS + b) % 3
            g_ = qkv_tile(gS * 128 + b * 32, b * 32, 32, slot, True,
                          kout=[(ks_d[b, 96:128, :], 0, 32)], vout=[(vs_d[b, 96:128, :], 0, 32)])
            next(g_)
            return g_

        for blk in range(NB):
            g0, g1 = 1 + 2 * blk, 2 + 2 * blk
            last = blk == NB - 1
            ng0 = g0 + 2
            ng1 = g1 + 2 if not last else None
            load_p(g0)
            load_p(g1)
            load_x(ng0)
            if ng1 is not None:
                load_x(ng1)
            load_stail(2)
            issue_w(5)
            run(qgen[0])
            cg = conv_gen(ST, 1, ST, None, last, False, 0)
            next(cg)
            at0 = mk_att(0, g0)
            next(at0)
            run(cg)
            run(at0)
            run(qgen[1])
            cg = conv_gen(ST, 1, ST, None, last, False, 1)
            next(cg)
            at1 = mk_att(1, g1)
            next(at1)
            run(cg)
            run(at1)
            n0 = norm_tile(ng0, 0)
            next(n0)
            conv_branch(ST, 1, ST, None, last, False, js=(2,))
            bks = mprime(ST, 0)
            n1 = None
            if ng1 is not None:
                n1 = norm_tile(ng1, 128)
                next(n1)
            conv_branch(ST, 1, ST, None, last, False, js=(3,))
            gate_a(ST)
            for k in range(8):
                nxt = mprime(ST, k + 1) if k < 7 else None
                if k == 7:
                    next(n0)
                ymerge(ST, k, bks)
                bks = nxt
            ra = r_phase(g0, 0, (g0 - 1) * 128)
            rb_ = r_phase(g1, 128, (g1 - 1) * 128)
            next(ra)
            run(n0)
            if n1 is not None:
                next(n1)
            next(rb_)
            if n1 is not None:
                run(n1)
            next(ra)
            next(ra)
            next(rb_)
            if not last:
                qgen = start_qkv_one(blk + 1, 0)
            else:
                sgen[0] = sample_q1(0)
            run(ra)
            next(rb_)
            if not last:
                qgen.append(start_qkv_one(blk + 1, 1)[0])
            else:
                sgen[1] = sample_q1(1)
            run(rb_)
        gS = NPT + 1
        load_stail(16)
        load_p(gS)
        for b in range(4):
            dma("sp", ckf[:], ck_d[b], [], [R(ckf)])
            dma("sp", cvf[:], cv_d[b], [], [R(cvf)])
            dma("sp", ks_d[b, 0:96, :], ck_d[b, 32:128, :], [], [])
            dma("sp", vs_d[b, 0:96, :], cv_d[b, 32:128, :], [], [])
            cp("pool", ckb[:], ckf[:], [R(ckf)], [R(ckb)])
            for rr in range(2):
                cp("act", vdc[:, :, rr * 64:(rr + 1) * 64], cvf[:].rearrange("p (h d) -> p h d", h=2), [R(cvf)], [R(vdc)])
            bt_, tt_ = bank()
            b16 = tt_[:].bitcast(BF16)
            for kv in range(2):
                tr(b16[0:64, kv * 128:(kv + 1) * 128], ckb[:, kv * 64:(kv + 1) * 64], ident[:], [R(ckb), Rid], [bt_])
            cp("act", KTc[:].rearrange("p k t -> p (k t)"), b16[0:64, 0:256], [bt_], [R(KTc)])
            slot = (gS + b) % 3
            run(sgen[b])
            blocks = [
                (KTc, R(KTc), vdc, R(vdc), 128, None, None),
                (KT[slot], R(KT[slot]), vd[slot], R(vd[slot]), 32, None, None),
            ]
            plan = [(0, 32, [(0, 0, 128), (1, 0, 32)])]
            cg = conv_gen(128, 4, 32, stail, False, True, b)
            next(cg)
            at_ = attend(32, blocks, plan, b * 32)
            next(at_)
            if b + 2 < 4:
                sgen[b + 2] = sample_q1(b + 2)
            run(cg)
            run(at_)
        gate_a(128)
        merge(128)
        run(r_phase(gS, 0, TOK_PC))
        for j in range(4):
            dma("sp", clast_d[:, j * 128:(j + 1) * 128].rearrange("t p -> p t"), coutp[:, j, :], [R(coutp)], [], slow=True)
            for b in range(4):
                dma("sp", cs_d[b, :, j * 128:(j + 1) * 128].rearrange("t p -> p t"), couts[:, j, b, :], [R(couts)], [], slow=True)

        sems = {}
        for nme in P.sem_names:
            sems[nme] = es.enter_context(nc.semaphore(nme))
        esem = {}
        sigcount = {}
        for e in ENGS:
            esem[e] = es.enter_context(nc.semaphore("eng_" + e))
            c = 0
            for op in P.ops[e]:
                if op.kind == "op" and op.sig:
                    c += 1
                    op.val = c
            sigcount[e] = c
        block = es.enter_context(nc.Block())

        def emit(ename, eng):
            inline_ok = ename in ("dve", "pool", "act", "pe")
            for op in P.ops[ename]:
                ws = list(op.waits)
                last = None
                if inline_ok and ws and op.kind == "op" and id(op) not in MULTI:
                    last = ws.pop()
                for w in ws:
                    if w[0] == "sem":
                        eng.wait_ge(sems[w[1]], w[2])
                    else:
                        X = w[1]
                        eng.wait_ge(esem[X.eng], X.val)
                ins = op.fn(eng)
                if last is not None:
                    if last[0] == "sem":
                        ins._wait_ge(sems[last[1]], last[2])
                    else:
                        ins._wait_ge(esem[last[1].eng], last[1].val)
                if op.kind == "dmaop":
                    ins.then_inc(sems[op.inc], 16)
                elif op.sig:
                    ins.then_inc(esem[ename], 1)
            if ename == "sp":
                for Rr in P.store_res:
                    eng.wait_ge(sems[Rr.rsem], Rr.rcount)
                if P.final_count:
                    eng.wait_ge(sems[P.final_sem], P.final_count)

        @block.sync
        def _(eng):
            emit("sp", eng)

        @block.scalar
        def _(eng):
            emit("act", eng)

        @block.vector
        def _(eng):
            emit("dve", eng)

        @block.gpsimd
        def _(eng):
            emit("pool", eng)

        @block.tensor
        def _(eng):
            emit("pe", eng)

    return nc


_NC_CACHE = {}


def kernel(x_prompt, x_sample, p_prompt, p_sample, cache_k, cache_v, state_conv,
           ln_g, w_in, q_norm_g, k_norm_g, sink, conv_w, w_attn_out, w_conv_out, w_o,
           w_ple_gate, w_ple_proj):
    f = np.float32
    xp = np.asarray(x_prompt, f).reshape(SEQ, D)
    xsm = np.asarray(x_sample, f).reshape(32 * 32, D)
    pp = np.asarray(p_prompt, f).reshape(SEQ, 256)
    psm = np.asarray(p_sample, f).reshape(32 * 32, 256)
    ck = np.asarray(cache_k, f).reshape(32, 128, 128)
    cv = np.asarray(cache_v, f).reshape(32, 128, 128)
    sc = np.asarray(state_conv, f).reshape(32, 2, 512)
    def rope_rows(pos):
        try:
            import jax
            import jax.numpy as jnp
            with jax.default_device(jax.devices("cpu")[0]):
                inv_freq = 10000.0 ** (-jnp.arange(0, 32, dtype=jnp.float32) * 2.0 / 64)
                ang = jnp.asarray(pos.astype(np.float32))[:, None] * inv_freq[None, :]
                c = np.asarray(jnp.cos(ang), dtype=f)
                s = np.asarray(jnp.sin(ang), dtype=f)
        except Exception:
            inv_freq = 10000.0 ** (-np.arange(0, 32, dtype=np.float64) * 2.0 / 64)
            ang = pos.astype(np.float64)[:, None] * inv_freq[None, :]
            c, s = np.cos(ang).astype(f), np.sin(ang).astype(f)
        return np.concatenate([c, c, -s, s], axis=1).astype(f)

    if "nc" not in _NC_CACHE:
        _NC_CACHE["nc"] = build_program()
    nc = _NC_CACHE["nc"]
    shared = {
        "ln_g": np.asarray(ln_g, f).reshape(D), "w_in": np.asarray(w_in, f).reshape(D, IN_DIM),
        "qg": np.asarray(q_norm_g, f).reshape(64), "kg": np.asarray(k_norm_g, f).reshape(64),
        "sink": np.asarray(sink, f).reshape(8), "conv_w": np.asarray(conv_w, f).reshape(3, 512),
        "w_a": np.asarray(w_attn_out, f).reshape(512, D), "w_b": np.asarray(w_conv_out, f).reshape(512, D),
        "w_o": np.asarray(w_o, f).reshape(D, D), "w_pg": np.asarray(w_ple_gate, f).reshape(D, D),
        "w_pp": np.asarray(w_ple_proj, f).reshape(256, D),
    }
    in_maps = []
    for c in range(NCORES):
        t0 = c * TOK_PC
        xhc = np.zeros((NROWS, D), f)
        phc = np.zeros((NROWS, 256), f)
        pos = np.zeros((NROWS,), np.float64)
        if c > 0:
            xhc[0:128] = xp[t0 - 128:t0]
        pos[0:128] = np.arange(t0 - 128, t0)
        xhc[128:128 + TOK_PC] = xp[t0:t0 + TOK_PC]
        phc[128:128 + TOK_PC] = pp[t0:t0 + TOK_PC]
        pos[128:128 + TOK_PC] = np.arange(t0, t0 + TOK_PC)
        xhc[128 + TOK_PC:] = xsm[c * 128:(c + 1) * 128]
        phc[128 + TOK_PC:] = psm[c * 128:(c + 1) * 128]
        pos[128 + TOK_PC:] = np.tile(1024 + np.arange(32), 4)
        hb = np.full((128, 1), -30000.0 if c == 0 else 0.0, f)
        m = dict(shared)
        m.update({"xh": xhc, "ph": phc, "rope": rope_rows(pos),
                  "ck": np.ascontiguousarray(ck[c * 4:(c + 1) * 4]), "cv": np.ascontiguousarray(cv[c * 4:(c + 1) * 4]),
                  "sconv": np.ascontiguousarray(sc[c * 4:(c + 1) * 4]), "hbias": hb})
        in_maps.append(m)
    out = run_bass_kernel_spmd(nc, in_maps, core_ids=list(range(NCORES)))
    rs = out.results
    y_prompt = np.concatenate([r["y"][0:TOK_PC] for r in rs], axis=0).reshape(1, SEQ, D)
    y_sample = np.concatenate([r["y"][TOK_PC:] for r in rs], axis=0).reshape(32, 32, D)
    k_prompt = rs[-1]["k_last"].reshape(1, 1, 128, 2, 64)
    v_prompt = rs[-1]["v_last"].reshape(1, 1, 128, 2, 64)
    conv_prompt = rs[-1]["c_last"].reshape(1, 1, 2, 512)
    k_sample = np.concatenate([r["ks"] for r in rs], axis=0).reshape(1, 32, 128, 2, 64)
    v_sample = np.concatenate([r["vs"] for r in rs], axis=0).reshape(1, 32, 128, 2, 64)
    conv_sample = np.concatenate([r["cs"] for r in rs], axis=0).reshape(1, 32, 2, 512)
    return (y_prompt.astype(f), y_sample.astype(f), k_prompt.astype(f), v_prompt.astype(f),
            conv_prompt.astype(f), k_sample.astype(f), v_sample.astype(f), conv_sample.astype(f))
```
